# Optimizing a Trainium2 kernel written in Bass

```python
import math
import jax, jax.numpy as jnp
from jax import lax
import numpy as np

D_MODEL = 1024
BATCH = 16
SEQ = 2048
DEPTH = 2

N_EVEN = (DEPTH + 1) // 2
N_ODD = DEPTH // 2
EPS = 1e-6
NEG = -1e30
TINY = 1e-30
Q_BLOCK = 128

D_FF = 2816

NUM_BUCKETS = 32
MAX_DISTANCE = 2048
N_BIAS_COLS = 16

MLA_HEADS = 8
MLA_Q_RANK = 256
MLA_KV_RANK = 128
MLA_NOPE = 64
MLA_ROPE = 32
MLA_V = 64
MLA_QK_DIM = MLA_NOPE + MLA_ROPE
ROPE_THETA = 10000.0

DIL_PATTERNS = ((128, 1), (512, 4), (2048, 16))
DIL_HEADS_PER_GROUP = 4
DIL_HEADS = DIL_HEADS_PER_GROUP * len(DIL_PATTERNS)
DIL_HEAD_DIM = 64
DIL_BIAS_COL0 = 0

SWA_HEADS = 8
SWA_KV_HEADS = 2
SWA_HEAD_DIM = 64
SWA_WINDOW = 128
SWA_BIAS_COL0 = 0

NSA_HEADS = 8
NSA_KV_HEADS = 2
NSA_HEAD_DIM = 64
NSA_CMP_LEN = 32
NSA_CMP_STRIDE = 16
NSA_CMP_HIDDEN = 128
NSA_SLC_BLOCK = 64
NSA_TOP_N = 16
NSA_WINDOW = 512
NSA_FORCE = 1e6
NSA_BIAS_COL0 = 8

AB_MLA_COLS = MLA_Q_RANK + MLA_KV_RANK + MLA_ROPE
AB_DIL_COLS = 3 * DIL_HEADS * DIL_HEAD_DIM
AB_IN = AB_MLA_COLS + AB_DIL_COLS
AB_MIX = MLA_HEADS * MLA_V + DIL_HEADS_PER_GROUP * DIL_HEAD_DIM
CD_SWA_COLS = (SWA_HEADS + 2 * SWA_KV_HEADS) * SWA_HEAD_DIM
CD_NSA_COLS = NSA_HEADS * NSA_HEAD_DIM + 6 * NSA_KV_HEADS * NSA_HEAD_DIM + 3 * NSA_HEADS
CD_IN = CD_SWA_COLS + CD_NSA_COLS
CD_MIX = SWA_HEADS * SWA_HEAD_DIM + NSA_HEADS * NSA_HEAD_DIM

kernel_name = "hybrid_mla_dilated_swa_nsa_macaron"


def rms_norm(x, g):
    xf = x.astype(jnp.float32)
    y = xf * lax.rsqrt(jnp.mean(xf * xf, axis=-1, keepdims=True) + EPS)
    return (y * g.astype(jnp.float32)).astype(x.dtype)


def qk_norm(x, g):
    xf = x.astype(jnp.float32)
    return xf * lax.rsqrt(jnp.mean(xf * xf, axis=-1, keepdims=True) + EPS) * g.astype(jnp.float32)


def swiglu_half(x, g, w_gate, w_up, w_down):
    h = rms_norm(x, g)
    return (jax.nn.silu(h @ w_gate) * (h @ w_up)) @ w_down


def t5_bucket(dist):
    n = jnp.maximum(dist, 0)
    max_exact = NUM_BUCKETS // 2
    nf = jnp.maximum(n, 1).astype(jnp.float32)
    large = max_exact + (jnp.log(nf / max_exact) / math.log(MAX_DISTANCE / max_exact)
                         * (NUM_BUCKETS - max_exact)).astype(jnp.int32)
    large = jnp.minimum(large, NUM_BUCKETS - 1)
    return jnp.where(n < max_exact, n, large)


def rope(x, pos):
    half = x.shape[-1] // 2
    inv = jnp.power(ROPE_THETA, -jnp.arange(half, dtype=jnp.float32) / half)
    ang = pos[:, None] * inv[None, :]
    cos, sin = jnp.cos(ang)[:, None, :], jnp.sin(ang)[:, None, :]
    x1, x2 = x[..., :half], x[..., half:]
    return jnp.concatenate([x1 * cos - x2 * sin, x2 * cos + x1 * sin], axis=-1)


def causal_block_attention(q, k, v, scale):
    B_, S, H, dk = q.shape
    nb = S // Q_BLOCK
    qb = q.reshape(B_, nb, Q_BLOCK, H, dk).swapaxes(0, 1)
    kpos = jnp.arange(S)

    def one_block(args):
        qi, i = args
        qpos = i * Q_BLOCK + jnp.arange(Q_BLOCK)
        logits = jnp.einsum('bqhd,bkhd->bhqk', qi, k) * scale
        logits = jnp.where(kpos[None, :] <= qpos[:, None], logits, NEG)
        p = jax.nn.softmax(logits, axis=-1)
        return jnp.einsum('bhqk,bkhd->bqhd', p, v)

    out = lax.map(one_block, (qb, jnp.arange(nb)))
    return out.swapaxes(0, 1).reshape(B_, S, H, v.shape[-1])


def banded_attention(q, k, v, window, bias_cols, dist_scale=1, sinks=None):
    B_, S, G, R, dh = q.shape
    nb = -(-S // Q_BLOCK)
    Sp = nb * Q_BLOCK
    n_prev = -(-(window - 1) // Q_BLOCK)
    KW = (n_prev + 1) * Q_BLOCK
    q = jnp.pad(q, ((0, 0), (0, Sp - S), (0, 0), (0, 0), (0, 0)))
    kv_pad = ((0, 0), (n_prev * Q_BLOCK, Sp - S), (0, 0), (0, 0))
    k = jnp.pad(k, kv_pad)
    v = jnp.pad(v, kv_pad)
    qi_ = jnp.arange(Q_BLOCK)
    kc = jnp.arange(KW)
    delta = qi_[:, None] + n_prev * Q_BLOCK - kc[None, :]
    band = (delta >= 0) & (delta < window)
    bias = bias_cols.astype(jnp.float32)[t5_bucket(delta * dist_scale)]
    bias = jnp.transpose(bias.reshape(Q_BLOCK, KW, G, R), (2, 3, 0, 1))
    scale = dh ** -0.5
    qb = q.reshape(B_, nb, Q_BLOCK, G, R, dh).swapaxes(0, 1)

    def one_block(args):
        qblk, i = args
        start = i * Q_BLOCK
        kw = lax.dynamic_slice_in_dim(k, start, KW, axis=1)
        vw = lax.dynamic_slice_in_dim(v, start, KW, axis=1)
        logits = jnp.einsum('bqgrd,bkgd->bgrqk', qblk, kw) * scale + bias
        valid = band & (start - n_prev * Q_BLOCK + kc >= 0)[None, :]
        logits = jnp.where(valid, logits, NEG)
        if sinks is not None:
            sink_col = jnp.broadcast_to(sinks.astype(jnp.float32)[:, :, None, None],
                                        logits.shape[:-1] + (1,))
            logits = jnp.concatenate([logits, sink_col], axis=-1)
        lse = jax.nn.logsumexp(logits, axis=-1, keepdims=True)
        p = jnp.exp(logits - lse)[..., :KW]
        out = jnp.einsum('bgrqk,bkgd->bqgrd', p, vw)
        return out, lse[..., 0]

    out, lse = lax.map(one_block, (qb, jnp.arange(nb)))
    out = out.swapaxes(0, 1).reshape(B_, Sp, G, R, dh)[:, :S]
    lse = jnp.transpose(lse, (1, 0, 4, 2, 3)).reshape(B_, Sp, G, R)[:, :S]
    return out, lse


def mla_mixer(z, q_a_norm, w_q_b, kv_a_norm, w_kv_b, q_gain, k_gain, pos):
    B_, S, _ = z.shape
    c_q, c_kv, k_rope = jnp.split(z, [MLA_Q_RANK, MLA_Q_RANK + MLA_KV_RANK], axis=-1)
    q = (rms_norm(c_q, q_a_norm) @ w_q_b).reshape(B_, S, MLA_HEADS, MLA_QK_DIM)
    kv = (rms_norm(c_kv, kv_a_norm) @ w_kv_b).reshape(B_, S, MLA_HEADS, MLA_NOPE + MLA_V)
    k_nope, v = kv[..., :MLA_NOPE], kv[..., MLA_NOPE:]
    k = jnp.concatenate([k_nope, jnp.broadcast_to(k_rope[:, :, None, :], (B_, S, MLA_HEADS, MLA_ROPE))], axis=-1)
    q = qk_norm(q, q_gain)
    k = qk_norm(k, k_gain)
    q = jnp.concatenate([q[..., :MLA_NOPE], rope(q[..., MLA_NOPE:], pos)], axis=-1)
    k = jnp.concatenate([k[..., :MLA_NOPE], rope(k[..., MLA_NOPE:], pos)], axis=-1)
    return causal_block_attention(q, k, v.astype(jnp.float32), MLA_QK_DIM ** -0.5)


def dilated_mixer(z, q_gain, k_gain, rel_bias):
    B_, S, _ = z.shape
    Hg, dh = DIL_HEADS_PER_GROUP, DIL_HEAD_DIM
    zd = z.reshape(B_, S, 3, len(DIL_PATTERNS), Hg, dh)
    q = qk_norm(zd[:, :, 0], q_gain)
    k = qk_norm(zd[:, :, 1], k_gain)
    v = zd[:, :, 2].astype(jnp.float32)
    outs, lses = [], []
    for g, (w, d) in enumerate(DIL_PATTERNS):
        Sd = S // d

        def to_res(t):
            return t.reshape(B_, Sd, d, Hg, dh).transpose(0, 2, 1, 3, 4).reshape(B_ * d, Sd, Hg, dh)

        cols = rel_bias[:, DIL_BIAS_COL0 + g * Hg: DIL_BIAS_COL0 + (g + 1) * Hg]
        o, lse = banded_attention(to_res(q[:, :, g])[:, :, :, None], to_res(k[:, :, g]), to_res(v[:, :, g]),
                                  w // d + 1, cols, dist_scale=d)
        outs.append(o[:, :, :, 0].reshape(B_, d, Sd, Hg, dh).transpose(0, 2, 1, 3, 4).reshape(B_, S, Hg, dh))
        lses.append(lse[..., 0].reshape(B_, d, Sd, Hg).transpose(0, 2, 1, 3).reshape(B_, S, Hg))
    wts = jax.nn.softmax(jnp.stack(lses), axis=0)
    return jnp.sum(wts[..., None] * jnp.stack(outs), axis=0)


def swa_mixer(z, q_gain, k_gain, sinks, rel_bias):
    B_, S, _ = z.shape
    G, R, dh = SWA_KV_HEADS, SWA_HEADS // SWA_KV_HEADS, SWA_HEAD_DIM
    zq, zk, zv = jnp.split(z, [SWA_HEADS * dh, SWA_HEADS * dh + G * dh], axis=-1)
    q = qk_norm(zq.reshape(B_, S, G, R, dh), q_gain)
    k = qk_norm(zk.reshape(B_, S, G, dh), k_gain)
    v = zv.reshape(B_, S, G, dh).astype(jnp.float32)
    cols = rel_bias[:, SWA_BIAS_COL0:SWA_BIAS_COL0 + SWA_HEADS]
    o, _ = banded_attention(q, k, v, SWA_WINDOW, cols, sinks=sinks.reshape(G, R))
    return o.reshape(B_, S, SWA_HEADS * dh)


def nsa_mixer(z, q_gain, k_gains, cmp_pos, cmp_w1, cmp_w2, rel_bias):
    B_, S, _ = z.shape
    G, R, dh = NSA_KV_HEADS, NSA_HEADS // NSA_KV_HEADS, NSA_HEAD_DIM
    f32 = jnp.float32
    offs = np.cumsum([NSA_HEADS * dh] + [G * dh] * 6).tolist()
    zq, zkc, zvc, zks, zvs, zkw, zvw, zg = jnp.split(z, offs, axis=-1)
    q = qk_norm(zq.reshape(B_, S, G, R, dh), q_gain)

    def kvs(t):
        return t.reshape(B_, S, G, dh)

    scale = dh ** -0.5
    cols = rel_bias[:, NSA_BIAS_COL0:NSA_BIAS_COL0 + NSA_HEADS]
    tq = jnp.arange(S)

    n_cmp = (S - NSA_CMP_LEN) // NSA_CMP_STRIDE + 1
    blk_idx = np.arange(n_cmp)[:, None] * NSA_CMP_STRIDE + np.arange(NSA_CMP_LEN)[None, :]

    def compress(t, pos_emb, w1, w2):
        blocks = kvs(t)[:, blk_idx] + pos_emb[:, None, :]
        flat = blocks.transpose(0, 1, 3, 2, 4).reshape(B_, n_cmp, G, NSA_CMP_LEN * dh)
        return jax.nn.gelu(flat @ w1) @ w2

    k_c = qk_norm(compress(zkc, cmp_pos[0], cmp_w1[0], cmp_w2[0]), k_gains[0])
    v_c = compress(zvc, cmp_pos[1], cmp_w1[1], cmp_w2[1]).astype(f32)
    cmp_ok = jnp.asarray(blk_idx[:, -1])[None, :] <= tq[:, None]
    logits = jnp.einsum('bsgrd,bcgd->bgrsc', q, k_c) * scale
    logits = jnp.where(cmp_ok, logits, NEG)
    e = jnp.exp(logits - jnp.max(logits, axis=-1, keepdims=True)) * cmp_ok
    p_cmp = e / jnp.maximum(jnp.sum(e, axis=-1, keepdims=True), TINY)
    o_cmp = jnp.einsum('bgrsc,bcgd->bsgrd', p_cmp, v_c)

    n_slc = S // NSA_SLC_BLOCK
    ci = np.arange(n_cmp)[:, None] * NSA_CMP_STRIDE
    sj = np.arange(n_slc)[None, :] * NSA_SLC_BLOCK
    overlap = jnp.asarray(((ci < sj + NSA_SLC_BLOCK) & (ci + NSA_CMP_LEN > sj)).astype(np.float32))
    imp = jnp.einsum('bgrsc,cj->bgsj', p_cmp, overlap)
    tb = tq // NSA_SLC_BLOCK
    jj = jnp.arange(n_slc)
    causal_blk = jj[None, :] <= tb[:, None]
    forced = (jj[None, :] == 0) | (jj[None, :] == tb[:, None]) | (jj[None, :] == tb[:, None] - 1)
    score = jnp.where(causal_blk, imp + jnp.where(forced, NSA_FORCE, 0.0), -NSA_FORCE)
    top_n = min(NSA_TOP_N, n_slc)
    _, sel = lax.top_k(score, top_n)

    k_s = qk_norm(kvs(zks), k_gains[1]).transpose(0, 2, 1, 3).reshape(B_, G, n_slc, NSA_SLC_BLOCK, dh)
    v_s = kvs(zvs).astype(f32).transpose(0, 2, 1, 3).reshape(B_, G, n_slc, NSA_SLC_BLOCK, dh)
    nb = S // Q_BLOCK
    q_blk = q.transpose(0, 2, 3, 1, 4).reshape(B_, G, R, nb, Q_BLOCK, dh).transpose(3, 0, 1, 2, 4, 5)
    sel_blk = sel.reshape(B_, G, nb, Q_BLOCK, top_n).transpose(2, 0, 1, 3, 4)
    bias_gr = cols.astype(f32).T.reshape(G, R, NUM_BUCKETS)
    b_ix = jnp.arange(B_)[:, None, None, None]
    g_ix = jnp.arange(G)[None, :, None, None]
    g6 = jnp.arange(G)[None, :, None, None, None, None]
    r6 = jnp.arange(R)[None, None, :, None, None, None]
    tok = jnp.arange(NSA_SLC_BLOCK)

    def slc_block(args):
        qi, si, i = args
        kg = k_s[b_ix, g_ix, si]
        vg = v_s[b_ix, g_ix, si]
        dist = (i * Q_BLOCK + jnp.arange(Q_BLOCK))[None, None, :, None, None] - (si[..., None] * NSA_SLC_BLOCK + tok)
        bias = bias_gr[g6, r6, t5_bucket(dist)[:, :, None]]
        lg = jnp.einsum('bgrqd,bgqnkd->bgrqnk', qi, kg) * scale + bias
        lg = jnp.where((dist >= 0)[:, :, None], lg, NEG)
        p = jax.nn.softmax(lg, axis=(-2, -1))
        return jnp.einsum('bgrqnk,bgqnkd->bqgrd', p, vg)

    o_slc = lax.map(slc_block, (q_blk, sel_blk, jnp.arange(nb))).swapaxes(0, 1).reshape(B_, S, G, R, dh)

    o_win, _ = banded_attention(q, qk_norm(kvs(zkw), k_gains[2]), kvs(zvw).astype(f32), NSA_WINDOW, cols)

    gate = jax.nn.sigmoid(zg.astype(f32)).reshape(B_, S, G, R, 3)
    out = gate[..., 0:1] * o_cmp + gate[..., 1:2] * o_slc + gate[..., 2:3] * o_win
    return out.reshape(B_, S, NSA_HEADS * dh)


def setup_inputs(seed: int = 0) -> dict:
    key = jax.random.key(seed)
    keys = iter(jax.random.split(key, 48))
    f32 = jnp.float32

    def w(shape, fan_in):
        return jax.random.normal(next(keys), shape, f32) * fan_in ** -0.5

    def gain(shape):
        return 1.0 + 0.05 * jax.random.normal(next(keys), shape, f32)

    NE, NO = N_EVEN, N_ODD
    return {
        "x": jax.random.normal(next(keys), (BATCH, SEQ, D_MODEL), f32),
        "rel_bias": 0.5 * jax.random.normal(next(keys), (NUM_BUCKETS, N_BIAS_COLS), f32),
        "ffn1_norm": gain((DEPTH, D_MODEL)),
        "ffn1_w_gate": w((DEPTH, D_MODEL, D_FF), D_MODEL),
        "ffn1_w_up": w((DEPTH, D_MODEL, D_FF), D_MODEL),
        "ffn1_w_down": w((DEPTH, D_FF, D_MODEL), D_FF),
        "mix_norm": gain((DEPTH, D_MODEL)),
        "ffn2_norm": gain((DEPTH, D_MODEL)),
        "ffn2_w_gate": w((DEPTH, D_MODEL, D_FF), D_MODEL),
        "ffn2_w_up": w((DEPTH, D_MODEL, D_FF), D_MODEL),
        "ffn2_w_down": w((DEPTH, D_FF, D_MODEL), D_FF),
        "ab_w_in": w((NE, D_MODEL, AB_IN), D_MODEL),
        "mla_q_a_norm": gain((NE, MLA_Q_RANK)),
        "mla_w_q_b": w((NE, MLA_Q_RANK, MLA_HEADS * MLA_QK_DIM), MLA_Q_RANK),
        "mla_kv_a_norm": gain((NE, MLA_KV_RANK)),
        "mla_w_kv_b": w((NE, MLA_KV_RANK, MLA_HEADS * (MLA_NOPE + MLA_V)), MLA_KV_RANK),
        "mla_q_norm": gain((NE, MLA_QK_DIM)),
        "mla_k_norm": gain((NE, MLA_QK_DIM)),
        "dil_q_norm": gain((NE, DIL_HEAD_DIM)),
        "dil_k_norm": gain((NE, DIL_HEAD_DIM)),
        "ab_w_out": w((NE, AB_MIX, D_MODEL), AB_MIX),
        "cd_w_in": w((NO, D_MODEL, CD_IN), D_MODEL),
        "swa_q_norm": gain((NO, SWA_HEAD_DIM)),
        "swa_k_norm": gain((NO, SWA_HEAD_DIM)),
        "swa_sinks": jax.random.normal(next(keys), (NO, SWA_HEADS), f32),
        "nsa_q_norm": gain((NO, NSA_HEAD_DIM)),
        "nsa_k_norm": gain((NO, 3, NSA_HEAD_DIM)),
        "nsa_cmp_pos": 0.1 * jax.random.normal(next(keys), (NO, 2, NSA_CMP_LEN, NSA_HEAD_DIM), f32),
        "nsa_cmp_w1": w((NO, 2, NSA_CMP_LEN * NSA_HEAD_DIM, NSA_CMP_HIDDEN), NSA_CMP_LEN * NSA_HEAD_DIM),
        "nsa_cmp_w2": w((NO, 2, NSA_CMP_HIDDEN, NSA_HEAD_DIM), NSA_CMP_HIDDEN),
        "cd_w_out": w((NO, CD_MIX, D_MODEL), CD_MIX),
    }


def reference(x, rel_bias, ffn1_norm, ffn1_w_gate, ffn1_w_up, ffn1_w_down, mix_norm,
              ffn2_norm, ffn2_w_gate, ffn2_w_up, ffn2_w_down, ab_w_in, mla_q_a_norm, mla_w_q_b,
              mla_kv_a_norm, mla_w_kv_b, mla_q_norm, mla_k_norm, dil_q_norm, dil_k_norm, ab_w_out,
              cd_w_in, swa_q_norm, swa_k_norm, swa_sinks, nsa_q_norm, nsa_k_norm, nsa_cmp_pos,
              nsa_cmp_w1, nsa_cmp_w2, cd_w_out):
    B_, S, _ = x.shape
    pos = jnp.arange(S, dtype=jnp.float32)
    for layer in range(DEPTH):
        x = x + 0.5 * swiglu_half(x, ffn1_norm[layer], ffn1_w_gate[layer], ffn1_w_up[layer], ffn1_w_down[layer])
        h = rms_norm(x, mix_norm[layer])
        if layer % 2 == 0:
            e = layer // 2
            z = h @ ab_w_in[e]
            o_a = mla_mixer(z[..., :AB_MLA_COLS], mla_q_a_norm[e], mla_w_q_b[e], mla_kv_a_norm[e],
                            mla_w_kv_b[e], mla_q_norm[e], mla_k_norm[e], pos)
            o_b = dilated_mixer(z[..., AB_MLA_COLS:], dil_q_norm[e], dil_k_norm[e], rel_bias)
            mixed = jnp.concatenate([o_a.reshape(B_, S, -1), o_b.reshape(B_, S, -1)], axis=-1)
            mixed = mixed.astype(x.dtype) @ ab_w_out[e]
        else:
            o = layer // 2
            z = h @ cd_w_in[o]
            o_c = swa_mixer(z[..., :CD_SWA_COLS], swa_q_norm[o], swa_k_norm[o], swa_sinks[o], rel_bias)
            o_d = nsa_mixer(z[..., CD_SWA_COLS:], nsa_q_norm[o], nsa_k_norm[o], nsa_cmp_pos[o],
                            nsa_cmp_w1[o], nsa_cmp_w2[o], rel_bias)
            mixed = jnp.concatenate([o_c, o_d], axis=-1).astype(x.dtype) @ cd_w_out[o]
        x = x + mixed
        x = x + 0.5 * swiglu_half(x, ffn2_norm[layer], ffn2_w_gate[layer], ffn2_w_up[layer], ffn2_w_down[layer])
    return x
```

```python
import math
import numpy as np
import ml_dtypes
import concourse.bass as bass
import concourse.mybir as mybir
from concourse.bass_utils import run_bass_kernel_spmd
from contextlib import ExitStack

F32 = mybir.dt.float32
BF16 = mybir.dt.bfloat16
ALU = mybir.AluOpType
AF = mybir.ActivationFunctionType
AX = mybir.AxisListType

D_MODEL = 1024
SEQ = 2048
D_FF = 2816
EPS = 1e-6
NCORES = 8
SEQ_PER_CORE = 2
NT = SEQ // 512
NCH = SEQ // 128


class Buf:
    __slots__ = ("name", "w", "r")

    def __init__(self, name=""):
        self.name = name
        self.w = None
        self.r = {}


class Sched:
    ENG = ("pe", "dve", "act", "pool", "sp")
    NO_SELF_WAIT = ("pe",)

    def __init__(self, nc, es, n_dma_sems=24):
        self.nc = nc
        self.sems = {e: es.enter_context(nc.semaphore("s_" + e)) for e in self.ENG}
        self.dsems = [es.enter_context(nc.semaphore("d%d" % i)) for i in range(n_dma_sems)]
        self.duse = [0] * n_dma_sems
        self.dnext = 0
        self.dnext_sw = 0
        self.cnt = {e: 0 for e in self.ENG}
        self.waited = {e: {} for e in self.ENG}
        self.prog = {e: [] for e in self.ENG}
        self.nops = 0

    def _wait(self, e, key, val):
        if self.waited[e].get(key, 0) >= val:
            return
        self.waited[e][key] = val
        sem = self.sems[key[1]] if key[0] == "e" else self.dsems[key[1]]
        self.prog[e].append(lambda eng, sem=sem, val=val: eng.wait_ge(sem, val))

    def op(self, e, fn, reads=(), writes=(), dma=False, sig=True):
        deps = {}

        def add(ev):
            if ev is None:
                return
            k, v = ev
            if deps.get(k, 0) < v:
                deps[k] = v

        for b in reads:
            add(b.w)
        for b in writes:
            add(b.w)
            for k, v in b.r.items():
                add((k, v))
        for k, v in deps.items():
            if k == ("e", e) and e in self.NO_SELF_WAIT:
                continue
            self._wait(e, k, v)
        self.nops += 1
        if dma:
            half = len(self.dsems) // 2
            if e == "pool":
                i = half + self.dnext_sw
                self.dnext_sw = (self.dnext_sw + 1) % (len(self.dsems) - half)
            else:
                i = self.dnext
                self.dnext = (i + 1) % half
            if self.duse[i] > 0:
                self._wait(e, ("d", i), 16 * self.duse[i])
            self.duse[i] += 1
            ev = (("d", i), 16 * self.duse[i])
            sem = self.dsems[i]
            self.prog[e].append(lambda eng, fn=fn, sem=sem: fn(eng).then_inc(sem, 16))
        elif sig:
            self.cnt[e] += 1
            ev = (("e", e), self.cnt[e])
            sem = self.sems[e]
            self.prog[e].append(lambda eng, fn=fn, sem=sem: fn(eng).then_inc(sem, 1))
        else:
            ev = (("e", e), self.cnt[e] + 1)
            self.prog[e].append(lambda eng, fn=fn: fn(eng))
        for b in reads:
            if b.r.get(ev[0], 0) < ev[1]:
                b.r[ev[0]] = ev[1]
        for b in writes:
            b.w = ev
            b.r = {}
        return ev

    def barrier(self):
        for e in self.ENG:
            for e2 in self.ENG:
                if e2 != e and self.cnt[e2] > 0:
                    self._wait(e, ("e", e2), self.cnt[e2])
            for i, u in enumerate(self.duse):
                if u:
                    self._wait(e, ("d", i), 16 * u)

    def finish(self):
        for i, u in enumerate(self.duse):
            if u:
                self._wait("sp", ("d", i), 16 * u)
        for e in self.ENG:
            if e != "sp" and self.cnt[e]:
                self._wait("sp", ("e", e), self.cnt[e])

    def emit(self):
        self.finish()
        nc = self.nc
        prog = self.prog
        with nc.Block() as block:
            @block.tensor
            def _(eng):
                for f in prog["pe"]:
                    f(eng)

            @block.vector
            def _(eng):
                for f in prog["dve"]:
                    f(eng)

            @block.scalar
            def _(eng):
                for f in prog["act"]:
                    f(eng)

            @block.gpsimd
            def _(eng):
                for f in prog["pool"]:
                    f(eng)

            @block.sync
            def _(eng):
                for f in prog["sp"]:
                    f(eng)


class Rot:
    def __init__(self, items):
        self.items = items
        self.i = 0

    def next(self):
        it = self.items[self.i % len(self.items)]
        self.i += 1
        return it


W_SPECS = [
    ("rel_bias", (32, 16)),
    ("ffn1_norm", (2, 1024)), ("ffn1_w_gate", (2, 1024, 2816)), ("ffn1_w_up", (2, 1024, 2816)),
    ("ffn1_w_down", (2, 2816, 1024)), ("mix_norm", (2, 1024)), ("ffn2_norm", (2, 1024)),
    ("ffn2_w_gate", (2, 1024, 2816)), ("ffn2_w_up", (2, 1024, 2816)), ("ffn2_w_down", (2, 2816, 1024)),
    ("ab_w_in", (1, 1024, 2720)), ("mla_q_a_norm", (1, 256)), ("mla_w_q_b", (1, 256, 768)),
    ("mla_kv_a_norm", (1, 128)), ("mla_w_kv_b", (1, 128, 1024)), ("mla_q_norm", (1, 96)),
    ("mla_k_norm", (1, 96)), ("dil_q_norm", (1, 64)), ("dil_k_norm", (1, 64)),
    ("ab_w_out", (1, 768, 1024)), ("cd_w_in", (1, 1024, 2072)), ("swa_q_norm", (1, 64)),
    ("swa_k_norm", (1, 64)), ("swa_sinks", (1, 8)), ("nsa_q_norm", (1, 64)), ("nsa_k_norm", (1, 3, 64)),
    ("nsa_cmp_pos", (1, 2, 32, 64)), ("nsa_cmp_w1", (1, 2, 2048, 128)), ("nsa_cmp_w2", (1, 2, 128, 64)),
    ("cd_w_out", (1, 1024, 1024)),
]


class Ctx:
    pass


NEGM = -30000.0
KINDS = {"dil0": (129, 1, 383), "dil1": (129, 4, 383), "dil2": (129, 16, 383), "swa": (128, 1, 383),
         "win": (512, 1, 767), "slc": (10 ** 9, 1, 2175)}


def t5_bucket_np(n):
    n = np.maximum(n, 0)
    nf = np.maximum(n, 1).astype(np.float32)
    large = 16 + (np.log(nf / np.float32(16)) / np.float32(math.log(2048 / 16)) * np.float32(16)).astype(np.int32)
    large = np.minimum(large, 31)
    return np.where(n < 16, n, large)


def make_consts():
    bf = ml_dtypes.bfloat16
    c = {}
    c["ident_f32"] = np.eye(128, dtype=np.float32)
    c["ident_bf"] = np.eye(128, dtype=np.float32).astype(bf)
    c["ones_bf"] = np.ones((128, 128), dtype=np.float32).astype(bf)
    c["anti_bf"] = np.eye(128, dtype=np.float32)[::-1].copy().astype(bf)
    bd = np.zeros((128, 128), np.float32)
    bd[:64, :64] = 1
    bd[64:, 64:] = 1
    c["bd64_bf"] = bd.astype(bf)
    for kind, (win, ds, L) in KINDS.items():
        oh = np.zeros((33, L), np.float32)
        n = np.arange(L) - 127
        valid = (n >= 0) & (n < win)
        b = t5_bucket_np(n * ds)
        oh[b[valid], np.arange(L)[valid]] = 1
        oh[32, ~valid] = 1
        c["oh_" + kind] = oh
    cm = np.full((128, SEQ), NEGM, np.float32)
    cc = np.arange(127)[:, None]
    tt = np.arange(SEQ)[None, :]
    cm[:127][(16 * cc + 31) <= tt] = 0
    c["cmpmask"] = cm.astype(bf)
    ov = np.zeros((128, 33), np.float32)
    ci = np.arange(127)[:, None] * 16
    sj = np.arange(32)[None, :] * 64
    ov[:127, :32] = ((ci < sj + 64) & (ci + 32 > sj)).astype(np.float32)
    ov[:127, 32] = 1
    c["ovl1"] = ov
    t = np.arange(SEQ)
    tb = t // 64
    jj = np.arange(32)
    causal = (jj[None, :] <= tb[:, None])
    forced = (jj[None, :] == 0) | (jj[None, :] == tb[:, None]) | (jj[None, :] == tb[:, None] - 1)
    c["causal01"] = causal.astype(np.float32).reshape(16, 128, 32).transpose(1, 0, 2).copy()
    sadd = np.where(causal, np.where(forced, 1e6, 0.0), -1e6).astype(np.float32)
    c["sadd"] = sadd.reshape(16, 128, 32).transpose(1, 0, 2).copy()
    es_ = np.zeros((32, 16, 128), np.float32)
    for cch in range(16):
        for p in range(128):
            es_[2 * cch + p // 64, cch, p] = -NEGM
    c["esel"] = es_.astype(bf)
    inv = np.power(np.float32(10000.0), -(np.arange(16, dtype=np.float32) / np.float32(16))).astype(np.float32)
    ang = (np.arange(SEQ, dtype=np.float32)[:, None] * inv[None, :]).astype(np.float32)
    cs_, sn_ = np.cos(ang).astype(np.float32), np.sin(ang).astype(np.float32)
    c["rope_cos"] = np.concatenate([cs_.T, cs_.T], axis=0).copy()
    c["rope_sin"] = np.concatenate([-sn_.T, sn_.T], axis=0).copy()
    kl = np.arange(128)[:, None]
    ql = np.arange(128)[None, :]
    c["tri"] = np.where(kl > ql, NEGM, 0.0).astype(np.float32).astype(bf)
    return c


CONST_DT = {"ident_f32": F32, "ident_bf": BF16, "ones_bf": BF16, "anti_bf": BF16, "bd64_bf": BF16,
            "cmpmask": BF16, "ovl1": F32, "causal01": F32, "sadd": F32, "esel": BF16,
            "rope_cos": F32, "rope_sin": F32, "tri": BF16}
for _k in KINDS:
    CONST_DT["oh_" + _k] = F32


def build_program(n_seq=SEQ_PER_CORE, phases=("ffn1", "mix", "ffn2"), layers=(0, 1), debug=False):
    nc = bass.Bass("TRN2", target_bir_lowering=False)
    C = Ctx()
    C.nc = nc
    C.d = {}
    C.d["x"] = nc.dram_tensor("x", [n_seq, SEQ, D_MODEL], F32, kind="ExternalInput")
    for name, shp in W_SPECS:
        C.d[name] = nc.dram_tensor(name, list(shp), F32, kind="ExternalInput")
    consts = make_consts()
    for name, arr in consts.items():
        C.d[name] = nc.dram_tensor(name, list(arr.shape), CONST_DT[name], kind="ExternalInput")
    C.d["y"] = nc.dram_tensor("y", [n_seq, SEQ, D_MODEL], F32, kind="ExternalOutput")
    C.Udil = nc.dram_tensor("Udil", [3, SEQ, 260], F32, kind=("ExternalOutput" if debug else "Internal"))
    C.UdilB = Buf("Udil")
    C.debug = debug
    C.dbg = {}

    def dbg_dump(name, src_ap, shape, dt, reads, dst_fn=None):
        if not C.debug:
            return
        if name not in C.dbg:
            C.dbg[name] = nc.dram_tensor("dbg_" + name, list(shape), dt, kind="ExternalOutput")
        dst = C.dbg[name].ap() if dst_fn is None else dst_fn(C.dbg[name].ap())
        C.S.op("sp", lambda e: e.dma_start(out=dst, in_=src_ap), reads=reads, dma=True)

    C.dbg_dump = dbg_dump

    with ExitStack() as es:
        S = Sched(nc, es)
        C.S = S
        C.es = es

        C.uid = 0

        def sb(name, shape, dt, stack=es):
            C.uid += 1
            return stack.enter_context(nc.sbuf_tensor("s%d_%s" % (C.uid, name), shape, dt))

        C.sb = sb
        C.banks = []
        for i in range(8):
            t = es.enter_context(nc.psum_tensor("pb%d" % i, [128, 512], F32))
            C.banks.append((t, Buf("pb%d" % i)))
        C.xT = sb("xT", [128, 8, SEQ], F32)
        C.XT = [[Buf("xT%d_%d" % (dc, t)) for t in range(NT)] for dc in range(8)]
        C.ident_f32 = sb("ident_f32", [128, 128], F32)
        C.ident_bf = sb("ident_bf", [128, 128], BF16)
        C.ones_bf = sb("ones_bf", [128, 128], BF16)
        C.CONST = Buf("const")
        C.anti_bf = sb("anti_bf", [128, 128], BF16)
        C.bd64_bf = sb("bd64_bf", [128, 128], BF16)
        for name in ("ident_f32", "ident_bf", "ones_bf", "anti_bf", "bd64_bf"):
            t = getattr(C, name)
            S.op("sp", lambda e, t=t, name=name: e.dma_start(out=t[:], in_=C.d[name].ap()),
                 writes=[C.CONST], dma=True)
        C.gains = sb("gains", [128, 3, 2, 8], F32)
        with nc.allow_non_contiguous_dma(reason="tiny gain vectors"):
            for wi, name in enumerate(("ffn1_norm", "mix_norm", "ffn2_norm")):
                for l in range(2):
                    S.op("sp", lambda e, wi=wi, l=l, name=name: e.dma_start(
                        out=C.gains[:, wi, l, :],
                        in_=C.d[name].ap()[l, :].rearrange("(c p) -> p c", p=128)),
                        writes=[C.CONST], dma=True)

            phase_init(C)
            for s in range(n_seq):
                phase_load(C, s)
                for layer in layers:
                    if "ffn1" in phases:
                        phase_ffn(C, layer, 0, s)
                    if "mix" in phases:
                        if layer == 0:
                            phase_mix0(C, s)
                        else:
                            phase_mix1(C, s)
                    if "ffn2" in phases:
                        phase_ffn(C, layer, 2, s)
                phase_store(C, s)
            S.emit()
    return nc, consts


def phase_load(C, s):
    nc, S = C.nc, C.S
    x = C.d["x"].ap()
    with ExitStack() as es:
        stage = [(C.sb("ldst%d" % i, [128, D_MODEL], F32, es), Buf()) for i in range(2)]
        brot = Rot(C.banks)
        for i in range(NCH):
            st, STB = stage[i % 2]
            S.op("sp", lambda e, st=st, i=i: e.dma_start(out=st[:], in_=x[s, i * 128:(i + 1) * 128, :]),
                 writes=[STB], dma=True)
            for half in range(2):
                bank, BB = brot.next()
                for q in range(4):
                    dc = half * 4 + q
                    S.op("pe", lambda e, bank=bank, st=st, q=q, dc=dc: e.transpose(
                        out=bank[:, q * 128:(q + 1) * 128], in_=st[:, dc * 128:(dc + 1) * 128],
                        identity=C.ident_f32[:]), reads=[STB, C.CONST], writes=[BB], sig=(q == 3))
                eng = "act" if half == 0 else "dve"
                outap = C.xT[:, half * 4:half * 4 + 4, i * 128:(i + 1) * 128]
                inap = bank[:].rearrange("p (q t) -> p q t", q=4)
                wr = [C.XT[half * 4 + q][i // 4] for q in range(4)]
                if eng == "act":
                    S.op("act", lambda e, o=outap, a=inap: e.copy(out=o, in_=a), reads=[BB], writes=wr)
                else:
                    S.op("dve", lambda e, o=outap, a=inap: e.tensor_copy(out=o, in_=a), reads=[BB], writes=wr)
    C.S.barrier()


def phase_store(C, s):
    nc, S = C.nc, C.S
    y = C.d["y"].ap()
    with ExitStack() as es:
        stage = [(C.sb("stst%d" % i, [128, D_MODEL], F32, es), Buf()) for i in range(2)]
        brot = Rot(C.banks)
        for i in range(NCH):
            st, STB = stage[i % 2]
            for half in range(2):
                bank, BB = brot.next()
                for q in range(4):
                    dc = half * 4 + q
                    S.op("pe", lambda e, bank=bank, q=q, dc=dc, i=i: e.transpose(
                        out=bank[:, q * 128:(q + 1) * 128], in_=C.xT[:, dc, i * 128:(i + 1) * 128],
                        identity=C.ident_f32[:]), reads=[C.XT[dc][i // 4], C.CONST], writes=[BB], sig=(q == 3))
                outap = st[:, half * 512:(half + 1) * 512]
                if half == 0:
                    S.op("act", lambda e, o=outap, bank=bank: e.copy(out=o, in_=bank[:]), reads=[BB], writes=[STB])
                else:
                    S.op("dve", lambda e, o=outap, bank=bank: e.tensor_copy(out=o, in_=bank[:]), reads=[BB], writes=[STB])
            S.op("sp", lambda e, st=st, i=i: e.dma_start(out=y[s, i * 128:(i + 1) * 128, :], in_=st[:]),
                 reads=[STB], dma=True)
    C.S.barrier()


def rmsnorm_tile(C, t, g, hT, H, sqt, SQ, rs):
    S = C.S
    sbank, SB = C.banks[7]
    tl = slice(t * 512, (t + 1) * 512)
    xr = [C.XT[dc][t] for dc in range(8)]
    S.op("act", lambda e: e.activation(out=sqt[:], in_=C.xT[:, :, tl], func=AF.Square), reads=xr, writes=[SQ])
    for dc in range(8):
        S.op("pe", lambda e, dc=dc: e.matmul(sbank[:], lhsT=C.ones_bf[:], rhs=sqt[:, dc, :],
                                             start=(dc == 0), stop=(dc == 7)),
             reads=[SQ, C.CONST], writes=[SB], sig=(dc == 7))
    rst, RS = rs[t % 2]
    S.op("act", lambda e: e.activation(out=rst[:], in_=sbank[:], func=AF.Sqrt, scale=1.0 / D_MODEL, bias=EPS),
         reads=[SB], writes=[RS])
    S.op("dve", lambda e: e.reciprocal(out=rst[:], in_=rst[:]), reads=[RS], writes=[RS])
    for dc in range(8):
        S.op("dve", lambda e, dc=dc: e.scalar_tensor_tensor(
            out=hT[:, dc, tl], in0=C.xT[:, dc, tl], scalar=g[:, dc:dc + 1], in1=rst[:],
            op0=ALU.mult, op1=ALU.mult), reads=[C.XT[dc][t], RS, C.CONST], writes=[H[t]])


def rmsnorm_hT(C, es_outer, g, hT, H):
    with ExitStack() as es:
        sqt = C.sb("sq", [128, 8, 512], BF16, es)
        SQ = Buf()
        rs = [(C.sb("rs%d" % i, [128, 512], F32, es), Buf()) for i in range(2)]
        for t in range(NT):
            rmsnorm_tile(C, t, g, hT, H, sqt, SQ, rs)
    C.S.barrier()


def phase_ffn(C, layer, which, s):
    nc, S = C.nc, C.S
    pre = "ffn1" if which == 0 else "ffn2"
    Wg = C.d[pre + "_w_gate"].ap()[layer]
    Wu = C.d[pre + "_w_up"].ap()[layer]
    Wd = C.d[pre + "_w_down"].ap()[layer]
    g = C.gains[:, which, layer, :]
    with ExitStack() as es:
        hT = C.sb("hT", [128, 8, SEQ], BF16, es)
        H = [Buf("h%d" % t) for t in range(NT)]
        nwb = 2
        wb = []
        for i in range(nwb):
            wb.append(dict(g=C.sb("wg%d" % i, [128, 8, 512], BF16, es), u=C.sb("wu%d" % i, [128, 8, 512], BF16, es),
                           d=C.sb("wd%d" % i, [128, 4, D_MODEL], BF16, es), G=Buf(), U=Buf(), D=Buf()))
        actT = C.sb("actT", [128, 2, 4, 512], BF16, es)
        ACT = [[Buf() for j in range(4)] for k in range(2)]
        silu = [(C.sb("silu%d" % i, [128, 512], F32, es), Buf()) for i in range(2)]
        bk = C.banks
        gate_rot = Rot([bk[0], bk[1]])
        up_rot = Rot([bk[2], bk[3]])
        down_rot = Rot([bk[4], bk[5], bk[6]])
        stat_bank = bk[7]
        silu_rot = Rot(silu)

        groups = [(0, 4), (4, 4), (8, 4), (12, 4), (16, 4), (20, 2)]

        def load_w(gi):
            f0, nf = groups[gi]
            w = wb[gi % nwb]
            fw = nf * 128
            S.op("pool", lambda e: e.dma_start(out=w["g"][:, :, 0:fw],
                                               in_=Wg[:, f0 * 128:f0 * 128 + fw].rearrange("(c p) f -> p c f", p=128)),
                 writes=[w["G"]], dma=True)
            S.op("pool", lambda e: e.dma_start(out=w["u"][:, :, 0:fw],
                                               in_=Wu[:, f0 * 128:f0 * 128 + fw].rearrange("(c p) f -> p c f", p=128)),
                 writes=[w["U"]], dma=True)
            S.op("pool", lambda e: e.dma_start(out=w["d"][:, 0:nf, :],
                                               in_=Wd[f0 * 128:f0 * 128 + fw, :].rearrange("(c p) d -> p c d", p=128)),
                 writes=[w["D"]], dma=True)

        load_w(0)
        load_w(1)

        n_sq = C.sb("nsq", [128, 8, 512], BF16, es)
        N_SQ = Buf()
        n_rs = [(C.sb("nrs%d" % i, [128, 512], F32, es), Buf()) for i in range(2)]
        rmsnorm_tile(C, 0, g, hT, H, n_sq, N_SQ, n_rs)

        items = [(gi, t) for gi in range(len(groups)) for t in range(NT)]

        def GU(k):
            gi, t = items[k]
            f0, nf = groups[gi]
            w = wb[gi % nwb]
            tl = slice(t * 512, (t + 1) * 512)
            for j in range(nf):
                pg, PG = gate_rot.next()
                pu, PU = up_rot.next()
                for kc in range(8):
                    S.op("pe", lambda e, pg=pg, kc=kc, j=j: e.matmul(
                        pg[:], lhsT=w["g"][:, kc, j * 128:(j + 1) * 128], rhs=hT[:, kc, tl],
                        start=(kc == 0), stop=(kc == 7)), reads=[w["G"], H[t]], writes=[PG], sig=(kc == 7))
                for kc in range(8):
                    S.op("pe", lambda e, pu=pu, kc=kc, j=j: e.matmul(
                        pu[:], lhsT=w["u"][:, kc, j * 128:(j + 1) * 128], rhs=hT[:, kc, tl],
                        start=(kc == 0), stop=(kc == 7)), reads=[w["U"], H[t]], writes=[PU], sig=(kc == 7))
                sl, SL = silu_rot.next()
                S.op("act", lambda e, sl=sl, pg=pg: e.activation(out=sl[:], in_=pg[:], func=AF.Silu),
                     reads=[PG], writes=[SL])
                S.op("dve", lambda e, sl=sl, pu=pu, j=j: e.tensor_tensor(
                    out=actT[:, k % 2, j, :], in0=pu[:], in1=sl[:], op=ALU.mult),
                    reads=[PU, SL], writes=[ACT[k % 2][j]])

        def DN(k):
            gi, t = items[k]
            f0, nf = groups[gi]
            w = wb[gi % nwb]
            tl = slice(t * 512, (t + 1) * 512)
            for dc in range(8):
                pd, PD = down_rot.next()
                for j in range(nf):
                    S.op("pe", lambda e, pd=pd, j=j, dc=dc: e.matmul(
                        pd[:], lhsT=w["d"][:, j, dc * 128:(dc + 1) * 128], rhs=actT[:, k % 2, j, :],
                        start=(j == 0), stop=(j == nf - 1)), reads=[w["D"], ACT[k % 2][j]], writes=[PD],
                        sig=(j == nf - 1))
                S.op("dve", lambda e, pd=pd, dc=dc: e.scalar_tensor_tensor(
                    out=C.xT[:, dc, tl], in0=pd[:], scalar=0.5, in1=C.xT[:, dc, tl],
                    op0=ALU.mult, op1=ALU.add), reads=[PD, C.XT[dc][t]], writes=[C.XT[dc][t]])

        for k in range(len(items)):
            gi, t = items[k]
            GU(k)
            if gi == 0 and t + 1 < NT:
                rmsnorm_tile(C, t + 1, g, hT, H, n_sq, N_SQ, n_rs)
            if k >= 1:
                DN(k - 1)
            if t == 0 and gi >= 1 and gi + 1 < len(groups):
                load_w(gi + 1)
        DN(len(items) - 1)
    C.S.barrier()


def phase_init(C):
    nc, S = C.nc, C.S
    C.F = {}
    with ExitStack() as es:
        tab = C.sb("tab", [33, 16], F32, es)
        TAB = Buf()
        S.op("sp", lambda e: e.dma_start(out=tab[0:32, :], in_=C.d["rel_bias"].ap()), writes=[TAB], dma=True)
        S.op("dve", lambda e: e.memset(tab[32:33, :], NEGM), writes=[TAB])
        oh = [(C.sb("oh%d" % i, [33, 512], F32, es), Buf()) for i in range(2)]
        fo = [(C.sb("fo%d" % i, [16, 512], F32, es), Buf()) for i in range(2)]
        brot = Rot(C.banks)
        k = 0
        for kind, (win, ds, L) in KINDS.items():
            C.F[kind] = nc.dram_tensor("F_" + kind, [16, L], F32, kind="Internal")
            C.FB = getattr(C, "FB", Buf("F"))
            for m0 in range(0, L, 512):
                n = min(512, L - m0)
                oht, OH = oh[k % 2]
                fot, FO = fo[k % 2]
                k += 1
                bank, BB = brot.next()
                S.op("sp", lambda e, oht=oht, m0=m0, n=n, kind=kind: e.dma_start(
                    out=oht[:, 0:n], in_=C.d["oh_" + kind].ap()[:, m0:m0 + n]), writes=[OH], dma=True)
                S.op("pe", lambda e, bank=bank, oht=oht, n=n: e.matmul(
                    bank[0:16, 0:n], lhsT=tab[:, :], rhs=oht[:, 0:n], start=True, stop=True),
                    reads=[TAB, OH], writes=[BB])
                S.op("dve", lambda e, bank=bank, fot=fot, n=n: e.tensor_copy(out=fot[:, 0:n], in_=bank[0:16, 0:n]),
                     reads=[BB], writes=[FO])
                S.op("sp", lambda e, fot=fot, m0=m0, n=n, kind=kind: e.dma_start(
                    out=C.F[kind].ap()[:, m0:m0 + n], in_=fot[:, 0:n]), reads=[FO], writes=[C.FB], dma=True)
    C.S.barrier()


def load_hankel(C, dst, DST, kind, o, h0, nh):
    S = C.S
    L = KINDS[kind][2]
    src = bass.AP(tensor=C.F[kind], offset=h0 * L + 128 * o, ap=[[1, 128], [L, nh], [1, 128]])
    return S.op("pool", lambda e: e.dma_start(out=dst, in_=src), reads=[C.FB], writes=[DST], dma=True)


class AttnPipe:
    def __init__(self, C, st_banks, pts):
        self.C = C
        self.st = Rot(st_banks)
        self.pt = Rot(pts)
        self.pending = None

    def unit(self, fills, pvs, krows=128, ncols=512, exp_f32=None, e0=0):
        C, S = self.C, self.C.S
        bank, BB = self.st.next()
        nf = len(fills)
        for fi, f in enumerate(fills):
            m = f.get("m", 128)
            c0 = f.get("c0", 0)
            n = f.get("n", ncols)
            S.op("pe", lambda e, f=f, m=m, c0=c0, n=n: e.matmul(
                bank[0:m, c0:c0 + n], lhsT=f["lhsT"], rhs=f["rhs"], start=f["start"], stop=f["stop"]),
                reads=f["reads"], writes=[BB], sig=(fi == nf - 1))
        pt, PT = self.pt.next()
        if exp_f32 is not None:
            ef, EF = exp_f32
            S.op("act", lambda e: e.activation(out=ef[0:krows, 0:ncols], in_=bank[0:krows, 0:ncols], func=AF.Exp),
                 reads=[BB], writes=[EF])
            S.op("dve", lambda e: e.tensor_copy(out=pt[0:krows, 0:ncols], in_=ef[0:krows, 0:ncols]),
                 reads=[EF], writes=[PT])
        else:
            S.op("act", lambda e: e.activation(out=pt[0:krows, e0:ncols], in_=bank[0:krows, e0:ncols], func=AF.Exp),
                 reads=[BB], writes=[PT])
        prev = self.pending
        self.pending = (pt, PT, pvs, krows)
        if prev is not None:
            self._emit_pv(prev)

    def _emit_pv(self, pend):
        S = self.C.S
        pt, PT, pvs, krows = pend
        npv = len(pvs)
        for pi, pv in enumerate(pvs):
            S.op("pe", lambda e, pv=pv: e.matmul(
                pv["out"], lhsT=(pv["lhsT"] if "lhsT" in pv else pt[0:krows, pv["c0"]:pv["c0"] + 128]), rhs=pv["v"],
                start=pv["start"], stop=pv["stop"]),
                reads=[PT] + pv["reads"], writes=[pv["OUT"]], sig=(pi == npv - 1))

    def flush(self):
        if self.pending is not None:
            self._emit_pv(self.pending)
            self.pending = None


def out_slot(banks2, slot):
    bank, BB = banks2[slot // 7]
    c = (slot % 7) * 65
    return bank[:, c:c + 65], BB


def fm_norm(C, zbank, ZB, dest, DEST, gain, sf, dh, scr, n=512, view=None):
    S = C.S
    sqb, SQ = scr["sq"].next()
    sbank, SB = scr["sbank"].next()
    rsb, RS = scr["rs"].next()
    S.op("act", lambda e: e.activation(out=sqb[:, 0:n], in_=zbank[:, 0:n], func=AF.Square), reads=[ZB], writes=[SQ])
    S.op("pe", lambda e: e.matmul(sbank[:, 0:n], lhsT=C.bd64_bf[:], rhs=sqb[:, 0:n], start=True, stop=True),
         reads=[SQ, C.CONST], writes=[SB])
    S.op("act", lambda e: e.activation(out=rsb[:, 0:n], in_=sbank[:, 0:n], func=AF.Sqrt, scale=1.0 / (dh * sf * sf),
                                       bias=EPS / (sf * sf)), reads=[SB], writes=[RS])
    S.op("dve", lambda e: e.reciprocal(out=rsb[:, 0:n], in_=rsb[:, 0:n]), reads=[RS], writes=[RS])
    vw = (lambda a: a) if view is None else view
    zin, rin = vw(zbank[:, 0:n]), vw(rsb[:, 0:n])
    S.op("dve", lambda e: e.scalar_tensor_tensor(out=dest, in0=zin, scalar=gain, in1=rin,
                                                 op0=ALU.mult, op1=ALU.mult),
         reads=[ZB, RS, C.CONST], writes=[DEST])


def out_proj_tile(C, t, mixedT, MT, wout, WO, nfc, brot):
    S = C.S
    tl = slice(t * 512, (t + 1) * 512)
    for dc in range(8):
        bank, BB = brot.next()
        for fc in range(nfc):
            S.op("pe", lambda e, bank=bank, fc=fc, dc=dc: e.matmul(
                bank[:], lhsT=wout[:, fc, dc * 128:(dc + 1) * 128], rhs=mixedT[:, fc, :],
                start=(fc == 0), stop=(fc == nfc - 1)), reads=[WO, MT], writes=[BB], sig=(fc == nfc - 1))
        S.op("dve", lambda e, bank=bank, dc=dc: e.tensor_tensor(
            out=C.xT[:, dc, tl], in0=bank[:], in1=C.xT[:, dc, tl], op=ALU.add),
            reads=[BB, C.XT[dc][t]], writes=[C.XT[dc][t]])


DIL_D = (1, 4, 16)


def phase_mix0(C, s):
    nc, S = C.nc, C.S
    Win = C.d["ab_w_in"].ap()[0]
    Wout = C.d["ab_w_out"].ap()[0]
    Wqb = C.d["mla_w_q_b"].ap()[0]
    Wkvb = C.d["mla_w_kv_b"].ap()[0]
    U = C.Udil
    UB = C.UdilB
    bk = C.banks
    sc_d = 64 ** -0.5
    sc_m = 96 ** -0.5
    with ExitStack() as es:
        cqn = C.sb("cqn", [128, 2, SEQ], BF16, es)
        ckvn = C.sb("ckvn", [128, SEQ], BF16, es)
        KR0 = C.sb("KR0", [32, SEQ], F32, es)
        sqkr = C.sb("sqkr", [32, SEQ], BF16, es)
        mla_out = C.sb("mla_out", [128, NCH, 512], BF16, es)
        gl = C.sb("gl0", [128, 16], F32, es)
        pts = [(C.sb("pt%d" % i, [128, 512], BF16, es), Buf()) for i in range(3)]
        small = C.sb("small", [128, 2, 16], F32, es)
        CQ, CKV, KRB = ([Buf() for _ in range(NT)] for _ in range(3))
        MO = [Buf() for _ in range(NCH)]
        GL, SMB = Buf(), [Buf(), Buf()]
        rope_cos = C.sb("rope_cos", [32, SEQ], F32, es)
        rope_sin = C.sb("rope_sin", [32, SEQ], F32, es)
        RT = Buf()
        S.op("sp", lambda e: e.dma_start(out=rope_cos[:], in_=C.d["rope_cos"].ap()), writes=[RT], dma=True)
        S.op("sp", lambda e: e.dma_start(out=rope_sin[:], in_=C.d["rope_sin"].ap()), writes=[RT], dma=True)

        def gload(col, rows, src):
            S.op("sp", lambda e: e.dma_start(out=gl[rows, col:col + 1], in_=src.rearrange("(d o) -> d o", o=1)),
                 writes=[GL], dma=True)

        dq, dk = C.d["dil_q_norm"].ap()[0], C.d["dil_k_norm"].ap()[0]
        qa, kva = C.d["mla_q_a_norm"].ap()[0], C.d["mla_kv_a_norm"].ap()[0]
        mq, mk = C.d["mla_q_norm"].ap()[0], C.d["mla_k_norm"].ap()[0]
        for hh in range(2):
            gload(0, slice(hh * 64, hh * 64 + 64), dq)
            gload(1, slice(hh * 64, hh * 64 + 64), dk)
        gload(2, slice(0, 128), qa[0:128])
        gload(3, slice(0, 128), qa[128:256])
        gload(4, slice(0, 128), kva)
        for col, v in ((5, mq), (8, mk)):
            gload(col, slice(0, 64), v[0:64])
            gload(col + 1, slice(0, 32), v[64:96])
            gload(col + 2, slice(0, 16), v[80:96])
            gload(col + 2, slice(16, 32), v[64:80])

        with ExitStack() as es2:
            hT = C.sb("hT", [128, 8, SEQ], BF16, es2)
            H = [Buf() for _ in range(NT)]
            rmsnorm_hT(C, es2, C.gains[:, 1, 0, :], hT, H)
            wb = C.sb("wb0", [128, 8, 768], BF16, es2)
            WB = Buf()
            scr = dict(sq=Rot([(C.sb("fsq%d" % i, [128, 512], BF16, es2), Buf()) for i in range(2)]),
                       rs=Rot([(C.sb("frs%d" % i, [128, 512], F32, es2), Buf()) for i in range(2)]),
                       sbank=Rot([bk[3], bk[4]]))
            tmp = Rot([(C.sb("tmp%d" % i, [32, 512], F32, es2), Buf()) for i in range(2)])
            zrot = Rot([bk[0], bk[1], bk[2]])
            trot = Rot([bk[5], bk[6]])

            def wload(pieces):
                for (a, c0, c1) in pieces:
                    S.op("pool", lambda e, a=a, c0=c0, c1=c1: e.dma_start(
                        out=wb[:, :, a:a + (c1 - c0)], in_=Win[:, c0:c1].rearrange("(c p) f -> p c f", p=128)),
                        writes=[WB], dma=True)

            def proj(t, lhs_fn, m=128):
                tl = slice(t * 512, (t + 1) * 512)
                zb, ZB = zrot.next()
                for kc in range(8):
                    l_ = lhs_fn(kc)
                    S.op("pe", lambda e, kc=kc, zb=zb, tl=tl, l_=l_: e.matmul(
                        zb[0:m, :], lhsT=l_, rhs=hT[:, kc, tl], start=(kc == 0), stop=(kc == 7)),
                        reads=[WB, H[t]], writes=[ZB], sig=(kc == 7))
                return tl, zb, ZB

            wload([(0, 0, 416), (416, 400, 416), (432, 384, 400)])
            for t in range(NT):
                zs = [proj(t, lambda kc, c=c: wb[:, kc, c * 128:(c + 1) * 128]) for c in range(2)]
                tl = zs[0][0]
                sbank, SB = scr["sbank"].next()
                for c in range(2):
                    sqb, SQ = scr["sq"].next()
                    S.op("act", lambda e, sqb=sqb, zb=zs[c][1]: e.activation(out=sqb[:], in_=zb[:], func=AF.Square),
                         reads=[zs[c][2]], writes=[SQ])
                    S.op("pe", lambda e, sqb=sqb, c=c, sbank=sbank: e.matmul(sbank[:], lhsT=C.ones_bf[:], rhs=sqb[:],
                                                                           start=(c == 0), stop=(c == 1)),
                         reads=[SQ, C.CONST], writes=[SB], sig=(c == 1))
                rsb, RS = scr["rs"].next()
                S.op("act", lambda e, rsb=rsb, sbank=sbank: e.activation(out=rsb[:], in_=sbank[:], func=AF.Sqrt,
                                                                        scale=1.0 / 256, bias=EPS), reads=[SB], writes=[RS])
                S.op("dve", lambda e, rsb=rsb: e.reciprocal(out=rsb[:], in_=rsb[:]), reads=[RS], writes=[RS])
                for c in range(2):
                    S.op("dve", lambda e, c=c, rsb=rsb, zb=zs[c][1], tl=tl: e.scalar_tensor_tensor(
                        out=cqn[:, c, tl], in0=zb[:], scalar=gl[:, 2 + c:3 + c], in1=rsb[:], op0=ALU.mult, op1=ALU.mult),
                        reads=[zs[c][2], RS, GL], writes=[CQ[t]])
                tl, zb, ZB = proj(t, lambda kc: wb[:, kc, 256:384])
                sbank, SB = scr["sbank"].next()
                sqb, SQ = scr["sq"].next()
                S.op("act", lambda e, sqb=sqb, zb=zb: e.activation(out=sqb[:], in_=zb[:], func=AF.Square), reads=[ZB], writes=[SQ])
                S.op("pe", lambda e, sqb=sqb, sbank=sbank: e.matmul(sbank[:], lhsT=C.ones_bf[:], rhs=sqb[:], start=True, stop=True),
                     reads=[SQ, C.CONST], writes=[SB])
                rsb, RS = scr["rs"].next()
                S.op("act", lambda e, rsb=rsb, sbank=sbank: e.activation(out=rsb[:], in_=sbank[:], func=AF.Sqrt,
                                                                        scale=1.0 / 128, bias=EPS), reads=[SB], writes=[RS])
                S.op("dve", lambda e, rsb=rsb: e.reciprocal(out=rsb[:], in_=rsb[:]), reads=[RS], writes=[RS])
                S.op("dve", lambda e, rsb=rsb, zb=zb, tl=tl: e.scalar_tensor_tensor(
                    out=ckvn[:, tl], in0=zb[:], scalar=gl[:, 4:5], in1=rsb[:], op0=ALU.mult, op1=ALU.mult),
                    reads=[ZB, RS, GL], writes=[CKV[t]])
                tl, z1, Z1 = proj(t, lambda kc: wb[:, kc, 384:416], m=32)
                tl, z2, Z2 = proj(t, lambda kc: wb[:, kc, 416:448], m=32)
                t1, T1 = tmp.next()
                t2, T2 = tmp.next()
                S.op("act", lambda e, z1=z1, tl=tl: e.activation(out=sqkr[:, tl], in_=z1[0:32, :], func=AF.Square),
                     reads=[Z1], writes=[KRB[t]])
                S.op("dve", lambda e, z1=z1, t1=t1, tl=tl: e.scalar_tensor_tensor(
                    out=t1[:], in0=z1[0:32, :], scalar=gl[0:32, 9:10], in1=rope_cos[:, tl], op0=ALU.mult, op1=ALU.mult),
                    reads=[Z1, GL, RT], writes=[T1])
                S.op("dve", lambda e, z2=z2, t2=t2, tl=tl: e.scalar_tensor_tensor(
                    out=t2[:], in0=z2[0:32, :], scalar=gl[0:32, 10:11], in1=rope_sin[:, tl], op0=ALU.mult, op1=ALU.mult),
                    reads=[Z2, GL, RT], writes=[T2])
                S.op("dve", lambda e, t1=t1, t2=t2, tl=tl: e.tensor_tensor(out=KR0[:, tl], in0=t1[:], in1=t2[:], op=ALU.add),
                     reads=[T1, T2], writes=[KRB[t]])

            qTd = C.sb("qTd", [128, 2, SEQ], BF16, es2)
            kTd = C.sb("kTd", [128, 2, SEQ], BF16, es2)
            v_dil = C.sb("v_dil", [128, NCH, 4, 65], BF16, es2)
            hk_dil = C.sb("hk_dil", [128, 2, 4, 128], BF16, es2)
            ust = Rot([(C.sb("ust%d" % i, [128, 260], F32, es2), Buf()) for i in range(2)])
            QD, KD = [Buf() for _ in range(NT)], [Buf() for _ in range(NT)]
            VD = [Buf() for _ in range(NCH)]
            HK = Buf()
            S.op("pool", lambda e: e.memset(v_dil[:], 1.0), writes=VD)
            pipe = AttnPipe(C, [bk[0], bk[1]], pts)
            obrot = Rot([bk[2]])
            for g in range(3):
                d = DIL_D[g]
                Sd = SEQ // d
                nchunk = Sd // 128
                base = 416 + g * 256
                wload([(0, base, base + 256), (256, base + 768, base + 1024), (512, base + 1536, base + 1792)])
                for o in range(2):
                    load_hankel(C, hk_dil[:, o], HK, "dil%d" % g, o, g * 4, 4)
                pw = 512 // d
                for t in range(NT):
                    for (off, dst, DST, gcol, sf) in ((0, qTd, QD, 0, sc_d), (256, kTd, KD, 1, 1.0)):
                        for pair in range(2):
                            tl, zb, ZB = proj(t, lambda kc, off=off, pair=pair: wb[:, kc, off + pair * 128:off + (pair + 1) * 128])
                            dview = dst[:, pair, :].rearrange("p (r u) -> p r u", r=d)[:, :, t * pw:(t + 1) * pw]
                            fm_norm(C, zb, ZB, dview, DST[t], gl[:, gcol:gcol + 1], sf, 64.0, scr,
                                    view=lambda a, d=d: a.rearrange("p (u r) -> p r u", r=d))
                for cc in range(NCH):
                    r, ic = cc // nchunk, cc % nchunk
                    t0 = r + d * 128 * ic
                    tb, TB = trot.next()
                    for kc in range(8):
                        S.op("pe", lambda e, kc=kc, tb=tb, t0=t0, d=d: e.matmul(
                            tb[:, 0:256], lhsT=hT[:, kc, t0:t0 + d * 127 + 1:d], rhs=wb[:, kc, 512:768],
                            start=(kc == 0), stop=(kc == 7)), reads=[WB] + H, writes=[TB], sig=(kc == 7))
                    S.op("act", lambda e, tb=tb, cc=cc: e.copy(out=v_dil[:, cc, :, 0:64],
                                                               in_=tb[:, 0:256].rearrange("p (h c) -> p h c", h=4)),
                         reads=[TB], writes=[VD[cc]])
                for cc in range(NCH):
                    r, ic = cc // nchunk, cc % nchunk
                    cs = [c for c in (ic - 1, ic) if c >= 0]
                    ob, OB = obrot.next()
                    obv = ob[:, 0:260].rearrange("p (h c) -> p h c", c=65)
                    qpos = cc * 128
                    for pair in range(2):
                        fills, pvs = [], []
                        blk = 0
                        nblk = 2 * len(cs)
                        for hh in range(2):
                            hs = slice(hh * 64, hh * 64 + 64)
                            for c in cs:
                                o = ic - c
                                kc_ = r * nchunk + c
                                fills.append(dict(c0=blk * 128, n=128, lhsT=kTd[hs, pair, kc_ * 128:(kc_ + 1) * 128],
                                                  rhs=qTd[hs, pair, qpos:qpos + 128], start=(blk == 0), stop=False,
                                                  reads=KD + QD))
                                fills.append(dict(c0=blk * 128, n=128, lhsT=C.anti_bf[:], rhs=hk_dil[:, o, pair * 2 + hh, :],
                                                  start=False, stop=(blk == nblk - 1), reads=[C.CONST, HK]))
                                pvs.append(dict(out=obv[:, pair * 2 + hh, :], OUT=OB, c0=blk * 128,
                                                v=v_dil[:, kc_, pair * 2 + hh, :],
                                                start=(pair == 0 and blk == 0), stop=(pair == 1 and blk == nblk - 1),
                                                reads=[VD[kc_]]))
                                blk += 1
                        pipe.unit(fills, pvs, ncols=nblk * 128)
                    pipe.flush()
                    us, US = ust.next()
                    S.op("act", lambda e, us=us, ob=ob: e.copy(out=us[:], in_=ob[:, 0:260]), reads=[OB], writes=[US])
                    dstU = U.ap()[g].rearrange("(u r) c -> r u c", r=d)[r, ic * 128:(ic + 1) * 128, :]
                    S.op("sp", lambda e, us=us, dstU=dstU: e.dma_start(out=dstU, in_=us[:]), reads=[US], writes=[UB], dma=True)
        S.barrier()

        with ExitStack() as es3:
            wqb = C.sb("wqb", [128, 2, 768], BF16, es3)
            wqsw = C.sb("wqsw", [128, 2, 8, 32], BF16, es3)
            wkvb = C.sb("wkvb", [128, 1024], BF16, es3)
            v_mla = C.sb("v_mla", [128, NCH, 8, 65], BF16, es3)
            tri = C.sb("tri", [128, 128], BF16, es3)
            W2, VM = Buf(), [Buf() for _ in range(NCH)]
            S.op("sp", lambda e: e.dma_start(out=tri[:], in_=C.d["tri"].ap()), writes=[RT], dma=True)
            S.op("pool", lambda e: e.dma_start(out=wqb[:], in_=Wqb.rearrange("(c p) f -> p c f", p=128)), writes=[W2], dma=True)
            wq4 = Wqb.rearrange("(c p) (h f) -> p c h f", p=128, f=96)
            for c_ in range(2):
                S.op("pool", lambda e, c_=c_: e.dma_start(out=wqsw[:, c_, :, 0:16], in_=wq4[:, c_, :, 80:96]), writes=[W2], dma=True)
                S.op("pool", lambda e, c_=c_: e.dma_start(out=wqsw[:, c_, :, 16:32], in_=wq4[:, c_, :, 64:80]), writes=[W2], dma=True)
            S.op("pool", lambda e: e.dma_start(out=wkvb[:], in_=Wkvb), writes=[W2], dma=True)
            wkv_v = C.sb("wkv_v", [128, 8, 64], BF16, es3)
            S.op("pool", lambda e: e.dma_start(out=wkv_v[:], in_=Wkvb.rearrange("p (h c) -> p h c", c=128)[:, :, 64:128]),
                 writes=[W2], dma=True)
            S.op("pool", lambda e: e.memset(v_mla[:], 1.0), writes=VM)
            trot = Rot([bk[5], bk[6]])
            for i in range(NCH):
                tb, TB = trot.next()
                S.op("pe", lambda e, tb=tb, i=i: e.matmul(
                    tb[:], lhsT=ckvn[:, i * 128:(i + 1) * 128],
                    rhs=wkv_v[:].rearrange("p h c -> p (h c)"), start=True, stop=True),
                    reads=[W2, CKV[i // 4]], writes=[TB])
                S.op("act", lambda e, tb=tb, i=i: e.copy(out=v_mla[:, i, :, 0:64],
                                                         in_=tb[:].rearrange("p (h c) -> p h c", c=64)),
                     reads=[TB], writes=[VM[i]])
            hd = [dict(qn=C.sb("qn%d" % k, [64, SEQ], BF16, es3), qr=C.sb("qr%d" % k, [32, SEQ], BF16, es3),
                       kn=C.sb("kn%d" % k, [64, SEQ], BF16, es3), kr=C.sb("kr%d" % k, [32, SEQ], BF16, es3),
                       B=[Buf() for _ in range(NT)]) for k in range(2)]
            sqs = Rot([(C.sb("msq%d" % i, [64, 3, 512], BF16, es3), Buf()) for i in range(2)])
            rss = Rot([(C.sb("mrs%d" % i, [64, 2, 512], F32, es3), Buf()) for i in range(2)])
            tmp = Rot([(C.sb("mtmp%d" % i, [32, 512], F32, es3), Buf()) for i in range(2)])
            zrot = Rot([bk[2], bk[3], bk[4]])
            srot = Rot([bk[5], bk[6]])
            pipe = AttnPipe(C, [bk[0], bk[1]], pts)
            obrot = Rot([bk[7]])
            par = 0
            for h in range(8):
                X = hd[h % 2]
                for t in range(NT):
                    tl = slice(t * 512, (t + 1) * 512)

                    def p2(lhs_fn, m, nk, src, SRC):
                        zb, ZB = zrot.next()
                        for kc in range(nk):
                            l_, r_ = lhs_fn(kc), src(kc)
                            S.op("pe", lambda e, kc=kc, zb=zb, l_=l_, r_=r_, m=m, nk=nk: e.matmul(
                                zb[0:m, :], lhsT=l_, rhs=r_, start=(kc == 0), stop=(kc == nk - 1)),
                                 reads=[W2, SRC], writes=[ZB], sig=(kc == nk - 1))
                        return zb, ZB

                    cq = lambda kc: cqn[:, kc, tl]
                    zq, ZQ = p2(lambda kc: wqb[:, kc, h * 96:h * 96 + 64], 64, 2, cq, CQ[t])
                    zr, ZR = p2(lambda kc: wqb[:, kc, h * 96 + 64:h * 96 + 96], 32, 2, cq, CQ[t])
                    zw, ZW = p2(lambda kc: wqsw[:, kc, h, :], 32, 2, cq, CQ[t])
                    sq3, SQ3 = sqs.next()
                    rs2, RS2 = rss.next()
                    S.op("act", lambda e, sq3=sq3, zq=zq: e.activation(out=sq3[0:64, 0, :], in_=zq[0:64, :], func=AF.Square),
                         reads=[ZQ], writes=[SQ3])
                    S.op("act", lambda e, sq3=sq3, zr=zr: e.activation(out=sq3[0:32, 1, :], in_=zr[0:32, :], func=AF.Square),
                         reads=[ZR], writes=[SQ3])
                    sb1, SB1 = srot.next()
                    S.op("pe", lambda e, sb1=sb1, sq3=sq3: e.matmul(sb1[0:64, :], lhsT=C.ones_bf[0:64, 0:64], rhs=sq3[0:64, 0, :],
                                                                   start=True, stop=False), reads=[SQ3, C.CONST], writes=[SB1], sig=False)
                    S.op("pe", lambda e, sb1=sb1, sq3=sq3: e.matmul(sb1[0:64, :], lhsT=C.ones_bf[0:32, 0:64], rhs=sq3[0:32, 1, :],
                                                                   start=False, stop=True), reads=[SQ3, C.CONST], writes=[SB1])
                    S.op("act", lambda e, sb1=sb1, rs2=rs2: e.activation(
                        out=rs2[0:64, 0, :], in_=sb1[0:64, :], func=AF.Sqrt, scale=1.0 / (96 * sc_m * sc_m), bias=EPS / (sc_m * sc_m)),
                        reads=[SB1], writes=[RS2])
                    S.op("dve", lambda e, rs2=rs2: e.reciprocal(out=rs2[0:64, 0, :], in_=rs2[0:64, 0, :]), reads=[RS2], writes=[RS2])
                    S.op("dve", lambda e, rs2=rs2, zq=zq, tl=tl, X=X: e.scalar_tensor_tensor(
                        out=X["qn"][:, tl], in0=zq[0:64, :], scalar=gl[0:64, 5:6], in1=rs2[0:64, 0, :], op0=ALU.mult, op1=ALU.mult),
                        reads=[ZQ, RS2, GL], writes=[X["B"][t]])
                    t1, T1 = tmp.next()
                    t2, T2 = tmp.next()
                    S.op("dve", lambda e, zr=zr, t1=t1, tl=tl: e.scalar_tensor_tensor(
                        out=t1[:], in0=zr[0:32, :], scalar=gl[0:32, 6:7], in1=rope_cos[:, tl], op0=ALU.mult, op1=ALU.mult),
                        reads=[ZR, GL, RT], writes=[T1])
                    S.op("dve", lambda e, zw=zw, t2=t2, tl=tl: e.scalar_tensor_tensor(
                        out=t2[:], in0=zw[0:32, :], scalar=gl[0:32, 7:8], in1=rope_sin[:, tl], op0=ALU.mult, op1=ALU.mult),
                        reads=[ZW, GL, RT], writes=[T2])
                    S.op("dve", lambda e, t1=t1, t2=t2: e.tensor_tensor(out=t1[:], in0=t1[:], in1=t2[:], op=ALU.add),
                         reads=[T1, T2], writes=[T1])
                    S.op("dve", lambda e, t1=t1, rs2=rs2, tl=tl, X=X: e.tensor_tensor(
                        out=X["qr"][:, tl], in0=t1[:], in1=rs2[0:32, 0, :], op=ALU.mult), reads=[T1, RS2], writes=[X["B"][t]])
                    zk, ZK = p2(lambda kc: wkvb[:, h * 128:h * 128 + 64], 64, 1, lambda kc: ckvn[:, tl], CKV[t])
                    S.op("act", lambda e, sq3=sq3, zk=zk: e.activation(out=sq3[0:64, 2, :], in_=zk[0:64, :], func=AF.Square),
                         reads=[ZK], writes=[SQ3])
                    sb2, SB2 = srot.next()
                    S.op("pe", lambda e, sb2=sb2, sq3=sq3: e.matmul(sb2[0:64, :], lhsT=C.ones_bf[0:64, 0:64], rhs=sq3[0:64, 2, :],
                                                                   start=True, stop=False), reads=[SQ3, C.CONST], writes=[SB2], sig=False)
                    S.op("pe", lambda e, sb2=sb2, tl=tl: e.matmul(sb2[0:64, :], lhsT=C.ones_bf[0:32, 0:64], rhs=sqkr[:, tl],
                                                                 start=False, stop=True), reads=[KRB[t], C.CONST], writes=[SB2])
                    S.op("act", lambda e, sb2=sb2, rs2=rs2: e.activation(
                        out=rs2[0:64, 1, :], in_=sb2[0:64, :], func=AF.Sqrt, scale=1.0 / 96, bias=EPS), reads=[SB2], writes=[RS2])
                    S.op("dve", lambda e, rs2=rs2: e.reciprocal(out=rs2[0:64, 1, :], in_=rs2[0:64, 1, :]), reads=[RS2], writes=[RS2])
                    S.op("dve", lambda e, rs2=rs2, zk=zk, tl=tl, X=X: e.scalar_tensor_tensor(
                        out=X["kn"][:, tl], in0=zk[0:64, :], scalar=gl[0:64, 8:9], in1=rs2[0:64, 1, :], op0=ALU.mult, op1=ALU.mult),
                        reads=[ZK, RS2, GL], writes=[X["B"][t]])
                    S.op("dve", lambda e, rs2=rs2, tl=tl, X=X: e.tensor_tensor(
                        out=X["kr"][:, tl], in0=KR0[:, tl], in1=rs2[0:32, 1, :], op=ALU.mult), reads=[KRB[t], RS2], writes=[X["B"][t]])
                for j4 in range(NT):
                    ncs = 4 * j4 + 4
                    ob, OB = obrot.next()
                    obv = ob[:, 0:260].rearrange("p (a c) -> p a c", c=65)
                    for c in range(ncs):
                        dd = c - 4 * j4
                        a0 = max(dd, 0)
                        c0 = a0 * 128
                        n = 512 - c0
                        ks = slice(c * 128, (c + 1) * 128)
                        qs = slice(j4 * 512 + c0, (j4 + 1) * 512)
                        fills = [dict(c0=c0, n=n, lhsT=X["kn"][:, ks], rhs=X["qn"][:, qs], start=True, stop=False, reads=X["B"]),
                                 dict(c0=c0, n=n, lhsT=X["kr"][:, ks], rhs=X["qr"][:, qs], start=False, stop=(dd < 0), reads=X["B"])]
                        if dd >= 0:
                            fills.append(dict(c0=c0, n=128, lhsT=C.ident_bf[:], rhs=tri[:], start=False, stop=True,
                                              reads=[C.CONST, RT]))
                        pvs = [dict(out=obv[:, a, :], OUT=OB, c0=a * 128, v=v_mla[:, c, h, :],
                                    start=(c == 0 and a == 0), stop=(c == ncs - 1 and a == 3), reads=[VM[c]])
                               for a in range(a0, 4)]
                        pipe.unit(fills, pvs, e0=c0)
                    pipe.flush()
                    par ^= 1
                    sm = small[:, par, :]
                    S.op("dve", lambda e, sm=sm, obv=obv: e.reciprocal(out=sm[:, 0:4], in_=obv[:, :, 64]), reads=[OB], writes=[SMB[par]])
                    for a in range(4):
                        S.op("dve", lambda e, sm=sm, obv=obv, a=a, j4=j4, h=h: e.tensor_scalar(
                            out=mla_out[:, j4 * 4 + a, h * 64:(h + 1) * 64], in0=obv[:, a, 0:64], scalar1=sm[:, a:a + 1],
                            scalar2=None, op0=ALU.mult), reads=[OB, SMB[par]], writes=[MO[j4 * 4 + a]])
        C.dbg_dump("mla_out", mla_out[:], [128, NCH, 512], BF16, MO)
        S.barrier()

        with ExitStack() as es4:
            wout = C.sb("wout0", [128, 6, D_MODEL], BF16, es4)
            WO = Buf()
            S.op("pool", lambda e: e.dma_start(out=wout[:], in_=Wout.rearrange("(c p) d -> p c d", p=128)), writes=[WO], dma=True)
            uts = Rot([(C.sb("ut%d" % i, [128, 3, 260], F32, es4), Buf()) for i in range(2)])
            dil_tm = Rot([(C.sb("diltm%d" % i, [128, 256], BF16, es4), Buf()) for i in range(2)])
            mixedT = Rot([(C.sb("mT0_%d" % i, [128, 6, 512], BF16, es4), Buf()) for i in range(2)])
            orot = Rot([bk[0], bk[1], bk[2]])
            mT, MTB = None, None
            par = 0
            for i in range(NCH):
                ut, UT = uts.next()
                S.op("sp", lambda e, ut=ut, i=i: e.dma_start(
                    out=ut[:], in_=U.ap()[:, i * 128:(i + 1) * 128, :].rearrange("g t c -> t g c")),
                    reads=[UB], writes=[UT], dma=True)
                S.op("pool", lambda e, ut=ut: e.tensor_tensor(out=ut[:, 0, :], in0=ut[:, 0, :], in1=ut[:, 1, :], op=ALU.add),
                     reads=[UT], writes=[UT])
                S.op("pool", lambda e, ut=ut: e.tensor_tensor(out=ut[:, 0, :], in0=ut[:, 0, :], in1=ut[:, 2, :], op=ALU.add),
                     reads=[UT], writes=[UT])
                par ^= 1
                sm = small[:, par, :]
                uv = ut[:, 0, :].rearrange("p (h c) -> p h c", c=65)
                S.op("dve", lambda e, sm=sm, uv=uv: e.reciprocal(out=sm[:, 0:4], in_=uv[:, :, 64]), reads=[UT], writes=[SMB[par]])
                dt_, DT_ = dil_tm.next()
                for hh in range(4):
                    S.op("dve", lambda e, sm=sm, uv=uv, hh=hh, dt_=dt_: e.tensor_scalar(
                        out=dt_[:, hh * 64:(hh + 1) * 64], in0=uv[:, hh, 0:64], scalar1=sm[:, hh:hh + 1], scalar2=None,
                        op0=ALU.mult), reads=[UT, SMB[par]], writes=[DT_])
                C.dbg_dump("dil_out", dt_[:], [NCH, 128, 256], BF16, [DT_], dst_fn=lambda a, i=i: a[i])
                if i % 4 == 0:
                    mT, MTB = mixedT.next()
                tb7, TB7 = bk[7]
                tb7v = tb7[:].bitcast(BF16)
                for fc in range(6):
                    src_ = mla_out[:, i, fc * 128:(fc + 1) * 128] if fc < 4 else dt_[:, (fc - 4) * 128:(fc - 3) * 128]
                    S.op("pe", lambda e, fc=fc, src_=src_, tb7v=tb7v: e.transpose(
                        out=tb7v[:, fc * 128:(fc + 1) * 128], in_=src_, identity=C.ident_bf[:]),
                        reads=[MO[i], DT_, C.CONST], writes=[TB7], sig=(fc == 5))
                S.op("act", lambda e, mT=mT, tb7v=tb7v, i=i: e.copy(
                    out=mT[:, :, (i % 4) * 128:(i % 4 + 1) * 128], in_=tb7v[:, 0:768].rearrange("p (f t) -> p f t", f=6)),
                    reads=[TB7], writes=[MTB])
                if i % 4 == 3:
                    out_proj_tile(C, i // 4, mT, MTB, wout, WO, 6, orot)
    S.barrier()


def phase_mix1(C, s):
    nc, S = C.nc, C.S
    Win = C.d["cd_w_in"].ap()[0]
    Wout = C.d["cd_w_out"].ap()[0]
    sc = 64 ** -0.5
    bk = C.banks
    with ExitStack() as es:
        swa_out = C.sb("swa_out", [128, NCH, 512], BF16, es)
        SWO = [Buf() for _ in range(NCH)]
        qT_nsa = C.sb("qT_nsa", [128, NCH, 4, 128], BF16, es)
        ksT = C.sb("ksT", [128, SEQ], BF16, es)
        kwT = C.sb("kwT", [128, SEQ], BF16, es)
        v_tm = C.sb("v_tm", [128, NCH, 4, 65], BF16, es)
        gates = C.sb("gates", [128, NCH, 24], F32, es)
        gl = C.sb("gl", [128, 8], F32, es)
        kcnT = C.sb("kcnT", [128, 128], BF16, es)
        vc_tm = C.sb("vc_tm", [128, 2, 65], BF16, es)
        QS, KS, QN, KSL, KW = ([Buf() for _ in range(NT)] for _ in range(5))
        VT = [Buf() for _ in range(NCH)]
        GT, GL, KCN, VCT = Buf(), Buf(), Buf(), Buf()
        S.op("pool", lambda e: e.memset(v_tm[:], 1.0), writes=VT)
        S.op("pool", lambda e: e.memset(vc_tm[:], 1.0), writes=[VCT])
        gsrc = [C.d["swa_q_norm"].ap()[0], C.d["swa_k_norm"].ap()[0], C.d["nsa_q_norm"].ap()[0],
                C.d["nsa_k_norm"].ap()[0, 0], C.d["nsa_k_norm"].ap()[0, 1], C.d["nsa_k_norm"].ap()[0, 2]]
        for k, v in enumerate(gsrc):
            for hh in range(2):
                S.op("sp", lambda e, k=k, v=v, hh=hh: e.dma_start(
                    out=gl[hh * 64:(hh + 1) * 64, k:k + 1], in_=v.rearrange("(d o) -> d o", o=1)),
                    writes=[GL], dma=True)

        pts = [(C.sb("pt%d" % i, [128, 512], BF16, es), Buf()) for i in range(3)]
        small = C.sb("small", [128, 2, 64], F32, es)
        esS = ExitStack()
        qT_swa = C.sb("qT_swa", [128, NCH, 4, 128], BF16, esS)
        kT_swa = C.sb("kT_swa", [128, SEQ], BF16, esS)
        v_swa = C.sb("v_swa", [128, NCH, 2, 65], BF16, esS)
        hk_swa = C.sb("hk_swa", [128, 2, 8, 128], BF16, esS)
        sinkexp = C.sb("sinkexp", [128, 8], F32, esS)
        SM = [Buf(), Buf()]
        VS = [Buf() for _ in range(NCH)]
        HK, CST = Buf(), Buf()
        S.op("pool", lambda e: e.memset(v_swa[:], 1.0), writes=VS)
        for o in range(2):
            load_hankel(C, hk_swa[:, o], HK, "swa", o, 0, 8)
        S.op("sp", lambda e: e.dma_start(out=sinkexp[:], in_=C.d["swa_sinks"].ap()[0].partition_broadcast(128)),
             writes=[CST], dma=True)
        S.op("act", lambda e: e.activation(out=sinkexp[:], in_=sinkexp[:], func=AF.Exp), reads=[CST], writes=[CST])
        kcT = C.sb("kcT", [128, SEQ], BF16, esS)
        vcT = C.sb("vcT", [128, SEQ], BF16, esS)
        KC, VC = [Buf() for _ in range(NT)], [Buf() for _ in range(NT)]
        with ExitStack() as es2:
            hT = C.sb("hT", [128, 8, SEQ], BF16, es2)
            H = [Buf() for _ in range(NT)]
            rmsnorm_hT(C, es2, C.gains[:, 1, 1, :], hT, H)
            wbs = Rot([(C.sb("wb%d" % i, [128, 8, 544], BF16, es2), Buf()) for i in range(1)])
            scr = dict(sq=Rot([(C.sb("fsq%d" % i, [128, 512], BF16, es2), Buf()) for i in range(2)]),
                       rs=Rot([(C.sb("frs%d" % i, [128, 512], F32, es2), Buf()) for i in range(2)]),
                       sbank=Rot([bk[3], bk[4]]))
            zrot = Rot([bk[0], bk[1], bk[2]])
            trot = Rot([bk[5], bk[6]])

            def wload(wb, WB, pieces):
                for (a, c0, c1) in pieces:
                    S.op("pool", lambda e, a=a, c0=c0, c1=c1: e.dma_start(
                        out=wb[:, :, a:a + (c1 - c0)], in_=Win[:, c0:c1].rearrange("(c p) f -> p c f", p=128)),
                        writes=[WB], dma=True)

            def fm(wb, WB, lhs_fn, post):
                for t in range(NT):
                    tl = slice(t * 512, (t + 1) * 512)
                    zb, ZB = zrot.next()
                    for kc in range(8):
                        S.op("pe", lambda e, kc=kc, zb=zb, tl=tl: e.matmul(
                            zb[:], lhsT=lhs_fn(kc), rhs=hT[:, kc, tl], start=(kc == 0), stop=(kc == 7)),
                            reads=[WB, H[t]], writes=[ZB], sig=(kc == 7))
                    post(t, tl, zb, ZB)

            def post_norm(dst_fn, DST, gcol, sf, view=None):
                def f(t, tl, zb, ZB):
                    fm_norm(C, zb, ZB, dst_fn(tl), DST[t], gl[:, gcol:gcol + 1], sf, 64.0, scr, view=view)
                return f

            qview = lambda a: a.rearrange("p (c q) -> p c q", c=4)

            def post_copy(dst_fn, DST):
                def f(t, tl, zb, ZB):
                    S.op("act", lambda e: e.copy(out=dst_fn(tl), in_=zb[:]), reads=[ZB], writes=[DST[t]])
                return f

            def tm(wb, WB, jobs):
                for i in range(NCH):
                    tb, TB = trot.next()
                    c0 = 0
                    for ji, (rhs_fn, n, post) in enumerate(jobs):
                        for kc in range(8):
                            S.op("pe", lambda e, kc=kc, c0=c0, n=n, rhs_fn=rhs_fn, i=i, tb=tb: e.matmul(
                                tb[:, c0:c0 + n], lhsT=hT[:, kc, i * 128:(i + 1) * 128], rhs=rhs_fn(kc),
                                start=(kc == 0), stop=(kc == 7)), reads=[WB, H[i // 4]], writes=[TB],
                                sig=(kc == 7 and ji == len(jobs) - 1))
                        c0 += n
                    c0 = 0
                    for (rhs_fn, n, post) in jobs:
                        post(i, tb[:, c0:c0 + n], TB)
                        c0 += n

            def post_v(vt, VTB, slot0):
                def f(i, reg, TB):
                    S.op("act", lambda e: e.copy(out=vt[:, i, slot0:slot0 + 2, 0:64],
                                                 in_=reg.rearrange("p (g d) -> p g d", g=2)),
                         reads=[TB], writes=[VTB[i]])
                return f

            def post_gate(i, reg, TB):
                S.op("act", lambda e: e.activation(out=gates[:, i, :], in_=reg, func=AF.Sigmoid),
                     reads=[TB], writes=[GT])

            wb, WB = wbs.next()
            wload(wb, WB, [(r * 128 + g * 64, (4 * g + r) * 64, (4 * g + r) * 64 + 64) for r in range(4) for g in range(2)])
            for r in range(4):
                fm(wb, WB, lambda kc, r=r, wb=wb: wb[:, kc, r * 128:(r + 1) * 128],
                   post_norm(lambda tl, r=r: qT_swa[:, tl.start // 128:tl.stop // 128, r, :], QS, 0, sc, view=qview))
            wbB, WBB = wbs.next()
            wload(wbB, WBB, [(0, 512, 768), (256, 1280, 1536)])
            fm(wbB, WBB, lambda kc: wbB[:, kc, 0:128], post_norm(lambda tl: kT_swa[:, tl], KS, 1, 1.0))
            fm(wbB, WBB, lambda kc: wbB[:, kc, 256:384], post_copy(lambda tl: kcT[:, tl], KC))
            fm(wbB, WBB, lambda kc: wbB[:, kc, 384:512], post_copy(lambda tl: vcT[:, tl], VC))
            tm(wbB, WBB, [(lambda kc: wbB[:, kc, 128:256], 128, post_v(v_swa, VS, 0))])
            wbC, WBC = wbs.next()
            wload(wbC, WBC, [(r * 128 + g * 64, 768 + (4 * g + r) * 64, 768 + (4 * g + r) * 64 + 64) for r in range(4) for g in range(2)])
            for r in range(4):
                fm(wbC, WBC, lambda kc, r=r: wbC[:, kc, r * 128:(r + 1) * 128],
                   post_norm(lambda tl, r=r: qT_nsa[:, tl.start // 128:tl.stop // 128, r, :], QN, 2, sc, view=qview))
            wbD, WBD = wbs.next()
            wload(wbD, WBD, [(0, 1536, 2072)])
            fm(wbD, WBD, lambda kc: wbD[:, kc, 0:128], post_norm(lambda tl: ksT[:, tl], KSL, 4, 1.0))
            fm(wbD, WBD, lambda kc: wbD[:, kc, 256:384], post_norm(lambda tl: kwT[:, tl], KW, 5, 1.0))
            tm(wbD, WBD, [(lambda kc: wbD[:, kc, 128:256], 128, post_v(v_tm, VT, 0)),
                          (lambda kc: wbD[:, kc, 384:512], 128, post_v(v_tm, VT, 2)),
                          (lambda kc: wbD[:, kc, 512:536], 24, post_gate)])

        S.barrier()
        esC = ExitStack()
        zrot = Rot([bk[0], bk[1], bk[2]])
        scr = dict(sq=Rot([(C.sb("fsq%d" % i, [128, 512], BF16, esC), Buf()) for i in range(2)]),
                   rs=Rot([(C.sb("frs%d" % i, [128, 512], F32, esC), Buf()) for i in range(2)]),
                   sbank=Rot([bk[3], bk[4]]))
        w1sb = C.sb("w1sb", [128, 2, 32, 128], BF16, esC)
        posT = C.sb("posT", [128, 2, 32], BF16, esC)
        w2kp = C.sb("w2kp", [128, 2, 128], BF16, esC)
        w2v = C.sb("w2v", [128, 64], BF16, esC)
        h1T = C.sb("h1T", [128, 2, 2, 128], BF16, esC)
        b1 = C.sb("b1", [128, 4], F32, esC)
        W1, H1, B1 = Buf(), Buf(), Buf()
        S.op("pool", lambda e: e.memset(w2kp[:], 0.0), writes=[W1])
        for kv in range(2):
            for hh in range(2):
                S.op("pool", lambda e, kv=kv, hh=hh: e.dma_start(
                    out=w1sb[hh * 64:(hh + 1) * 64, kv, :, :],
                    in_=C.d["nsa_cmp_w1"].ap()[0, kv].rearrange("(l d) h -> d l h", d=64)), writes=[W1], dma=True)
                S.op("pool", lambda e, kv=kv, hh=hh: e.dma_start(
                    out=posT[hh * 64:(hh + 1) * 64, kv, :],
                    in_=C.d["nsa_cmp_pos"].ap()[0, kv].rearrange("l d -> d l")), writes=[W1], dma=True)
        S.op("pool", lambda e: e.dma_start(out=w2kp[:, 0, 0:64], in_=C.d["nsa_cmp_w2"].ap()[0, 0]), writes=[W1], dma=True)
        S.op("pool", lambda e: e.dma_start(out=w2kp[:, 1, 64:128], in_=C.d["nsa_cmp_w2"].ap()[0, 0]), writes=[W1], dma=True)
        S.op("pool", lambda e: e.dma_start(out=w2v[:], in_=C.d["nsa_cmp_w2"].ap()[0, 1]), writes=[W1], dma=True)
        for kv in range(2):
            src, SRC = (kcT, KC) if kv == 0 else (vcT, VC)
            for g in range(2):
                gs = slice(g * 64, (g + 1) * 64)
                zb, ZB = zrot.next()
                for l in range(32):
                    S.op("pe", lambda e, l=l, gs=gs, kv=kv, src=src, zb=zb: e.matmul(
                        zb[:, 0:127], lhsT=w1sb[gs, kv, l, :], rhs=src[gs, l:l + 16 * 126 + 1:16],
                        start=(l == 0), stop=(l == 31)), reads=[W1] + SRC, writes=[ZB], sig=False)
                for l in range(32):
                    S.op("pe", lambda e, l=l, gs=gs, kv=kv, zb=zb: e.matmul(
                        zb[:, 128:129], lhsT=w1sb[gs, kv, l, :], rhs=posT[gs, kv, l:l + 1],
                        start=(l == 0), stop=(l == 31)), reads=[W1], writes=[ZB], sig=(l == 31))
                j = kv * 2 + g
                S.op("dve", lambda e, j=j, zb=zb: e.tensor_copy(out=b1[:, j:j + 1], in_=zb[:, 128:129]),
                     reads=[ZB], writes=[B1])
                S.op("act", lambda e, j=j, zb=zb, kv=kv, g=g: e.activation(
                    out=h1T[:, kv, g, 0:127], in_=zb[:, 0:127], func=AF.Gelu_apprx_tanh, bias=b1[:, j:j + 1], scale=1.0),
                    reads=[ZB, B1], writes=[H1])
        zb, ZB = zrot.next()
        for g in range(2):
            S.op("pe", lambda e, g=g, zb=zb: e.matmul(zb[:, 0:127], lhsT=w2kp[:, g, :], rhs=h1T[:, 0, g, 0:127],
                                                     start=(g == 0), stop=(g == 1)),
                 reads=[W1, H1], writes=[ZB], sig=(g == 1))
        fm_norm(C, zb, ZB, kcnT[:, 0:127], KCN, gl[:, 3:4], 1.0, 64.0, scr, n=127)
        for g in range(2):
            zb, ZB = zrot.next()
            S.op("pe", lambda e, g=g, zb=zb: e.matmul(zb[0:127, 0:64], lhsT=h1T[:, 1, g, 0:127], rhs=w2v[:],
                                                     start=True, stop=True), reads=[W1, H1], writes=[ZB])
            S.op("act", lambda e, g=g, zb=zb: e.copy(out=vc_tm[0:127, g, 0:64], in_=zb[0:127, 0:64]),
                 reads=[ZB], writes=[VCT])

        esC.close()
        S.barrier()

        pipe = AttnPipe(C, [bk[0], bk[1]], pts)
        ob = [bk[2], bk[3], bk[4]]

        def obv(b):
            return ob[b][0][:, 0:260].rearrange("p (r c) -> p r c", c=65)

        par = 0
        for i in range(NCH):
            qs = slice(i * 128, (i + 1) * 128)
            for g in range(2):
                gs = slice(g * 64, (g + 1) * 64)
                cs = [c for c in (i - 1, i) if c >= 0]
                for c in cs:
                    o = i - c
                    fills = [dict(lhsT=kT_swa[gs, c * 128:(c + 1) * 128], rhs=qT_swa[gs, i, :, :].rearrange("p r q -> p (r q)"), start=True, stop=False,
                                  reads=[KS[c // 4], QS[i // 4]]),
                             dict(lhsT=C.anti_bf[:], rhs=hk_swa[:, o, g * 4:(g + 1) * 4, :].rearrange("p h q -> p (h q)"), start=False, stop=True,
                                  reads=[C.CONST, HK])]
                    pvs = [dict(out=obv(g)[:, r, :], OUT=ob[g][1], c0=r * 128, v=v_swa[:, c, g, :],
                                start=(c == cs[0] and r == 0), stop=(c == i and r == 3), reads=[VS[c]]) for r in range(4)]
                    pipe.unit(fills, pvs)
            pipe.flush()
            for g in range(2):
                par ^= 1
                sm = small[:, par, :]
                S.op("dve", lambda e, g=g, sm=sm: e.tensor_tensor(out=sm[:, 0:4], in0=obv(g)[:, :, 64],
                                                                  in1=sinkexp[:, g * 4:(g + 1) * 4], op=ALU.add),
                     reads=[ob[g][1], CST], writes=[SM[par]])
                S.op("dve", lambda e, sm=sm: e.reciprocal(out=sm[:, 0:4], in_=sm[:, 0:4]), reads=[SM[par]], writes=[SM[par]])
                for r in range(4):
                    hcol = (g * 4 + r) * 64
                    S.op("dve", lambda e, g=g, r=r, sm=sm, hcol=hcol, i=i: e.tensor_scalar(
                        out=swa_out[:, i, hcol:hcol + 64], in0=obv(g)[:, r, 0:64], scalar1=sm[:, r:r + 1], scalar2=None,
                        op0=ALU.mult), reads=[ob[g][1], SM[par]], writes=[SWO[i]])

        C.dbg_dump("swa_out", swa_out[:], [128, NCH, 512], BF16, SWO)
        esS.close()
        S.barrier()

        hk_slc = C.sb("hk_slc", [128, 14, 8, 128], BF16, es)
        hk_win4 = C.sb("hk_win4", [128, 8, 128], BF16, es)
        for o in range(14):
            load_hankel(C, hk_slc[:, o], HK, "slc", o, 8, 8)
        load_hankel(C, hk_win4[:], HK, "win", 4, 8, 8)
        cmpmask = C.sb("cmpmask", [128, SEQ], BF16, es)
        ovl1 = C.sb("ovl1", [128, 33], F32, es)
        causal01 = C.sb("causal01", [128, 16, 32], F32, es)
        sadd = C.sb("sadd", [128, 16, 32], F32, es)
        esel = C.sb("esel", [32, 16, 128], BF16, es)
        wout = C.sb("wout", [128, 8, D_MODEL], BF16, es)
        WO = Buf()
        for nm, t in (("cmpmask", cmpmask), ("ovl1", ovl1), ("causal01", causal01), ("sadd", sadd), ("esel", esel)):
            S.op("sp", lambda e, nm=nm, t=t: e.dma_start(out=t[:], in_=C.d[nm].ap()), writes=[CST], dma=True)
        S.op("pool", lambda e: e.dma_start(out=wout[:], in_=Wout.rearrange("(c p) d -> p c d", p=128)),
             writes=[WO], dma=True)
        efs = Rot([(C.sb("ef%d" % i, [128, 512], F32, es), Buf()) for i in range(2)])
        mixed_tm = Rot([(C.sb("mtm%d" % i, [128, 512], BF16, es), Buf()) for i in range(2)])
        mixedT = Rot([(C.sb("mT%d" % i, [128, 8, 512], BF16, es), Buf()) for i in range(1)])
        acc = C.sb("acc", [128, 2, 4, 64], F32, es)
        imp = C.sb("imp", [128, 2, 3, 32], F32, es)
        m8 = C.sb("m8", [128, 2, 16], F32, es)
        selm1 = C.sb("selm1", [128, 2, 32], BF16, es)
        selT = C.sb("selT", [32, 2, 4, 128], BF16, es)
        SELT = [Buf(), Buf()]
        orot7 = Rot([bk[7]])

        mstate = dict(mT=None, MTB=None)

        def epilogue(i, mtm, MTM):
            if i % 4 == 0:
                mstate["mT"], mstate["MTB"] = mixedT.next()
            mT, MTB = mstate["mT"], mstate["MTB"]
            C.dbg_dump("nsa_out", mtm[:], [NCH, 128, 512], BF16, [MTM], dst_fn=lambda a, i=i: a[i])
            tb7, TB7 = bk[7]
            tb7v = tb7[:].bitcast(BF16)
            for fc in range(8):
                src_ = swa_out[:, i, fc * 128:(fc + 1) * 128] if fc < 4 else mtm[:, (fc - 4) * 128:(fc - 3) * 128]
                S.op("pe", lambda e, fc=fc, src_=src_, tb7v=tb7v: e.transpose(
                    out=tb7v[:, fc * 128:(fc + 1) * 128], in_=src_, identity=C.ident_bf[:]),
                    reads=[MTM, SWO[i], C.CONST], writes=[TB7], sig=(fc == 7))
            S.op("act", lambda e, mT=mT, tb7v=tb7v, i=i: e.copy(
                out=mT[:, :, (i % 4) * 128:(i % 4 + 1) * 128], in_=tb7v.rearrange("p (f t) -> p f t", f=8)),
                reads=[TB7], writes=[MTB])
            if i % 4 == 3:
                out_proj_tile(C, i // 4, mT, MTB, wout, WO, 8, orot7)

        for i in range(NCH):
            qs = slice(i * 128, (i + 1) * 128)
            prev_m = (mtm, MTM) if i >= 1 else None
            mtm, MTM = mixed_tm.next()
            for g in range(2):
                gs = slice(g * 64, (g + 1) * 64)
                par ^= 1
                sm = small[:, par, :]
                ef, EF = efs.next()
                ib, IB = bk[5]
                fills = [dict(m=127, lhsT=kcnT[gs, 0:127], rhs=qT_nsa[gs, i, :, :].rearrange("p r q -> p (r q)"), start=True, stop=False,
                              reads=[KCN, QN[i // 4]])]
                for r in range(4):
                    fills.append(dict(m=127, c0=r * 128, n=128, lhsT=C.ident_bf[0:127, 0:127], rhs=cmpmask[0:127, qs],
                                      start=False, stop=(r == 3), reads=[C.CONST, CST]))
                pvs = [dict(out=obv(0)[:, r, :], OUT=ob[0][1], c0=r * 128, v=vc_tm[0:127, g, :], start=(r == 0), stop=(r == 3),
                            reads=[VCT]) for r in range(4)]
                pvs += [dict(out=ib[:, r * 33:(r + 1) * 33], OUT=IB, c0=r * 128, v=ovl1[0:127, :], start=(r == 0), stop=(r == 3),
                             reads=[CST, EF], lhsT=ef[0:127, r * 128:(r + 1) * 128]) for r in range(4)]
                pipe.unit(fills, pvs, krows=127, exp_f32=(ef, EF))
                cs = [c for c in range(i - 4, i + 1) if c >= 0]
                for c in cs:
                    o = i - c
                    hk = (hk_slc[:, o, g * 4:(g + 1) * 4, :] if o < 4 else hk_win4[:, g * 4:(g + 1) * 4, :]).rearrange("p h q -> p (h q)")
                    fills = [dict(lhsT=kwT[gs, c * 128:(c + 1) * 128], rhs=qT_nsa[gs, i, :, :].rearrange("p r q -> p (r q)"), start=True, stop=False,
                                  reads=[KW[c // 4], QN[i // 4]]),
                             dict(lhsT=C.anti_bf[:], rhs=hk, start=False, stop=True, reads=[C.CONST, HK])]
                    pvs = [dict(out=obv(2)[:, r, :], OUT=ob[2][1], c0=r * 128, v=v_tm[:, c, 2 + g, :],
                                start=(c == cs[0] and r == 0), stop=(c == i and r == 3), reads=[VT[c]]) for r in range(4)]
                    pipe.unit(fills, pvs)
                ibv = ib[:, 0:132].rearrange("p (r c) -> p r c", c=33)
                S.op("dve", lambda e, sm=sm, ibv=ibv: e.tensor_scalar(out=sm[:, 0:4], in0=ibv[:, :, 32], scalar1=1e-30,
                                                                      scalar2=None, op0=ALU.max),
                     reads=[IB], writes=[SM[par]])
                S.op("dve", lambda e, sm=sm: e.reciprocal(out=sm[:, 0:4], in_=sm[:, 0:4]), reads=[SM[par]], writes=[SM[par]])
                im = imp[:, par, 0, :]
                sc1 = imp[:, par, 1, :]
                sc2 = imp[:, par, 2, :]
                S.op("dve", lambda e, sm=sm, im=im, ibv=ibv: e.tensor_scalar(
                    out=im, in0=ibv[:, 0, 0:32], scalar1=sm[:, 0:1], scalar2=None, op0=ALU.mult),
                    reads=[IB, SM[par]], writes=[SM[par]])
                for r in range(1, 4):
                    S.op("dve", lambda e, sm=sm, im=im, ibv=ibv, r=r: e.scalar_tensor_tensor(
                        out=im, in0=ibv[:, r, 0:32], scalar=sm[:, r:r + 1], in1=im, op0=ALU.mult, op1=ALU.add),
                        reads=[IB, SM[par]], writes=[SM[par]])
                S.op("dve", lambda e, im=im, sc1=sc1, i=i: e.tensor_tensor(out=sc1, in0=im, in1=causal01[:, i, :], op=ALU.mult),
                     reads=[SM[par], CST], writes=[SM[par]])
                S.op("dve", lambda e, sc1=sc1, i=i: e.tensor_tensor(out=sc1, in0=sc1, in1=sadd[:, i, :], op=ALU.add),
                     reads=[SM[par], CST], writes=[SM[par]])
                mm = m8[:, par, :]
                S.op("dve", lambda e, mm=mm, sc1=sc1: e.max(out=mm[:, 0:8], in_=sc1), reads=[SM[par]], writes=[SM[par]])
                S.op("dve", lambda e, mm=mm, sc1=sc1, sc2=sc2: e.match_replace(
                    out=sc2, in_to_replace=mm[:, 0:8], in_values=sc1, imm_value=-1e30), reads=[SM[par]], writes=[SM[par]])
                S.op("dve", lambda e, mm=mm, sc2=sc2: e.max(out=mm[:, 8:16], in_=sc2), reads=[SM[par]], writes=[SM[par]])
                sl = selm1[:, par, :]
                S.op("dve", lambda e, mm=mm, sc1=sc1, sl=sl: e.tensor_scalar(
                    out=sl, in0=sc1, scalar1=mm[:, 15:16], scalar2=-1.0, op0=ALU.is_ge, op1=ALU.add),
                    reads=[SM[par]], writes=[SM[par]])
                if g == 0 and prev_m is not None:
                    epilogue(i - 1, prev_m[0], prev_m[1])
                tb6, TB6 = bk[6]
                tb6v = tb6[:].bitcast(BF16)
                S.op("pe", lambda e, sl=sl, tb6v=tb6v: e.transpose(out=tb6v[0:32, 0:128], in_=sl, identity=C.ident_bf[:]),
                     reads=[SM[par], C.CONST], writes=[TB6])
                for r in range(4):
                    S.op("act", lambda e, r=r, tb6v=tb6v, par=par: e.copy(out=selT[0:32, par, r, :], in_=tb6v[0:32, 0:128]),
                         reads=[TB6], writes=[SELT[par]])
                for c in range(i + 1):
                    o = i - c
                    fills = [dict(lhsT=ksT[gs, c * 128:(c + 1) * 128], rhs=qT_nsa[gs, i, :, :].rearrange("p r q -> p (r q)"), start=True, stop=False,
                                  reads=[KSL[c // 4], QN[i // 4]]),
                             dict(lhsT=C.anti_bf[:], rhs=hk_slc[:, min(o, 13), g * 4:(g + 1) * 4, :].rearrange("p h q -> p (h q)"), start=False, stop=False,
                                  reads=[C.CONST, HK]),
                             dict(lhsT=esel[0:32, c, :], rhs=selT[0:32, par, :, :].rearrange("p r q -> p (r q)"), start=False, stop=True,
                                  reads=[CST, SELT[par]])]
                    pvs = [dict(out=obv(1)[:, r, :], OUT=ob[1][1], c0=r * 128, v=v_tm[:, c, g, :],
                                start=(c == 0 and r == 0), stop=(c == i and r == 3), reads=[VT[c]]) for r in range(4)]
                    pipe.unit(fills, pvs)
                pipe.flush()
                gv = gates[:, i, g * 12:(g + 1) * 12].rearrange("p (r b) -> p r b", b=3)
                for b in range(3):
                    sb4 = sm[:, 8 + b * 4:12 + b * 4]
                    S.op("dve", lambda e, b=b, sb4=sb4: e.tensor_scalar(out=sb4, in0=obv(b)[:, :, 64], scalar1=1e-30,
                                                                        scalar2=None, op0=ALU.max),
                         reads=[ob[b][1]], writes=[SM[par]])
                    S.op("dve", lambda e, sb4=sb4: e.reciprocal(out=sb4, in_=sb4), reads=[SM[par]], writes=[SM[par]])
                    S.op("dve", lambda e, b=b, sb4=sb4, gv=gv: e.tensor_tensor(out=sb4, in0=sb4, in1=gv[:, :, b], op=ALU.mult),
                         reads=[SM[par], GT], writes=[SM[par]])
                for r in range(4):
                    ar = acc[:, par, r, :]
                    hcol = (g * 4 + r) * 64
                    S.op("dve", lambda e, r=r, ar=ar, sm=sm: e.tensor_scalar(
                        out=ar, in0=obv(0)[:, r, 0:64], scalar1=sm[:, 8 + r:9 + r], scalar2=None, op0=ALU.mult),
                        reads=[ob[0][1], SM[par]], writes=[SM[par]])
                    S.op("dve", lambda e, r=r, ar=ar, sm=sm: e.scalar_tensor_tensor(
                        out=ar, in0=obv(1)[:, r, 0:64], scalar=sm[:, 12 + r:13 + r], in1=ar, op0=ALU.mult, op1=ALU.add),
                        reads=[ob[1][1], SM[par]], writes=[SM[par]])
                    S.op("dve", lambda e, r=r, ar=ar, sm=sm, hcol=hcol, mtm=mtm: e.scalar_tensor_tensor(
                        out=mtm[:, hcol:hcol + 64], in0=obv(2)[:, r, 0:64], scalar=sm[:, 16 + r:17 + r], in1=ar,
                        op0=ALU.mult, op1=ALU.add), reads=[ob[2][1], SM[par]], writes=[MTM])
        epilogue(NCH - 1, mtm, MTM)
    S.barrier()


_CACHE = {}


def kernel(**inputs):
    x = np.ascontiguousarray(np.asarray(inputs["x"], dtype=np.float32))
    if "prog" not in _CACHE:
        _CACHE["prog"] = build_program()
    nc, consts = _CACHE["prog"]
    shared = {name: np.ascontiguousarray(np.asarray(inputs[name], dtype=np.float32)) for name, _ in W_SPECS}
    shared.update(consts)
    in_maps = []
    for c in range(NCORES):
        m = dict(shared)
        m["x"] = x[c * SEQ_PER_CORE:(c + 1) * SEQ_PER_CORE]
        in_maps.append(m)
    res = run_bass_kernel_spmd(nc, in_maps, core_ids=list(range(NCORES)))
    return np.concatenate([np.asarray(r["y"], dtype=np.float32) for r in res.results], axis=0)
```

```python
import math
import numpy as np
import ml_dtypes
import concourse.bass as bass
import concourse.mybir as mybir
from concourse.bass_utils import run_bass_kernel_spmd
from contextlib import ExitStack

F32 = mybir.dt.float32
BF16 = mybir.dt.bfloat16
ALU = mybir.AluOpType
AF = mybir.ActivationFunctionType
AX = mybir.AxisListType

D_MODEL = 1024
SEQ = 2048
D_FF = 2816
EPS = 1e-6
NCORES = 8
SEQ_PER_CORE = 2
NT = SEQ // 512
NCH = SEQ // 128


class Buf:
    __slots__ = ("name", "w", "r")

    def __init__(self, name=""):
        self.name = name
        self.w = None
        self.r = {}


class Sched:
    ENG = ("pe", "dve", "act", "pool", "sp")

    def __init__(self, nc, es, n_dma_sems=24):
        self.nc = nc
        self.sems = {e: es.enter_context(nc.semaphore("s_" + e)) for e in self.ENG}
        self.dsems = [es.enter_context(nc.semaphore("d%d" % i)) for i in range(n_dma_sems)]
        self.duse = [0] * n_dma_sems
        self.dnext = 0
        self.dnext_sw = 0
        self.cnt = {e: 0 for e in self.ENG}
        self.waited = {e: {} for e in self.ENG}
        self.prog = {e: [] for e in self.ENG}
        self.nops = 0

    def _wait(self, e, key, val):
        if self.waited[e].get(key, 0) >= val:
            return
        self.waited[e][key] = val
        sem = self.sems[key[1]] if key[0] == "e" else self.dsems[key[1]]
        self.prog[e].append(lambda eng, sem=sem, val=val: eng.wait_ge(sem, val))

    def op(self, e, fn, reads=(), writes=(), dma=False, sig=True):
        deps = {}

        def add(ev):
            if ev is None:
                return
            k, v = ev
            if deps.get(k, 0) < v:
                deps[k] = v

        for b in reads:
            add(b.w)
        for b in writes:
            add(b.w)
            for k, v in b.r.items():
                add((k, v))
        for k, v in deps.items():
            if e == "pe" and k == ("e", "pe"):
                continue
            self._wait(e, k, v)
        self.nops += 1
        if dma:
            half = len(self.dsems) // 2
            if e == "pool":
                i = half + self.dnext_sw
                self.dnext_sw = (self.dnext_sw + 1) % (len(self.dsems) - half)
            else:
                i = self.dnext
                self.dnext = (i + 1) % half
            if self.duse[i] > 0:
                self._wait(e, ("d", i), 16 * self.duse[i])
            self.duse[i] += 1
            ev = (("d", i), 16 * self.duse[i])
            sem = self.dsems[i]
            self.prog[e].append(lambda eng, fn=fn, sem=sem: fn(eng).then_inc(sem, 16))
        elif sig:
            self.cnt[e] += 1
            ev = (("e", e), self.cnt[e])
            sem = self.sems[e]
            self.prog[e].append(lambda eng, fn=fn, sem=sem: fn(eng).then_inc(sem, 1))
        else:
            ev = (("e", e), self.cnt[e] + 1)
            self.prog[e].append(lambda eng, fn=fn: fn(eng))
        for b in reads:
            if b.r.get(ev[0], 0) < ev[1]:
                b.r[ev[0]] = ev[1]
        for b in writes:
            b.w = ev
            b.r = {}
        return ev

    def barrier(self):
        for e in self.ENG:
            for e2 in self.ENG:
                if e2 != e and self.cnt[e2] > 0:
                    self._wait(e, ("e", e2), self.cnt[e2])
            for i, u in enumerate(self.duse):
                if u:
                    self._wait(e, ("d", i), 16 * u)

    def finish(self):
        for i, u in enumerate(self.duse):
            if u:
                self._wait("sp", ("d", i), 16 * u)
        for e in self.ENG:
            if e != "sp" and self.cnt[e]:
                self._wait("sp", ("e", e), self.cnt[e])

    def emit(self):
        self.finish()
        nc = self.nc
        prog = self.prog
        with nc.Block() as block:
            @block.tensor
            def _(eng):
                for f in prog["pe"]:
                    f(eng)

            @block.vector
            def _(eng):
                for f in prog["dve"]:
                    f(eng)

            @block.scalar
            def _(eng):
                for f in prog["act"]:
                    f(eng)

            @block.gpsimd
            def _(eng):
                for f in prog["pool"]:
                    f(eng)

            @block.sync
            def _(eng):
                for f in prog["sp"]:
                    f(eng)


class Rot:
    def __init__(self, items):
        self.items = items
        self.i = 0

    def next(self):
        it = self.items[self.i % len(self.items)]
        self.i += 1
        return it


W_SPECS = [
    ("rel_bias", (32, 16)),
    ("ffn1_norm", (2, 1024)), ("ffn1_w_gate", (2, 1024, 2816)), ("ffn1_w_up", (2, 1024, 2816)),
    ("ffn1_w_down", (2, 2816, 1024)), ("mix_norm", (2, 1024)), ("ffn2_norm", (2, 1024)),
    ("ffn2_w_gate", (2, 1024, 2816)), ("ffn2_w_up", (2, 1024, 2816)), ("ffn2_w_down", (2, 2816, 1024)),
    ("ab_w_in", (1, 1024, 2720)), ("mla_q_a_norm", (1, 256)), ("mla_w_q_b", (1, 256, 768)),
    ("mla_kv_a_norm", (1, 128)), ("mla_w_kv_b", (1, 128, 1024)), ("mla_q_norm", (1, 96)),
    ("mla_k_norm", (1, 96)), ("dil_q_norm", (1, 64)), ("dil_k_norm", (1, 64)),
    ("ab_w_out", (1, 768, 1024)), ("cd_w_in", (1, 1024, 2072)), ("swa_q_norm", (1, 64)),
    ("swa_k_norm", (1, 64)), ("swa_sinks", (1, 8)), ("nsa_q_norm", (1, 64)), ("nsa_k_norm", (1, 3, 64)),
    ("nsa_cmp_pos", (1, 2, 32, 64)), ("nsa_cmp_w1", (1, 2, 2048, 128)), ("nsa_cmp_w2", (1, 2, 128, 64)),
    ("cd_w_out", (1, 1024, 1024)),
]


class Ctx:
    pass


NEGM = -30000.0
KINDS = {"dil0": (129, 1, 383), "dil1": (129, 4, 383), "dil2": (129, 16, 383), "swa": (128, 1, 383),
         "win": (512, 1, 767), "slc": (10 ** 9, 1, 2175)}


def t5_bucket_np(n):
    n = np.maximum(n, 0)
    nf = np.maximum(n, 1).astype(np.float32)
    large = 16 + (np.log(nf / np.float32(16)) / np.float32(math.log(2048 / 16)) * np.float32(16)).astype(np.int32)
    large = np.minimum(large, 31)
    return np.where(n < 16, n, large)


def make_consts():
    bf = ml_dtypes.bfloat16
    c = {}
    c["ident_f32"] = np.eye(128, dtype=np.float32)
    c["ident_bf"] = np.eye(128, dtype=np.float32).astype(bf)
    c["ones_bf"] = np.ones((128, 128), dtype=np.float32).astype(bf)
    c["anti_bf"] = np.eye(128, dtype=np.float32)[::-1].copy().astype(bf)
    bd = np.zeros((128, 128), np.float32)
    bd[:64, :64] = 1
    bd[64:, 64:] = 1
    c["bd64_bf"] = bd.astype(bf)
    for kind, (win, ds, L) in KINDS.items():
        oh = np.zeros((33, L), np.float32)
        n = np.arange(L) - 127
        valid = (n >= 0) & (n < win)
        b = t5_bucket_np(n * ds)
        oh[b[valid], np.arange(L)[valid]] = 1
        oh[32, ~valid] = 1
        c["oh_" + kind] = oh
    cm = np.full((128, SEQ), NEGM, np.float32)
    cc = np.arange(127)[:, None]
    tt = np.arange(SEQ)[None, :]
    cm[:127][(16 * cc + 31) <= tt] = 0
    c["cmpmask"] = cm.astype(bf)
    ov = np.zeros((128, 33), np.float32)
    ci = np.arange(127)[:, None] * 16
    sj = np.arange(32)[None, :] * 64
    ov[:127, :32] = ((ci < sj + 64) & (ci + 32 > sj)).astype(np.float32)
    ov[:127, 32] = 1
    c["ovl1"] = ov
    t = np.arange(SEQ)
    tb = t // 64
    jj = np.arange(32)
    causal = (jj[None, :] <= tb[:, None])
    forced = (jj[None, :] == 0) | (jj[None, :] == tb[:, None]) | (jj[None, :] == tb[:, None] - 1)
    c["causal01"] = causal.astype(np.float32).reshape(16, 128, 32).transpose(1, 0, 2).copy()
    sadd = np.where(causal, np.where(forced, 1e6, 0.0), -1e6).astype(np.float32)
    c["sadd"] = sadd.reshape(16, 128, 32).transpose(1, 0, 2).copy()
    es_ = np.zeros((32, 16, 128), np.float32)
    for cch in range(16):
        for p in range(128):
            es_[2 * cch + p // 64, cch, p] = -NEGM
    c["esel"] = es_.astype(bf)
    inv = np.power(np.float32(10000.0), -(np.arange(16, dtype=np.float32) / np.float32(16))).astype(np.float32)
    ang = (np.arange(SEQ, dtype=np.float32)[:, None] * inv[None, :]).astype(np.float32)
    cs_, sn_ = np.cos(ang).astype(np.float32), np.sin(ang).astype(np.float32)
    c["rope_cos"] = np.concatenate([cs_.T, cs_.T], axis=0).copy()
    c["rope_sin"] = np.concatenate([-sn_.T, sn_.T], axis=0).copy()
    kl = np.arange(128)[:, None]
    ql = np.arange(128)[None, :]
    c["tri"] = np.where(kl > ql, NEGM, 0.0).astype(np.float32).astype(bf)
    return c


CONST_DT = {"ident_f32": F32, "ident_bf": BF16, "ones_bf": BF16, "anti_bf": BF16, "bd64_bf": BF16,
            "cmpmask": BF16, "ovl1": F32, "causal01": F32, "sadd": F32, "esel": BF16,
            "rope_cos": F32, "rope_sin": F32, "tri": BF16}
for _k in KINDS:
    CONST_DT["oh_" + _k] = F32


def build_program(n_seq=SEQ_PER_CORE, phases=("ffn1", "mix", "ffn2"), layers=(0, 1), debug=False):
    nc = bass.Bass("TRN2", target_bir_lowering=False)
    C = Ctx()
    C.nc = nc
    C.d = {}
    C.d["x"] = nc.dram_tensor("x", [n_seq, SEQ, D_MODEL], F32, kind="ExternalInput")
    for name, shp in W_SPECS:
        C.d[name] = nc.dram_tensor(name, list(shp), F32, kind="ExternalInput")
    consts = make_consts()
    for name, arr in consts.items():
        C.d[name] = nc.dram_tensor(name, list(arr.shape), CONST_DT[name], kind="ExternalInput")
    C.d["y"] = nc.dram_tensor("y", [n_seq, SEQ, D_MODEL], F32, kind="ExternalOutput")
    C.Udil = nc.dram_tensor("Udil", [3, SEQ, 260], F32, kind=("ExternalOutput" if debug else "Internal"))
    C.UdilB = Buf("Udil")
    C.debug = debug
    C.dbg = {}

    def dbg_dump(name, src_ap, shape, dt, reads, dst_fn=None):
        if not C.debug:
            return
        if name not in C.dbg:
            C.dbg[name] = nc.dram_tensor("dbg_" + name, list(shape), dt, kind="ExternalOutput")
        dst = C.dbg[name].ap() if dst_fn is None else dst_fn(C.dbg[name].ap())
        C.S.op("sp", lambda e: e.dma_start(out=dst, in_=src_ap), reads=reads, dma=True)

    C.dbg_dump = dbg_dump

    with ExitStack() as es:
        S = Sched(nc, es)
        C.S = S
        C.es = es

        C.uid = 0

        def sb(name, shape, dt, stack=es):
            C.uid += 1
            return stack.enter_context(nc.sbuf_tensor("s%d_%s" % (C.uid, name), shape, dt))

        C.sb = sb
        C.banks = []
        for i in range(8):
            t = es.enter_context(nc.psum_tensor("pb%d" % i, [128, 512], F32))
            C.banks.append((t, Buf("pb%d" % i)))
        C.xT = sb("xT", [128, 8, SEQ], F32)
        C.XT = [[Buf("xT%d_%d" % (dc, t)) for t in range(NT)] for dc in range(8)]
        C.ident_f32 = sb("ident_f32", [128, 128], F32)
        C.ident_bf = sb("ident_bf", [128, 128], BF16)
        C.ones_bf = sb("ones_bf", [128, 128], BF16)
        C.CONST = Buf("const")
        C.anti_bf = sb("anti_bf", [128, 128], BF16)
        C.bd64_bf = sb("bd64_bf", [128, 128], BF16)
        for name in ("ident_f32", "ident_bf", "ones_bf", "anti_bf", "bd64_bf"):
            t = getattr(C, name)
            S.op("sp", lambda e, t=t, name=name: e.dma_start(out=t[:], in_=C.d[name].ap()),
                 writes=[C.CONST], dma=True)
        C.gains = sb("gains", [128, 3, 2, 8], F32)
        with nc.allow_non_contiguous_dma(reason="tiny gain vectors"):
            for wi, name in enumerate(("ffn1_norm", "mix_norm", "ffn2_norm")):
                for l in range(2):
                    S.op("sp", lambda e, wi=wi, l=l, name=name: e.dma_start(
                        out=C.gains[:, wi, l, :],
                        in_=C.d[name].ap()[l, :].rearrange("(c p) -> p c", p=128)),
                        writes=[C.CONST], dma=True)

            phase_init(C)
            for s in range(n_seq):
                phase_load(C, s)
                for layer in layers:
                    if "ffn1" in phases:
                        phase_ffn(C, layer, 0, s)
                    if "mix" in phases:
                        if layer == 0:
                            phase_mix0(C, s)
                        else:
                            phase_mix1(C, s)
                    if "ffn2" in phases:
                        phase_ffn(C, layer, 2, s)
                phase_store(C, s)
            S.emit()
    return nc, consts


def phase_load(C, s):
    nc, S = C.nc, C.S
    x = C.d["x"].ap()
    with ExitStack() as es:
        stage = [(C.sb("ldst%d" % i, [128, D_MODEL], F32, es), Buf()) for i in range(2)]
        brot = Rot(C.banks)
        for i in range(NCH):
            st, STB = stage[i % 2]
            S.op("sp", lambda e, st=st, i=i: e.dma_start(out=st[:], in_=x[s, i * 128:(i + 1) * 128, :]),
                 writes=[STB], dma=True)
            for half in range(2):
                bank, BB = brot.next()
                for q in range(4):
                    dc = half * 4 + q
                    S.op("pe", lambda e, bank=bank, st=st, q=q, dc=dc: e.transpose(
                        out=bank[:, q * 128:(q + 1) * 128], in_=st[:, dc * 128:(dc + 1) * 128],
                        identity=C.ident_f32[:]), reads=[STB, C.CONST], writes=[BB], sig=(q == 3))
                eng = "act" if half == 0 else "dve"
                outap = C.xT[:, half * 4:half * 4 + 4, i * 128:(i + 1) * 128]
                inap = bank[:].rearrange("p (q t) -> p q t", q=4)
                wr = [C.XT[half * 4 + q][i // 4] for q in range(4)]
                if eng == "act":
                    S.op("act", lambda e, o=outap, a=inap: e.copy(out=o, in_=a), reads=[BB], writes=wr)
                else:
                    S.op("dve", lambda e, o=outap, a=inap: e.tensor_copy(out=o, in_=a), reads=[BB], writes=wr)
    C.S.barrier()


def phase_store(C, s):
    nc, S = C.nc, C.S
    y = C.d["y"].ap()
    with ExitStack() as es:
        stage = [(C.sb("stst%d" % i, [128, D_MODEL], F32, es), Buf()) for i in range(2)]
        brot = Rot(C.banks)
        for i in range(NCH):
            st, STB = stage[i % 2]
            for half in range(2):
                bank, BB = brot.next()
                for q in range(4):
                    dc = half * 4 + q
                    S.op("pe", lambda e, bank=bank, q=q, dc=dc, i=i: e.transpose(
                        out=bank[:, q * 128:(q + 1) * 128], in_=C.xT[:, dc, i * 128:(i + 1) * 128],
                        identity=C.ident_f32[:]), reads=[C.XT[dc][i // 4], C.CONST], writes=[BB], sig=(q == 3))
                outap = st[:, half * 512:(half + 1) * 512]
                if half == 0:
                    S.op("act", lambda e, o=outap, bank=bank: e.copy(out=o, in_=bank[:]), reads=[BB], writes=[STB])
                else:
                    S.op("dve", lambda e, o=outap, bank=bank: e.tensor_copy(out=o, in_=bank[:]), reads=[BB], writes=[STB])
            S.op("sp", lambda e, st=st, i=i: e.dma_start(out=y[s, i * 128:(i + 1) * 128, :], in_=st[:]),
                 reads=[STB], dma=True)
    C.S.barrier()


def rmsnorm_hT(C, es_outer, g, hT, H):
    S = C.S
    with ExitStack() as es:
        sqt = C.sb("sq", [128, 8, 512], BF16, es)
        SQ = Buf()
        rs = [(C.sb("rs%d" % i, [128, 512], F32, es), Buf()) for i in range(2)]
        sbank, SB = C.banks[7]
        for t in range(NT):
            tl = slice(t * 512, (t + 1) * 512)
            xr = [C.XT[dc][t] for dc in range(8)]
            S.op("act", lambda e, tl=tl: e.activation(out=sqt[:], in_=C.xT[:, :, tl], func=AF.Square),
                 reads=xr, writes=[SQ])
            for dc in range(8):
                S.op("pe", lambda e, dc=dc: e.matmul(sbank[:], lhsT=C.ones_bf[:], rhs=sqt[:, dc, :],
                                                     start=(dc == 0), stop=(dc == 7)),
                     reads=[SQ, C.CONST], writes=[SB], sig=(dc == 7))
            rst, RS = rs[t % 2]
            S.op("act", lambda e, rst=rst: e.activation(out=rst[:], in_=sbank[:], func=AF.Sqrt,
                                                        scale=1.0 / D_MODEL, bias=EPS),
                 reads=[SB], writes=[RS])
            S.op("dve", lambda e, rst=rst: e.reciprocal(out=rst[:], in_=rst[:]), reads=[RS], writes=[RS])
            for dc in range(8):
                S.op("dve", lambda e, dc=dc, tl=tl, rst=rst: e.scalar_tensor_tensor(
                    out=hT[:, dc, tl], in0=C.xT[:, dc, tl], scalar=g[:, dc:dc + 1], in1=rst[:],
                    op0=ALU.mult, op1=ALU.mult), reads=[C.XT[dc][t], RS, C.CONST], writes=[H[t]])
    C.S.barrier()


def phase_ffn(C, layer, which, s):
    nc, S = C.nc, C.S
    pre = "ffn1" if which == 0 else "ffn2"
    Wg = C.d[pre + "_w_gate"].ap()[layer]
    Wu = C.d[pre + "_w_up"].ap()[layer]
    Wd = C.d[pre + "_w_down"].ap()[layer]
    g = C.gains[:, which, layer, :]
    with ExitStack() as es:
        hT = C.sb("hT", [128, 8, SEQ], BF16, es)
        H = [Buf("h%d" % t) for t in range(NT)]
        nwb = 2
        wb = []
        for i in range(nwb):
            wb.append(dict(g=C.sb("wg%d" % i, [128, 8, 512], BF16, es), u=C.sb("wu%d" % i, [128, 8, 512], BF16, es),
                           d=C.sb("wd%d" % i, [128, 4, D_MODEL], BF16, es), G=Buf(), U=Buf(), D=Buf()))
        actT = C.sb("actT", [128, 2, 4, 512], BF16, es)
        ACT = [[Buf() for j in range(4)] for k in range(2)]
        silu = [(C.sb("silu%d" % i, [128, 512], F32, es), Buf()) for i in range(2)]
        bk = C.banks
        gate_rot = Rot([bk[0], bk[1]])
        up_rot = Rot([bk[2], bk[3]])
        down_rot = Rot([bk[4], bk[5], bk[6]])
        stat_bank = bk[7]
        silu_rot = Rot(silu)

        groups = [(0, 4), (4, 4), (8, 4), (12, 4), (16, 4), (20, 2)]

        def load_w(gi):
            f0, nf = groups[gi]
            w = wb[gi % nwb]
            fw = nf * 128
            S.op("pool", lambda e: e.dma_start(out=w["g"][:, :, 0:fw],
                                               in_=Wg[:, f0 * 128:f0 * 128 + fw].rearrange("(c p) f -> p c f", p=128)),
                 writes=[w["G"]], dma=True)
            S.op("pool", lambda e: e.dma_start(out=w["u"][:, :, 0:fw],
                                               in_=Wu[:, f0 * 128:f0 * 128 + fw].rearrange("(c p) f -> p c f", p=128)),
                 writes=[w["U"]], dma=True)
            S.op("pool", lambda e: e.dma_start(out=w["d"][:, 0:nf, :],
                                               in_=Wd[f0 * 128:f0 * 128 + fw, :].rearrange("(c p) d -> p c d", p=128)),
                 writes=[w["D"]], dma=True)

        load_w(0)
        load_w(1)

        rmsnorm_hT(C, es, g, hT, H)

        items = [(gi, t) for gi in range(len(groups)) for t in range(NT)]

        def GU(k):
            gi, t = items[k]
            f0, nf = groups[gi]
            w = wb[gi % nwb]
            tl = slice(t * 512, (t + 1) * 512)
            for j in range(nf):
                pg, PG = gate_rot.next()
                pu, PU = up_rot.next()
                for kc in range(8):
                    S.op("pe", lambda e, pg=pg, kc=kc, j=j: e.matmul(
                        pg[:], lhsT=w["g"][:, kc, j * 128:(j + 1) * 128], rhs=hT[:, kc, tl],
                        start=(kc == 0), stop=(kc == 7)), reads=[w["G"], H[t]], writes=[PG], sig=(kc == 7))
                for kc in range(8):
                    S.op("pe", lambda e, pu=pu, kc=kc, j=j: e.matmul(
                        pu[:], lhsT=w["u"][:, kc, j * 128:(j + 1) * 128], rhs=hT[:, kc, tl],
                        start=(kc == 0), stop=(kc == 7)), reads=[w["U"], H[t]], writes=[PU], sig=(kc == 7))
                sl, SL = silu_rot.next()
                S.op("act", lambda e, sl=sl, pg=pg: e.activation(out=sl[:], in_=pg[:], func=AF.Silu),
                     reads=[PG], writes=[SL])
                S.op("dve", lambda e, sl=sl, pu=pu, j=j: e.tensor_tensor(
                    out=actT[:, k % 2, j, :], in0=pu[:], in1=sl[:], op=ALU.mult),
                    reads=[PU, SL], writes=[ACT[k % 2][j]])

        def DN(k):
            gi, t = items[k]
            f0, nf = groups[gi]
            w = wb[gi % nwb]
            tl = slice(t * 512, (t + 1) * 512)
            for dc in range(8):
                pd, PD = down_rot.next()
                for j in range(nf):
                    S.op("pe", lambda e, pd=pd, j=j, dc=dc: e.matmul(
                        pd[:], lhsT=w["d"][:, j, dc * 128:(dc + 1) * 128], rhs=actT[:, k % 2, j, :],
                        start=(j == 0), stop=(j == nf - 1)), reads=[w["D"], ACT[k % 2][j]], writes=[PD],
                        sig=(j == nf - 1))
                S.op("dve", lambda e, pd=pd, dc=dc: e.scalar_tensor_tensor(
                    out=C.xT[:, dc, tl], in0=pd[:], scalar=0.5, in1=C.xT[:, dc, tl],
                    op0=ALU.mult, op1=ALU.add), reads=[PD, C.XT[dc][t]], writes=[C.XT[dc][t]])

        for k in range(len(items)):
            gi, t = items[k]
            GU(k)
            if k >= 1:
                DN(k - 1)
            if t == 0 and gi >= 1 and gi + 1 < len(groups):
                load_w(gi + 1)
        DN(len(items) - 1)
    C.S.barrier()


def phase_init(C):
    nc, S = C.nc, C.S
    C.F = {}
    with ExitStack() as es:
        tab = C.sb("tab", [33, 16], F32, es)
        TAB = Buf()
        S.op("sp", lambda e: e.dma_start(out=tab[0:32, :], in_=C.d["rel_bias"].ap()), writes=[TAB], dma=True)
        S.op("dve", lambda e: e.memset(tab[32:33, :], NEGM), writes=[TAB])
        oh = [(C.sb("oh%d" % i, [33, 512], F32, es), Buf()) for i in range(2)]
        fo = [(C.sb("fo%d" % i, [16, 512], F32, es), Buf()) for i in range(2)]
        brot = Rot(C.banks)
        k = 0
        for kind, (win, ds, L) in KINDS.items():
            C.F[kind] = nc.dram_tensor("F_" + kind, [16, L], F32, kind="Internal")
            C.FB = getattr(C, "FB", Buf("F"))
            for m0 in range(0, L, 512):
                n = min(512, L - m0)
                oht, OH = oh[k % 2]
                fot, FO = fo[k % 2]
                k += 1
                bank, BB = brot.next()
                S.op("sp", lambda e, oht=oht, m0=m0, n=n, kind=kind: e.dma_start(
                    out=oht[:, 0:n], in_=C.d["oh_" + kind].ap()[:, m0:m0 + n]), writes=[OH], dma=True)
                S.op("pe", lambda e, bank=bank, oht=oht, n=n: e.matmul(
                    bank[0:16, 0:n], lhsT=tab[:, :], rhs=oht[:, 0:n], start=True, stop=True),
                    reads=[TAB, OH], writes=[BB])
                S.op("dve", lambda e, bank=bank, fot=fot, n=n: e.tensor_copy(out=fot[:, 0:n], in_=bank[0:16, 0:n]),
                     reads=[BB], writes=[FO])
                S.op("sp", lambda e, fot=fot, m0=m0, n=n, kind=kind: e.dma_start(
                    out=C.F[kind].ap()[:, m0:m0 + n], in_=fot[:, 0:n]), reads=[FO], writes=[C.FB], dma=True)
    C.S.barrier()


def load_hankel(C, dst, DST, kind, o, h0, nh):
    S = C.S
    L = KINDS[kind][2]
    src = bass.AP(tensor=C.F[kind], offset=h0 * L + 128 * o, ap=[[1, 128], [L, nh], [1, 128]])
    return S.op("pool", lambda e: e.dma_start(out=dst, in_=src), reads=[C.FB], writes=[DST], dma=True)


class AttnPipe:
    def __init__(self, C, st_banks, pts):
        self.C = C
        self.st = Rot(st_banks)
        self.pt = Rot(pts)
        self.pending = None

    def unit(self, fills, pvs, krows=128, ncols=512, exp_f32=None, e0=0):
        C, S = self.C, self.C.S
        bank, BB = self.st.next()
        nf = len(fills)
        for fi, f in enumerate(fills):
            m = f.get("m", 128)
            c0 = f.get("c0", 0)
            n = f.get("n", ncols)
            S.op("pe", lambda e, f=f, m=m, c0=c0, n=n: e.matmul(
                bank[0:m, c0:c0 + n], lhsT=f["lhsT"], rhs=f["rhs"], start=f["start"], stop=f["stop"]),
                reads=f["reads"], writes=[BB], sig=(fi == nf - 1))
        pt, PT = self.pt.next()
        if exp_f32 is not None:
            ef, EF = exp_f32
            S.op("act", lambda e: e.activation(out=ef[0:krows, 0:ncols], in_=bank[0:krows, 0:ncols], func=AF.Exp),
                 reads=[BB], writes=[EF])
            S.op("dve", lambda e: e.tensor_copy(out=pt[0:krows, 0:ncols], in_=ef[0:krows, 0:ncols]),
                 reads=[EF], writes=[PT])
        else:
            S.op("act", lambda e: e.activation(out=pt[0:krows, e0:ncols], in_=bank[0:krows, e0:ncols], func=AF.Exp),
                 reads=[BB], writes=[PT])
        prev = self.pending
        self.pending = (pt, PT, pvs, krows)
        if prev is not None:
            self._emit_pv(prev)

    def _emit_pv(self, pend):
        S = self.C.S
        pt, PT, pvs, krows = pend
        npv = len(pvs)
        for pi, pv in enumerate(pvs):
            S.op("pe", lambda e, pv=pv: e.matmul(
                pv["out"], lhsT=(pv["lhsT"] if "lhsT" in pv else pt[0:krows, pv["c0"]:pv["c0"] + 128]), rhs=pv["v"],
                start=pv["start"], stop=pv["stop"]),
                reads=[PT] + pv["reads"], writes=[pv["OUT"]], sig=(pi == npv - 1))

    def flush(self):
        if self.pending is not None:
            self._emit_pv(self.pending)
            self.pending = None


def out_slot(banks2, slot):
    bank, BB = banks2[slot // 7]
    c = (slot % 7) * 65
    return bank[:, c:c + 65], BB


def fm_norm(C, zbank, ZB, dest, DEST, gain, sf, dh, scr, n=512, view=None):
    S = C.S
    sqb, SQ = scr["sq"].next()
    sbank, SB = scr["sbank"].next()
    rsb, RS = scr["rs"].next()
    S.op("act", lambda e: e.activation(out=sqb[:, 0:n], in_=zbank[:, 0:n], func=AF.Square), reads=[ZB], writes=[SQ])
    S.op("pe", lambda e: e.matmul(sbank[:, 0:n], lhsT=C.bd64_bf[:], rhs=sqb[:, 0:n], start=True, stop=True),
         reads=[SQ, C.CONST], writes=[SB])
    S.op("act", lambda e: e.activation(out=rsb[:, 0:n], in_=sbank[:, 0:n], func=AF.Sqrt, scale=1.0 / (dh * sf * sf),
                                       bias=EPS / (sf * sf)), reads=[SB], writes=[RS])
    S.op("dve", lambda e: e.reciprocal(out=rsb[:, 0:n], in_=rsb[:, 0:n]), reads=[RS], writes=[RS])
    if isinstance(dest, list):
        for (d_, ps_) in dest:
            S.op("dve", lambda e, d_=d_, ps_=ps_: e.scalar_tensor_tensor(
                out=d_, in0=zbank[ps_, 0:n], scalar=gain[ps_], in1=rsb[ps_, 0:n], op0=ALU.mult, op1=ALU.mult),
                reads=[ZB, RS, C.CONST], writes=[DEST])
        return
    vw = (lambda a: a) if view is None else view
    zin, rin = vw(zbank[:, 0:n]), vw(rsb[:, 0:n])
    S.op("dve", lambda e: e.scalar_tensor_tensor(out=dest, in0=zin, scalar=gain, in1=rin,
                                                 op0=ALU.mult, op1=ALU.mult),
         reads=[ZB, RS, C.CONST], writes=[DEST])


def out_proj_tile(C, t, mixedT, MT, wout, WO, nfc, brot):
    S = C.S
    tl = slice(t * 512, (t + 1) * 512)
    for dc in range(8):
        bank, BB = brot.next()
        for fc in range(nfc):
            S.op("pe", lambda e, bank=bank, fc=fc, dc=dc: e.matmul(
                bank[:], lhsT=wout[:, fc, dc * 128:(dc + 1) * 128], rhs=mixedT[:, fc, :],
                start=(fc == 0), stop=(fc == nfc - 1)), reads=[WO, MT], writes=[BB], sig=(fc == nfc - 1))
        S.op("dve", lambda e, bank=bank, dc=dc: e.tensor_tensor(
            out=C.xT[:, dc, tl], in0=bank[:], in1=C.xT[:, dc, tl], op=ALU.add),
            reads=[BB, C.XT[dc][t]], writes=[C.XT[dc][t]])


DIL_D = (1, 4, 16)


def phase_mix0(C, s):
    nc, S = C.nc, C.S
    Win = C.d["ab_w_in"].ap()[0]
    Wout = C.d["ab_w_out"].ap()[0]
    Wqb = C.d["mla_w_q_b"].ap()[0]
    Wkvb = C.d["mla_w_kv_b"].ap()[0]
    U = C.Udil
    UB = C.UdilB
    bk = C.banks
    sc_d = 64 ** -0.5
    sc_m = 96 ** -0.5
    with ExitStack() as es:
        cqn = C.sb("cqn", [128, 2, SEQ], BF16, es)
        ckvn = C.sb("ckvn", [128, SEQ], BF16, es)
        KR0 = C.sb("KR0", [32, SEQ], F32, es)
        sqkr = C.sb("sqkr", [32, SEQ], BF16, es)
        mla_out = C.sb("mla_out", [128, NCH, 512], BF16, es)
        gl = C.sb("gl0", [128, 16], F32, es)
        pts = [(C.sb("pt%d" % i, [128, 512], BF16, es), Buf()) for i in range(3)]
        small = C.sb("small", [128, 2, 16], F32, es)
        CQ, CKV, KRB = ([Buf() for _ in range(NT)] for _ in range(3))
        MO = [Buf() for _ in range(NCH)]
        GL, SMB = Buf(), [Buf(), Buf()]
        rope_cos = C.sb("rope_cos", [32, SEQ], F32, es)
        rope_sin = C.sb("rope_sin", [32, SEQ], F32, es)
        RT = Buf()
        S.op("sp", lambda e: e.dma_start(out=rope_cos[:], in_=C.d["rope_cos"].ap()), writes=[RT], dma=True)
        S.op("sp", lambda e: e.dma_start(out=rope_sin[:], in_=C.d["rope_sin"].ap()), writes=[RT], dma=True)

        def gload(col, rows, src):
            S.op("sp", lambda e: e.dma_start(out=gl[rows, col:col + 1], in_=src.rearrange("(d o) -> d o", o=1)),
                 writes=[GL], dma=True)

        dq, dk = C.d["dil_q_norm"].ap()[0], C.d["dil_k_norm"].ap()[0]
        qa, kva = C.d["mla_q_a_norm"].ap()[0], C.d["mla_kv_a_norm"].ap()[0]
        mq, mk = C.d["mla_q_norm"].ap()[0], C.d["mla_k_norm"].ap()[0]
        for hh in range(2):
            gload(0, slice(hh * 64, hh * 64 + 64), dq)
            gload(1, slice(hh * 64, hh * 64 + 64), dk)
        gload(2, slice(0, 128), qa[0:128])
        gload(3, slice(0, 128), qa[128:256])
        gload(4, slice(0, 128), kva)
        for col, v in ((5, mq), (8, mk)):
            gload(col, slice(0, 64), v[0:64])
            gload(col + 1, slice(0, 32), v[64:96])
            gload(col + 2, slice(0, 16), v[80:96])
            gload(col + 2, slice(16, 32), v[64:80])

        with ExitStack() as es2:
            hT = C.sb("hT", [128, 8, SEQ], BF16, es2)
            H = [Buf() for _ in range(NT)]
            rmsnorm_hT(C, es2, C.gains[:, 1, 0, :], hT, H)
            wb = C.sb("wb0", [128, 8, 768], BF16, es2)
            WB = Buf()
            scr = dict(sq=Rot([(C.sb("fsq%d" % i, [128, 512], BF16, es2), Buf()) for i in range(2)]),
                       rs=Rot([(C.sb("frs%d" % i, [128, 512], F32, es2), Buf()) for i in range(2)]),
                       sbank=Rot([bk[3], bk[4]]))
            tmp = Rot([(C.sb("tmp%d" % i, [32, 512], F32, es2), Buf()) for i in range(2)])
            zrot = Rot([bk[0], bk[1], bk[2]])
            trot = Rot([bk[5], bk[6]])

            def wload(pieces):
                for (a, c0, c1) in pieces:
                    S.op("pool", lambda e, a=a, c0=c0, c1=c1: e.dma_start(
                        out=wb[:, :, a:a + (c1 - c0)], in_=Win[:, c0:c1].rearrange("(c p) f -> p c f", p=128)),
                        writes=[WB], dma=True)

            def proj(t, lhs_fn, m=128):
                tl = slice(t * 512, (t + 1) * 512)
                zb, ZB = zrot.next()
                for kc in range(8):
                    l_ = lhs_fn(kc)
                    S.op("pe", lambda e, kc=kc, zb=zb, tl=tl, l_=l_: e.matmul(
                        zb[0:m, :], lhsT=l_, rhs=hT[:, kc, tl], start=(kc == 0), stop=(kc == 7)),
                        reads=[WB, H[t]], writes=[ZB], sig=(kc == 7))
                return tl, zb, ZB

            wload([(0, 0, 416), (416, 400, 416), (432, 384, 400)])
            for t in range(NT):
                zs = [proj(t, lambda kc, c=c: wb[:, kc, c * 128:(c + 1) * 128]) for c in range(2)]
                tl = zs[0][0]
                sbank, SB = scr["sbank"].next()
                for c in range(2):
                    sqb, SQ = scr["sq"].next()
                    S.op("act", lambda e, sqb=sqb, zb=zs[c][1]: e.activation(out=sqb[:], in_=zb[:], func=AF.Square),
                         reads=[zs[c][2]], writes=[SQ])
                    S.op("pe", lambda e, sqb=sqb, c=c, sbank=sbank: e.matmul(sbank[:], lhsT=C.ones_bf[:], rhs=sqb[:],
                                                                           start=(c == 0), stop=(c == 1)),
                         reads=[SQ, C.CONST], writes=[SB], sig=(c == 1))
                rsb, RS = scr["rs"].next()
                S.op("act", lambda e, rsb=rsb, sbank=sbank: e.activation(out=rsb[:], in_=sbank[:], func=AF.Sqrt,
                                                                        scale=1.0 / 256, bias=EPS), reads=[SB], writes=[RS])
                S.op("dve", lambda e, rsb=rsb: e.reciprocal(out=rsb[:], in_=rsb[:]), reads=[RS], writes=[RS])
                for c in range(2):
                    S.op("dve", lambda e, c=c, rsb=rsb, zb=zs[c][1], tl=tl: e.scalar_tensor_tensor(
                        out=cqn[:, c, tl], in0=zb[:], scalar=gl[:, 2 + c:3 + c], in1=rsb[:], op0=ALU.mult, op1=ALU.mult),
                        reads=[zs[c][2], RS, GL], writes=[CQ[t]])
                tl, zb, ZB = proj(t, lambda kc: wb[:, kc, 256:384])
                sbank, SB = scr["sbank"].next()
                sqb, SQ = scr["sq"].next()
                S.op("act", lambda e, sqb=sqb, zb=zb: e.activation(out=sqb[:], in_=zb[:], func=AF.Square), reads=[ZB], writes=[SQ])
                S.op("pe", lambda e, sqb=sqb, sbank=sbank: e.matmul(sbank[:], lhsT=C.ones_bf[:], rhs=sqb[:], start=True, stop=True),
                     reads=[SQ, C.CONST], writes=[SB])
                rsb, RS = scr["rs"].next()
                S.op("act", lambda e, rsb=rsb, sbank=sbank: e.activation(out=rsb[:], in_=sbank[:], func=AF.Sqrt,
                                                                        scale=1.0 / 128, bias=EPS), reads=[SB], writes=[RS])
                S.op("dve", lambda e, rsb=rsb: e.reciprocal(out=rsb[:], in_=rsb[:]), reads=[RS], writes=[RS])
                S.op("dve", lambda e, rsb=rsb, zb=zb, tl=tl: e.scalar_tensor_tensor(
                    out=ckvn[:, tl], in0=zb[:], scalar=gl[:, 4:5], in1=rsb[:], op0=ALU.mult, op1=ALU.mult),
                    reads=[ZB, RS, GL], writes=[CKV[t]])
                tl, z1, Z1 = proj(t, lambda kc: wb[:, kc, 384:416], m=32)
                tl, z2, Z2 = proj(t, lambda kc: wb[:, kc, 416:448], m=32)
                t1, T1 = tmp.next()
                t2, T2 = tmp.next()
                S.op("act", lambda e, z1=z1, tl=tl: e.activation(out=sqkr[:, tl], in_=z1[0:32, :], func=AF.Square),
                     reads=[Z1], writes=[KRB[t]])
                S.op("dve", lambda e, z1=z1, t1=t1, tl=tl: e.scalar_tensor_tensor(
                    out=t1[:], in0=z1[0:32, :], scalar=gl[0:32, 9:10], in1=rope_cos[:, tl], op0=ALU.mult, op1=ALU.mult),
                    reads=[Z1, GL, RT], writes=[T1])
                S.op("dve", lambda e, z2=z2, t2=t2, tl=tl: e.scalar_tensor_tensor(
                    out=t2[:], in0=z2[0:32, :], scalar=gl[0:32, 10:11], in1=rope_sin[:, tl], op0=ALU.mult, op1=ALU.mult),
                    reads=[Z2, GL, RT], writes=[T2])
                S.op("dve", lambda e, t1=t1, t2=t2, tl=tl: e.tensor_tensor(out=KR0[:, tl], in0=t1[:], in1=t2[:], op=ALU.add),
                     reads=[T1, T2], writes=[KRB[t]])

            qTd = C.sb("qTd", [128, 2, SEQ], BF16, es2)
            kTd = C.sb("kTd", [128, 2, SEQ], BF16, es2)
            v_dil = C.sb("v_dil", [128, NCH, 4, 65], BF16, es2)
            hk_dil = C.sb("hk_dil", [128, 2, 4, 128], BF16, es2)
            ust = Rot([(C.sb("ust%d" % i, [128, 260], F32, es2), Buf()) for i in range(2)])
            QD, KD = [Buf() for _ in range(NT)], [Buf() for _ in range(NT)]
            VD = [Buf() for _ in range(NCH)]
            HK = Buf()
            S.op("pool", lambda e: e.memset(v_dil[:], 1.0), writes=VD)
            pipe = AttnPipe(C, [bk[0], bk[1]], pts)
            obrot = Rot([bk[2]])
            for g in range(3):
                d = DIL_D[g]
                Sd = SEQ // d
                nchunk = Sd // 128
                base = 416 + g * 256
                wload([(0, base, base + 256), (256, base + 768, base + 1024), (512, base + 1536, base + 1792)])
                for o in range(2):
                    load_hankel(C, hk_dil[:, o], HK, "dil%d" % g, o, g * 4, 4)
                pw = 512 // d
                for t in range(NT):
                    for (off, dst, DST, gcol, sf) in ((0, qTd, QD, 0, sc_d), (256, kTd, KD, 1, 1.0)):
                        for pair in range(2):
                            tl, zb, ZB = proj(t, lambda kc, off=off, pair=pair: wb[:, kc, off + pair * 128:off + (pair + 1) * 128])
                            dview = dst[:, pair, :].rearrange("p (r u) -> p r u", r=d)[:, :, t * pw:(t + 1) * pw]
                            fm_norm(C, zb, ZB, dview, DST[t], gl[:, gcol:gcol + 1], sf, 64.0, scr,
                                    view=lambda a, d=d: a.rearrange("p (u r) -> p r u", r=d))
                for cc in range(NCH):
                    r, ic = cc // nchunk, cc % nchunk
                    t0 = r + d * 128 * ic
                    tb, TB = trot.next()
                    for kc in range(8):
                        S.op("pe", lambda e, kc=kc, tb=tb, t0=t0, d=d: e.matmul(
                            tb[:, 0:256], lhsT=hT[:, kc, t0:t0 + d * 127 + 1:d], rhs=wb[:, kc, 512:768],
                            start=(kc == 0), stop=(kc == 7)), reads=[WB] + H, writes=[TB], sig=(kc == 7))
                    S.op("act", lambda e, tb=tb, cc=cc: e.copy(out=v_dil[:, cc, :, 0:64],
                                                               in_=tb[:, 0:256].rearrange("p (h c) -> p h c", h=4)),
                         reads=[TB], writes=[VD[cc]])
                for cc in range(NCH):
                    r, ic = cc // nchunk, cc % nchunk
                    cs = [c for c in (ic - 1, ic) if c >= 0]
                    ob, OB = obrot.next()
                    obv = ob[:, 0:260].rearrange("p (h c) -> p h c", c=65)
                    qpos = cc * 128
                    for pair in range(2):
                        fills, pvs = [], []
                        blk = 0
                        nblk = 2 * len(cs)
                        for hh in range(2):
                            hs = slice(hh * 64, hh * 64 + 64)
                            for c in cs:
                                o = ic - c
                                kc_ = r * nchunk + c
                                fills.append(dict(c0=blk * 128, n=128, lhsT=kTd[hs, pair, kc_ * 128:(kc_ + 1) * 128],
                                                  rhs=qTd[hs, pair, qpos:qpos + 128], start=(blk == 0), stop=False,
                                                  reads=KD + QD))
                                fills.append(dict(c0=blk * 128, n=128, lhsT=C.anti_bf[:], rhs=hk_dil[:, o, pair * 2 + hh, :],
                                                  start=False, stop=(blk == nblk - 1), reads=[C.CONST, HK]))
                                pvs.append(dict(out=obv[:, pair * 2 + hh, :], OUT=OB, c0=blk * 128,
                                                v=v_dil[:, kc_, pair * 2 + hh, :],
                                                start=(pair == 0 and blk == 0), stop=(pair == 1 and blk == nblk - 1),
                                                reads=[VD[kc_]]))
                                blk += 1
                        pipe.unit(fills, pvs, ncols=nblk * 128)
                    pipe.flush()
                    us, US = ust.next()
                    S.op("act", lambda e, us=us, ob=ob: e.copy(out=us[:], in_=ob[:, 0:260]), reads=[OB], writes=[US])
                    dstU = U.ap()[g].rearrange("(u r) c -> r u c", r=d)[r, ic * 128:(ic + 1) * 128, :]
                    S.op("sp", lambda e, us=us, dstU=dstU: e.dma_start(out=dstU, in_=us[:]), reads=[US], writes=[UB], dma=True)
        S.barrier()

        with ExitStack() as es3:
            wqb = C.sb("wqb", [128, 2, 768], BF16, es3)
            wqsw = C.sb("wqsw", [128, 2, 8, 32], BF16, es3)
            wkvb = C.sb("wkvb", [128, 1024], BF16, es3)
            v_mla = C.sb("v_mla", [128, NCH, 8, 65], BF16, es3)
            tri = C.sb("tri", [128, 128], BF16, es3)
            W2, VM = Buf(), [Buf() for _ in range(NCH)]
            S.op("sp", lambda e: e.dma_start(out=tri[:], in_=C.d["tri"].ap()), writes=[RT], dma=True)
            S.op("pool", lambda e: e.dma_start(out=wqb[:], in_=Wqb.rearrange("(c p) f -> p c f", p=128)), writes=[W2], dma=True)
            wq4 = Wqb.rearrange("(c p) (h f) -> p c h f", p=128, f=96)
            for c_ in range(2):
                S.op("pool", lambda e, c_=c_: e.dma_start(out=wqsw[:, c_, :, 0:16], in_=wq4[:, c_, :, 80:96]), writes=[W2], dma=True)
                S.op("pool", lambda e, c_=c_: e.dma_start(out=wqsw[:, c_, :, 16:32], in_=wq4[:, c_, :, 64:80]), writes=[W2], dma=True)
            S.op("pool", lambda e: e.dma_start(out=wkvb[:], in_=Wkvb), writes=[W2], dma=True)
            wkv_v = C.sb("wkv_v", [128, 8, 64], BF16, es3)
            S.op("pool", lambda e: e.dma_start(out=wkv_v[:], in_=Wkvb.rearrange("p (h c) -> p h c", c=128)[:, :, 64:128]),
                 writes=[W2], dma=True)
            S.op("pool", lambda e: e.memset(v_mla[:], 1.0), writes=VM)
            trot = Rot([bk[5], bk[6]])
            for i in range(NCH):
                tb, TB = trot.next()
                S.op("pe", lambda e, tb=tb, i=i: e.matmul(
                    tb[:], lhsT=ckvn[:, i * 128:(i + 1) * 128],
                    rhs=wkv_v[:].rearrange("p h c -> p (h c)"), start=True, stop=True),
                    reads=[W2, CKV[i // 4]], writes=[TB])
                S.op("act", lambda e, tb=tb, i=i: e.copy(out=v_mla[:, i, :, 0:64],
                                                         in_=tb[:].rearrange("p (h c) -> p h c", c=64)),
                     reads=[TB], writes=[VM[i]])
            hd = [dict(qn=C.sb("qn%d" % k, [128, SEQ], BF16, es3), qr=C.sb("qr%d" % k, [128, SEQ], BF16, es3),
                       kn=C.sb("kn%d" % k, [128, SEQ], BF16, es3), kr=C.sb("kr%d" % k, [128, SEQ], BF16, es3),
                       B=[Buf() for _ in range(NT)]) for k in range(2)]
            for X_ in hd:
                for nm_ in ("qn", "qr", "kn", "kr"):
                    S.op("pool", lambda e, t_=X_[nm_]: e.memset(t_[:], 0.0), writes=X_["B"])
            sqs = Rot([(C.sb("msq%d" % i, [64, 3, 512], BF16, es3), Buf()) for i in range(2)])
            rss = Rot([(C.sb("mrs%d" % i, [64, 2, 512], F32, es3), Buf()) for i in range(2)])
            tmp = Rot([(C.sb("mtmp%d" % i, [32, 512], F32, es3), Buf()) for i in range(2)])
            zrot = Rot([bk[2], bk[3], bk[4]])
            srot = Rot([bk[5], bk[6]])
            pipe = AttnPipe(C, [bk[0], bk[1]], pts)
            obrot = Rot([bk[7]])
            par = 0
            for h in range(8):
                X = hd[h % 2]
                for t in range(NT):
                    tl = slice(t * 512, (t + 1) * 512)

                    def p2(lhs_fn, m, nk, src, SRC):
                        zb, ZB = zrot.next()
                        for kc in range(nk):
                            l_, r_ = lhs_fn(kc), src(kc)
                            S.op("pe", lambda e, kc=kc, zb=zb, l_=l_, r_=r_, m=m, nk=nk: e.matmul(
                                zb[0:m, :], lhsT=l_, rhs=r_, start=(kc == 0), stop=(kc == nk - 1)),
                                 reads=[W2, SRC], writes=[ZB], sig=(kc == nk - 1))
                        return zb, ZB

                    cq = lambda kc: cqn[:, kc, tl]
                    zq, ZQ = p2(lambda kc: wqb[:, kc, h * 96:h * 96 + 64], 64, 2, cq, CQ[t])
                    zr, ZR = p2(lambda kc: wqb[:, kc, h * 96 + 64:h * 96 + 96], 32, 2, cq, CQ[t])
                    zw, ZW = p2(lambda kc: wqsw[:, kc, h, :], 32, 2, cq, CQ[t])
                    sq3, SQ3 = sqs.next()
                    rs2, RS2 = rss.next()
                    S.op("act", lambda e, sq3=sq3, zq=zq: e.activation(out=sq3[0:64, 0, :], in_=zq[0:64, :], func=AF.Square),
                         reads=[ZQ], writes=[SQ3])
                    S.op("act", lambda e, sq3=sq3, zr=zr: e.activation(out=sq3[0:32, 1, :], in_=zr[0:32, :], func=AF.Square),
                         reads=[ZR], writes=[SQ3])
                    sb1, SB1 = srot.next()
                    S.op("pe", lambda e, sb1=sb1, sq3=sq3: e.matmul(sb1[0:64, :], lhsT=C.ones_bf[0:64, 0:64], rhs=sq3[0:64, 0, :],
                                                                   start=True, stop=False), reads=[SQ3, C.CONST], writes=[SB1], sig=False)
                    S.op("pe", lambda e, sb1=sb1, sq3=sq3: e.matmul(sb1[0:64, :], lhsT=C.ones_bf[0:32, 0:64], rhs=sq3[0:32, 1, :],
                                                                   start=False, stop=True), reads=[SQ3, C.CONST], writes=[SB1])
                    S.op("act", lambda e, sb1=sb1, rs2=rs2: e.activation(
                        out=rs2[0:64, 0, :], in_=sb1[0:64, :], func=AF.Sqrt, scale=1.0 / (96 * sc_m * sc_m), bias=EPS / (sc_m * sc_m)),
                        reads=[SB1], writes=[RS2])
                    S.op("dve", lambda e, rs2=rs2: e.reciprocal(out=rs2[0:64, 0, :], in_=rs2[0:64, 0, :]), reads=[RS2], writes=[RS2])
                    S.op("dve", lambda e, rs2=rs2, zq=zq, tl=tl, X=X: e.scalar_tensor_tensor(
                        out=X["qn"][0:64, tl], in0=zq[0:64, :], scalar=gl[0:64, 5:6], in1=rs2[0:64, 0, :], op0=ALU.mult, op1=ALU.mult),
                        reads=[ZQ, RS2, GL], writes=[X["B"][t]])
                    t1, T1 = tmp.next()
                    t2, T2 = tmp.next()
                    S.op("dve", lambda e, zr=zr, t1=t1, tl=tl: e.scalar_tensor_tensor(
                        out=t1[:], in0=zr[0:32, :], scalar=gl[0:32, 6:7], in1=rope_cos[:, tl], op0=ALU.mult, op1=ALU.mult),
                        reads=[ZR, GL, RT], writes=[T1])
                    S.op("dve", lambda e, zw=zw, t2=t2, tl=tl: e.scalar_tensor_tensor(
                        out=t2[:], in0=zw[0:32, :], scalar=gl[0:32, 7:8], in1=rope_sin[:, tl], op0=ALU.mult, op1=ALU.mult),
                        reads=[ZW, GL, RT], writes=[T2])
                    S.op("dve", lambda e, t1=t1, t2=t2: e.tensor_tensor(out=t1[:], in0=t1[:], in1=t2[:], op=ALU.add),
                         reads=[T1, T2], writes=[T1])
                    S.op("dve", lambda e, t1=t1, rs2=rs2, tl=tl, X=X: e.tensor_tensor(
                        out=X["qr"][0:32, tl], in0=t1[:], in1=rs2[0:32, 0, :], op=ALU.mult), reads=[T1, RS2], writes=[X["B"][t]])
                    zk, ZK = p2(lambda kc: wkvb[:, h * 128:h * 128 + 64], 64, 1, lambda kc: ckvn[:, tl], CKV[t])
                    S.op("act", lambda e, sq3=sq3, zk=zk: e.activation(out=sq3[0:64, 2, :], in_=zk[0:64, :], func=AF.Square),
                         reads=[ZK], writes=[SQ3])
                    sb2, SB2 = srot.next()
                    S.op("pe", lambda e, sb2=sb2, sq3=sq3: e.matmul(sb2[0:64, :], lhsT=C.ones_bf[0:64, 0:64], rhs=sq3[0:64, 2, :],
                                                                   start=True, stop=False), reads=[SQ3, C.CONST], writes=[SB2], sig=False)
                    S.op("pe", lambda e, sb2=sb2, tl=tl: e.matmul(sb2[0:64, :], lhsT=C.ones_bf[0:32, 0:64], rhs=sqkr[:, tl],
                                                                 start=False, stop=True), reads=[KRB[t], C.CONST], writes=[SB2])
                    S.op("act", lambda e, sb2=sb2, rs2=rs2: e.activation(
                        out=rs2[0:64, 1, :], in_=sb2[0:64, :], func=AF.Sqrt, scale=1.0 / 96, bias=EPS), reads=[SB2], writes=[RS2])
                    S.op("dve", lambda e, rs2=rs2: e.reciprocal(out=rs2[0:64, 1, :], in_=rs2[0:64, 1, :]), reads=[RS2], writes=[RS2])
                    S.op("dve", lambda e, rs2=rs2, zk=zk, tl=tl, X=X: e.scalar_tensor_tensor(
                        out=X["kn"][0:64, tl], in0=zk[0:64, :], scalar=gl[0:64, 8:9], in1=rs2[0:64, 1, :], op0=ALU.mult, op1=ALU.mult),
                        reads=[ZK, RS2, GL], writes=[X["B"][t]])
                    S.op("dve", lambda e, rs2=rs2, tl=tl, X=X: e.tensor_tensor(
                        out=X["kr"][0:32, tl], in0=KR0[:, tl], in1=rs2[0:32, 1, :], op=ALU.mult), reads=[KRB[t], RS2], writes=[X["B"][t]])
                for j4 in range(NT):
                    ncs = 4 * j4 + 4
                    ob, OB = obrot.next()
                    obv = ob[:, 0:260].rearrange("p (a c) -> p a c", c=65)
                    for c in range(ncs):
                        dd = c - 4 * j4
                        a0 = max(dd, 0)
                        c0 = a0 * 128
                        n = 512 - c0
                        ks = slice(c * 128, (c + 1) * 128)
                        qs = slice(j4 * 512 + c0, (j4 + 1) * 512)
                        fills = [dict(c0=c0, n=n, lhsT=X["kn"][:, ks], rhs=X["qn"][:, qs], start=True, stop=False, reads=X["B"]),
                                 dict(c0=c0, n=n, lhsT=X["kr"][:, ks], rhs=X["qr"][:, qs], start=False, stop=(dd < 0), reads=X["B"])]
                        if dd >= 0:
                            fills.append(dict(c0=c0, n=128, lhsT=C.ident_bf[:], rhs=tri[:], start=False, stop=True,
                                              reads=[C.CONST, RT]))
                        pvs = [dict(out=obv[:, a, :], OUT=OB, c0=a * 128, v=v_mla[:, c, h, :],
                                    start=(c == 0 and a == 0), stop=(c == ncs - 1 and a == 3), reads=[VM[c]])
                               for a in range(a0, 4)]
                        pipe.unit(fills, pvs, e0=c0)
                    pipe.flush()
                    par ^= 1
                    sm = small[:, par, :]
                    S.op("dve", lambda e, sm=sm, obv=obv: e.reciprocal(out=sm[:, 0:4], in_=obv[:, :, 64]), reads=[OB], writes=[SMB[par]])
                    for a in range(4):
                        S.op("dve", lambda e, sm=sm, obv=obv, a=a, j4=j4, h=h: e.tensor_scalar(
                            out=mla_out[:, j4 * 4 + a, h * 64:(h + 1) * 64], in0=obv[:, a, 0:64], scalar1=sm[:, a:a + 1],
                            scalar2=None, op0=ALU.mult), reads=[OB, SMB[par]], writes=[MO[j4 * 4 + a]])
        C.dbg_dump("mla_out", mla_out[:], [128, NCH, 512], BF16, MO)
        S.barrier()

        with ExitStack() as es4:
            wout = C.sb("wout0", [128, 6, D_MODEL], BF16, es4)
            WO = Buf()
            S.op("pool", lambda e: e.dma_start(out=wout[:], in_=Wout.rearrange("(c p) d -> p c d", p=128)), writes=[WO], dma=True)
            uts = Rot([(C.sb("ut%d" % i, [128, 3, 260], F32, es4), Buf()) for i in range(2)])
            dil_tm = Rot([(C.sb("diltm%d" % i, [128, 256], BF16, es4), Buf()) for i in range(2)])
            mixedT = Rot([(C.sb("mT0_%d" % i, [128, 6, 512], BF16, es4), Buf()) for i in range(2)])
            orot = Rot([bk[0], bk[1], bk[2]])
            mT, MTB = None, None
            par = 0
            for i in range(NCH):
                ut, UT = uts.next()
                S.op("sp", lambda e, ut=ut, i=i: e.dma_start(
                    out=ut[:], in_=U.ap()[:, i * 128:(i + 1) * 128, :].rearrange("g t c -> t g c")),
                    reads=[UB], writes=[UT], dma=True)
                S.op("pool", lambda e, ut=ut: e.tensor_tensor(out=ut[:, 0, :], in0=ut[:, 0, :], in1=ut[:, 1, :], op=ALU.add),
                     reads=[UT], writes=[UT])
                S.op("pool", lambda e, ut=ut: e.tensor_tensor(out=ut[:, 0, :], in0=ut[:, 0, :], in1=ut[:, 2, :], op=ALU.add),
                     reads=[UT], writes=[UT])
                par ^= 1
                sm = small[:, par, :]
                uv = ut[:, 0, :].rearrange("p (h c) -> p h c", c=65)
                S.op("dve", lambda e, sm=sm, uv=uv: e.reciprocal(out=sm[:, 0:4], in_=uv[:, :, 64]), reads=[UT], writes=[SMB[par]])
                dt_, DT_ = dil_tm.next()
                for hh in range(4):
                    S.op("dve", lambda e, sm=sm, uv=uv, hh=hh, dt_=dt_: e.tensor_scalar(
                        out=dt_[:, hh * 64:(hh + 1) * 64], in0=uv[:, hh, 0:64], scalar1=sm[:, hh:hh + 1], scalar2=None,
                        op0=ALU.mult), reads=[UT, SMB[par]], writes=[DT_])
                C.dbg_dump("dil_out", dt_[:], [NCH, 128, 256], BF16, [DT_], dst_fn=lambda a, i=i: a[i])
                if i % 4 == 0:
                    mT, MTB = mixedT.next()
                tb7, TB7 = bk[7]
                tb7v = tb7[:].bitcast(BF16)
                for fc in range(6):
                    src_ = mla_out[:, i, fc * 128:(fc + 1) * 128] if fc < 4 else dt_[:, (fc - 4) * 128:(fc - 3) * 128]
                    S.op("pe", lambda e, fc=fc, src_=src_, tb7v=tb7v: e.transpose(
                        out=tb7v[:, fc * 128:(fc + 1) * 128], in_=src_, identity=C.ident_bf[:]),
                        reads=[MO[i], DT_, C.CONST], writes=[TB7], sig=(fc == 5))
                S.op("act", lambda e, mT=mT, tb7v=tb7v, i=i: e.copy(
                    out=mT[:, :, (i % 4) * 128:(i % 4 + 1) * 128], in_=tb7v[:, 0:768].rearrange("p (f t) -> p f t", f=6)),
                    reads=[TB7], writes=[MTB])
                if i % 4 == 3:
                    out_proj_tile(C, i // 4, mT, MTB, wout, WO, 6, orot)
    S.barrier()


def phase_mix1(C, s):
    nc, S = C.nc, C.S
    Win = C.d["cd_w_in"].ap()[0]
    Wout = C.d["cd_w_out"].ap()[0]
    sc = 64 ** -0.5
    bk = C.banks
    with ExitStack() as es:
        swa_out = C.sb("swa_out", [128, NCH, 512], BF16, es)
        SWO = [Buf() for _ in range(NCH)]
        qT_nsa = C.sb("qT_nsa", [128, NCH, 4, 128], BF16, es)
        ksT = C.sb("ksT", [128, 2, SEQ], BF16, es)
        kwT = C.sb("kwT", [128, 2, SEQ], BF16, es)
        v_tm = C.sb("v_tm", [128, NCH, 4, 65], BF16, es)
        gates = C.sb("gates", [128, NCH, 24], F32, es)
        gl = C.sb("gl", [128, 8], F32, es)
        kcnT = C.sb("kcnT", [128, 128], BF16, es)
        vc_tm = C.sb("vc_tm", [128, 2, 65], BF16, es)
        QS, KS, QN, KSL, KW = ([Buf() for _ in range(NT)] for _ in range(5))
        VT = [Buf() for _ in range(NCH)]
        GT, GL, KCN, VCT = Buf(), Buf(), Buf(), Buf()
        S.op("pool", lambda e: e.memset(v_tm[:], 1.0), writes=VT)
        S.op("pool", lambda e: e.memset(vc_tm[:], 1.0), writes=[VCT])
        S.op("pool", lambda e: e.memset(ksT[:], 0.0), writes=KSL)
        S.op("pool", lambda e: e.memset(kwT[:], 0.0), writes=KW)
        gsrc = [C.d["swa_q_norm"].ap()[0], C.d["swa_k_norm"].ap()[0], C.d["nsa_q_norm"].ap()[0],
                C.d["nsa_k_norm"].ap()[0, 0], C.d["nsa_k_norm"].ap()[0, 1], C.d["nsa_k_norm"].ap()[0, 2]]
        for k, v in enumerate(gsrc):
            for hh in range(2):
                S.op("sp", lambda e, k=k, v=v, hh=hh: e.dma_start(
                    out=gl[hh * 64:(hh + 1) * 64, k:k + 1], in_=v.rearrange("(d o) -> d o", o=1)),
                    writes=[GL], dma=True)

        pts = [(C.sb("pt%d" % i, [128, 512], BF16, es), Buf()) for i in range(3)]
        small = C.sb("small", [128, 2, 64], F32, es)
        esS = ExitStack()
        qT_swa = C.sb("qT_swa", [128, NCH, 4, 128], BF16, esS)
        kT_swa = C.sb("kT_swa", [128, SEQ], BF16, esS)
        v_swa = C.sb("v_swa", [128, NCH, 2, 65], BF16, esS)
        sinkexp = C.sb("sinkexp", [128, 8], F32, esS)
        SM = [Buf(), Buf()]
        VS = [Buf() for _ in range(NCH)]
        HK, CST = Buf(), Buf()
        S.op("pool", lambda e: e.memset(v_swa[:], 1.0), writes=VS)
        S.op("sp", lambda e: e.dma_start(out=sinkexp[:], in_=C.d["swa_sinks"].ap()[0].partition_broadcast(128)),
             writes=[CST], dma=True)
        S.op("act", lambda e: e.activation(out=sinkexp[:], in_=sinkexp[:], func=AF.Exp), reads=[CST], writes=[CST])
        kcT = C.sb("kcT", [128, SEQ], BF16, esS)
        vcT = C.sb("vcT", [128, SEQ], BF16, esS)
        KC, VC = [Buf() for _ in range(NT)], [Buf() for _ in range(NT)]
        with ExitStack() as es2:
            hT = C.sb("hT", [128, 8, SEQ], BF16, es2)
            H = [Buf() for _ in range(NT)]
            rmsnorm_hT(C, es2, C.gains[:, 1, 1, :], hT, H)
            wbs = Rot([(C.sb("wb%d" % i, [128, 8, 544], BF16, es2), Buf()) for i in range(1)])
            scr = dict(sq=Rot([(C.sb("fsq%d" % i, [128, 512], BF16, es2), Buf()) for i in range(2)]),
                       rs=Rot([(C.sb("frs%d" % i, [128, 512], F32, es2), Buf()) for i in range(2)]),
                       sbank=Rot([bk[3], bk[4]]))
            zrot = Rot([bk[0], bk[1], bk[2]])
            trot = Rot([bk[5], bk[6]])

            def wload(wb, WB, pieces):
                for (a, c0, c1) in pieces:
                    S.op("pool", lambda e, a=a, c0=c0, c1=c1: e.dma_start(
                        out=wb[:, :, a:a + (c1 - c0)], in_=Win[:, c0:c1].rearrange("(c p) f -> p c f", p=128)),
                        writes=[WB], dma=True)

            def fm(wb, WB, lhs_fn, post):
                for t in range(NT):
                    tl = slice(t * 512, (t + 1) * 512)
                    zb, ZB = zrot.next()
                    for kc in range(8):
                        S.op("pe", lambda e, kc=kc, zb=zb, tl=tl: e.matmul(
                            zb[:], lhsT=lhs_fn(kc), rhs=hT[:, kc, tl], start=(kc == 0), stop=(kc == 7)),
                            reads=[WB, H[t]], writes=[ZB], sig=(kc == 7))
                    post(t, tl, zb, ZB)

            def post_norm(dst_fn, DST, gcol, sf, view=None):
                def f(t, tl, zb, ZB):
                    fm_norm(C, zb, ZB, dst_fn(tl), DST[t], gl[:, gcol:gcol + 1], sf, 64.0, scr, view=view)
                return f

            qview = lambda a: a.rearrange("p (c q) -> p c q", c=4)

            def post_copy(dst_fn, DST):
                def f(t, tl, zb, ZB):
                    S.op("act", lambda e: e.copy(out=dst_fn(tl), in_=zb[:]), reads=[ZB], writes=[DST[t]])
                return f

            def tm(wb, WB, jobs):
                for i in range(NCH):
                    tb, TB = trot.next()
                    c0 = 0
                    for ji, (rhs_fn, n, post) in enumerate(jobs):
                        for kc in range(8):
                            S.op("pe", lambda e, kc=kc, c0=c0, n=n, rhs_fn=rhs_fn, i=i, tb=tb: e.matmul(
                                tb[:, c0:c0 + n], lhsT=hT[:, kc, i * 128:(i + 1) * 128], rhs=rhs_fn(kc),
                                start=(kc == 0), stop=(kc == 7)), reads=[WB, H[i // 4]], writes=[TB],
                                sig=(kc == 7 and ji == len(jobs) - 1))
                        c0 += n
                    c0 = 0
                    for (rhs_fn, n, post) in jobs:
                        post(i, tb[:, c0:c0 + n], TB)
                        c0 += n

            def post_v(vt, VTB, slot0):
                def f(i, reg, TB):
                    S.op("act", lambda e: e.copy(out=vt[:, i, slot0:slot0 + 2, 0:64],
                                                 in_=reg.rearrange("p (g d) -> p g d", g=2)),
                         reads=[TB], writes=[VTB[i]])
                return f

            def post_gate(i, reg, TB):
                S.op("act", lambda e: e.activation(out=gates[:, i, :], in_=reg, func=AF.Sigmoid),
                     reads=[TB], writes=[GT])

            wb, WB = wbs.next()
            wload(wb, WB, [(r * 128 + g * 64, (4 * g + r) * 64, (4 * g + r) * 64 + 64) for r in range(4) for g in range(2)])
            for r in range(4):
                fm(wb, WB, lambda kc, r=r, wb=wb: wb[:, kc, r * 128:(r + 1) * 128],
                   post_norm(lambda tl, r=r: qT_swa[:, tl.start // 128:tl.stop // 128, r, :], QS, 0, sc, view=qview))
            wbB, WBB = wbs.next()
            wload(wbB, WBB, [(0, 512, 768), (256, 1280, 1536)])
            fm(wbB, WBB, lambda kc: wbB[:, kc, 0:128], post_norm(lambda tl: kT_swa[:, tl], KS, 1, 1.0))
            fm(wbB, WBB, lambda kc: wbB[:, kc, 256:384], post_copy(lambda tl: kcT[:, tl], KC))
            fm(wbB, WBB, lambda kc: wbB[:, kc, 384:512], post_copy(lambda tl: vcT[:, tl], VC))
            tm(wbB, WBB, [(lambda kc: wbB[:, kc, 128:256], 128, post_v(v_swa, VS, 0))])
            wbC, WBC = wbs.next()
            wload(wbC, WBC, [(r * 128 + g * 64, 768 + (4 * g + r) * 64, 768 + (4 * g + r) * 64 + 64) for r in range(4) for g in range(2)])
            for r in range(4):
                fm(wbC, WBC, lambda kc, r=r: wbC[:, kc, r * 128:(r + 1) * 128],
                   post_norm(lambda tl, r=r: qT_nsa[:, tl.start // 128:tl.stop // 128, r, :], QN, 2, sc, view=qview))
            wbD, WBD = wbs.next()
            wload(wbD, WBD, [(0, 1536, 2072)])
            fm(wbD, WBD, lambda kc: wbD[:, kc, 0:128], post_norm(lambda tl: [(ksT[0:64, 0, tl], slice(0, 64)), (ksT[64:128, 1, tl], slice(64, 128))], KSL, 4, 1.0))
            fm(wbD, WBD, lambda kc: wbD[:, kc, 256:384], post_norm(lambda tl: [(kwT[0:64, 0, tl], slice(0, 64)), (kwT[64:128, 1, tl], slice(64, 128))], KW, 5, 1.0))
            tm(wbD, WBD, [(lambda kc: wbD[:, kc, 128:256], 128, post_v(v_tm, VT, 0)),
                          (lambda kc: wbD[:, kc, 384:512], 128, post_v(v_tm, VT, 2)),
                          (lambda kc: wbD[:, kc, 512:536], 24, post_gate)])

        S.barrier()
        esC = ExitStack()
        zrot = Rot([bk[0], bk[1], bk[2]])
        scr = dict(sq=Rot([(C.sb("fsq%d" % i, [128, 512], BF16, esC), Buf()) for i in range(2)]),
                   rs=Rot([(C.sb("frs%d" % i, [128, 512], F32, esC), Buf()) for i in range(2)]),
                   sbank=Rot([bk[3], bk[4]]))
        w1sb = C.sb("w1sb", [128, 2, 32, 128], BF16, esC)
        posT = C.sb("posT", [128, 2, 32], BF16, esC)
        w2kp = C.sb("w2kp", [128, 2, 128], BF16, esC)
        w2v = C.sb("w2v", [128, 64], BF16, esC)
        h1T = C.sb("h1T", [128, 2, 2, 128], BF16, esC)
        b1 = C.sb("b1", [128, 4], F32, esC)
        W1, H1, B1 = Buf(), Buf(), Buf()
        S.op("pool", lambda e: e.memset(w2kp[:], 0.0), writes=[W1])
        for kv in range(2):
            for hh in range(2):
                S.op("pool", lambda e, kv=kv, hh=hh: e.dma_start(
                    out=w1sb[hh * 64:(hh + 1) * 64, kv, :, :],
                    in_=C.d["nsa_cmp_w1"].ap()[0, kv].rearrange("(l d) h -> d l h", d=64)), writes=[W1], dma=True)
                S.op("pool", lambda e, kv=kv, hh=hh: e.dma_start(
                    out=posT[hh * 64:(hh + 1) * 64, kv, :],
                    in_=C.d["nsa_cmp_pos"].ap()[0, kv].rearrange("l d -> d l")), writes=[W1], dma=True)
        S.op("pool", lambda e: e.dma_start(out=w2kp[:, 0, 0:64], in_=C.d["nsa_cmp_w2"].ap()[0, 0]), writes=[W1], dma=True)
        S.op("pool", lambda e: e.dma_start(out=w2kp[:, 1, 64:128], in_=C.d["nsa_cmp_w2"].ap()[0, 0]), writes=[W1], dma=True)
        S.op("pool", lambda e: e.dma_start(out=w2v[:], in_=C.d["nsa_cmp_w2"].ap()[0, 1]), writes=[W1], dma=True)
        for kv in range(2):
            src, SRC = (kcT, KC) if kv == 0 else (vcT, VC)
            for g in range(2):
                gs = slice(g * 64, (g + 1) * 64)
                zb, ZB = zrot.next()
                for l in range(32):
                    S.op("pe", lambda e, l=l, gs=gs, kv=kv, src=src, zb=zb: e.matmul(
                        zb[:, 0:127], lhsT=w1sb[gs, kv, l, :], rhs=src[gs, l:l + 16 * 126 + 1:16],
                        start=(l == 0), stop=(l == 31)), reads=[W1] + SRC, writes=[ZB], sig=False)
                for l in range(32):
                    S.op("pe", lambda e, l=l, gs=gs, kv=kv, zb=zb: e.matmul(
                        zb[:, 128:129], lhsT=w1sb[gs, kv, l, :], rhs=posT[gs, kv, l:l + 1],
                        start=(l == 0), stop=(l == 31)), reads=[W1], writes=[ZB], sig=(l == 31))
                j = kv * 2 + g
                S.op("dve", lambda e, j=j, zb=zb: e.tensor_copy(out=b1[:, j:j + 1], in_=zb[:, 128:129]),
                     reads=[ZB], writes=[B1])
                S.op("act", lambda e, j=j, zb=zb, kv=kv, g=g: e.activation(
                    out=h1T[:, kv, g, 0:127], in_=zb[:, 0:127], func=AF.Gelu_apprx_tanh, bias=b1[:, j:j + 1], scale=1.0),
                    reads=[ZB, B1], writes=[H1])
        zb, ZB = zrot.next()
        for g in range(2):
            S.op("pe", lambda e, g=g, zb=zb: e.matmul(zb[:, 0:127], lhsT=w2kp[:, g, :], rhs=h1T[:, 0, g, 0:127],
                                                     start=(g == 0), stop=(g == 1)),
                 reads=[W1, H1], writes=[ZB], sig=(g == 1))
        fm_norm(C, zb, ZB, kcnT[:, 0:127], KCN, gl[:, 3:4], 1.0, 64.0, scr, n=127)
        for g in range(2):
            zb, ZB = zrot.next()
            S.op("pe", lambda e, g=g, zb=zb: e.matmul(zb[0:127, 0:64], lhsT=h1T[:, 1, g, 0:127], rhs=w2v[:],
                                                     start=True, stop=True), reads=[W1, H1], writes=[ZB])
            S.op("act", lambda e, g=g, zb=zb: e.copy(out=vc_tm[0:127, g, 0:64], in_=zb[0:127, 0:64]),
                 reads=[ZB], writes=[VCT])

        esC.close()
        S.barrier()

        esH = ExitStack()
        hk_swa = C.sb("hk_swa", [128, 2, 8, 128], BF16, esH)
        for o in range(2):
            load_hankel(C, hk_swa[:, o], HK, "swa", o, 0, 8)
        pipe = AttnPipe(C, [bk[0], bk[1]], pts)
        ob = [bk[2], bk[3], bk[4]]

        def obv(b):
            return ob[b][0][:, 0:260].rearrange("p (r c) -> p r c", c=65)

        par = 0
        for i in range(NCH):
            qs = slice(i * 128, (i + 1) * 128)
            for g in range(2):
                gs = slice(g * 64, (g + 1) * 64)
                cs = [c for c in (i - 1, i) if c >= 0]
                for c in cs:
                    o = i - c
                    fills = [dict(lhsT=kT_swa[gs, c * 128:(c + 1) * 128], rhs=qT_swa[gs, i, :, :].rearrange("p r q -> p (r q)"), start=True, stop=False,
                                  reads=[KS[c // 4], QS[i // 4]]),
                             dict(lhsT=C.anti_bf[:], rhs=hk_swa[:, o, g * 4:(g + 1) * 4, :].rearrange("p h q -> p (h q)"), start=False, stop=True,
                                  reads=[C.CONST, HK])]
                    pvs = [dict(out=obv(g)[:, r, :], OUT=ob[g][1], c0=r * 128, v=v_swa[:, c, g, :],
                                start=(c == cs[0] and r == 0), stop=(c == i and r == 3), reads=[VS[c]]) for r in range(4)]
                    pipe.unit(fills, pvs)
            pipe.flush()
            for g in range(2):
                par ^= 1
                sm = small[:, par, :]
                S.op("dve", lambda e, g=g, sm=sm: e.tensor_tensor(out=sm[:, 0:4], in0=obv(g)[:, :, 64],
                                                                  in1=sinkexp[:, g * 4:(g + 1) * 4], op=ALU.add),
                     reads=[ob[g][1], CST], writes=[SM[par]])
                S.op("dve", lambda e, sm=sm: e.reciprocal(out=sm[:, 0:4], in_=sm[:, 0:4]), reads=[SM[par]], writes=[SM[par]])
                for r in range(4):
                    hcol = (g * 4 + r) * 64
                    S.op("dve", lambda e, g=g, r=r, sm=sm, hcol=hcol, i=i: e.tensor_scalar(
                        out=swa_out[:, i, hcol:hcol + 64], in0=obv(g)[:, r, 0:64], scalar1=sm[:, r:r + 1], scalar2=None,
                        op0=ALU.mult), reads=[ob[g][1], SM[par]], writes=[SWO[i]])

        C.dbg_dump("swa_out", swa_out[:], [128, NCH, 512], BF16, SWO)
        esH.close()
        esS.close()
        S.barrier()

        hk_slc = C.sb("hk_slc", [128, 14, 8, 128], BF16, es)
        hk_win4 = C.sb("hk_win4", [128, 8, 128], BF16, es)
        for o in range(14):
            load_hankel(C, hk_slc[:, o], HK, "slc", o, 8, 8)
        load_hankel(C, hk_win4[:], HK, "win", 4, 8, 8)
        cmpmask = C.sb("cmpmask", [128, SEQ], BF16, es)
        ovl1 = C.sb("ovl1", [128, 33], F32, es)
        causal01 = C.sb("causal01", [128, 16, 32], F32, es)
        sadd = C.sb("sadd", [128, 16, 32], F32, es)
        esel = C.sb("esel", [128, 16, 128], BF16, es)
        wout = C.sb("wout", [128, 8, D_MODEL], BF16, es)
        WO = Buf()
        S.op("pool", lambda e: e.memset(esel[:], 0.0), writes=[CST])
        for nm, t in (("cmpmask", cmpmask[:]), ("ovl1", ovl1[:]), ("causal01", causal01[:]), ("sadd", sadd[:]), ("esel", esel[0:32])):
            S.op("sp", lambda e, nm=nm, t=t: e.dma_start(out=t, in_=C.d[nm].ap()), writes=[CST], dma=True)
        S.op("pool", lambda e: e.dma_start(out=wout[:], in_=Wout.rearrange("(c p) d -> p c d", p=128)),
             writes=[WO], dma=True)
        efs = Rot([(C.sb("ef%d" % i, [128, 512], F32, es), Buf()) for i in range(2)])
        mixed_tm = Rot([(C.sb("mtm%d" % i, [128, 512], BF16, es), Buf()) for i in range(2)])
        mixedT = Rot([(C.sb("mT%d" % i, [128, 8, 512], BF16, es), Buf()) for i in range(1)])
        acc = C.sb("acc", [128, 2, 4, 64], F32, es)
        imp = C.sb("imp", [128, 2, 3, 32], F32, es)
        m8 = C.sb("m8", [128, 2, 16], F32, es)
        selm1 = C.sb("selm1", [128, 2, 32], BF16, es)
        selT = C.sb("selT", [128, 2, 4, 128], BF16, es)
        SELT = [Buf(), Buf()]
        S.op("pool", lambda e: e.memset(selT[:], 0.0), writes=SELT)
        orot7 = Rot([bk[7]])

        mT, MTB = None, None
        for i in range(NCH):
            qs = slice(i * 128, (i + 1) * 128)
            mtm, MTM = mixed_tm.next()
            for g in range(2):
                gs = slice(g * 64, (g + 1) * 64)
                par ^= 1
                sm = small[:, par, :]
                ef, EF = efs.next()
                ib, IB = bk[5]
                fills = [dict(m=127, lhsT=kcnT[gs, 0:127], rhs=qT_nsa[gs, i, :, :].rearrange("p r q -> p (r q)"), start=True, stop=False,
                              reads=[KCN, QN[i // 4]])]
                for r in range(4):
                    fills.append(dict(m=127, c0=r * 128, n=128, lhsT=C.ident_bf[0:127, 0:127], rhs=cmpmask[0:127, qs],
                                      start=False, stop=(r == 3), reads=[C.CONST, CST]))
                pvs = [dict(out=obv(0)[:, r, :], OUT=ob[0][1], c0=r * 128, v=vc_tm[0:127, g, :], start=(r == 0), stop=(r == 3),
                            reads=[VCT]) for r in range(4)]
                pvs += [dict(out=ib[:, r * 33:(r + 1) * 33], OUT=IB, c0=r * 128, v=ovl1[0:127, :], start=(r == 0), stop=(r == 3),
                             reads=[CST, EF], lhsT=ef[0:127, r * 128:(r + 1) * 128]) for r in range(4)]
                pipe.unit(fills, pvs, krows=127, exp_f32=(ef, EF))
                cs = [c for c in range(i - 4, i + 1) if c >= 0]
                for c in cs:
                    o = i - c
                    hk = (hk_slc[:, o, g * 4:(g + 1) * 4, :] if o < 4 else hk_win4[:, g * 4:(g + 1) * 4, :]).rearrange("p h q -> p (h q)")
                    fills = [dict(lhsT=kwT[:, g, c * 128:(c + 1) * 128], rhs=qT_nsa[:, i, :, :].rearrange("p r q -> p (r q)"), start=True, stop=False,
                                  reads=[KW[c // 4], QN[i // 4]]),
                             dict(lhsT=C.anti_bf[:], rhs=hk, start=False, stop=True, reads=[C.CONST, HK])]
                    pvs = [dict(out=obv(2)[:, r, :], OUT=ob[2][1], c0=r * 128, v=v_tm[:, c, 2 + g, :],
                                start=(c == cs[0] and r == 0), stop=(c == i and r == 3), reads=[VT[c]]) for r in range(4)]
                    pipe.unit(fills, pvs)
                ibv = ib[:, 0:132].rearrange("p (r c) -> p r c", c=33)
                S.op("dve", lambda e, sm=sm, ibv=ibv: e.tensor_scalar(out=sm[:, 0:4], in0=ibv[:, :, 32], scalar1=1e-30,
                                                                      scalar2=None, op0=ALU.max),
                     reads=[IB], writes=[SM[par]])
                S.op("dve", lambda e, sm=sm: e.reciprocal(out=sm[:, 0:4], in_=sm[:, 0:4]), reads=[SM[par]], writes=[SM[par]])
                im = imp[:, par, 0, :]
                sc1 = imp[:, par, 1, :]
                sc2 = imp[:, par, 2, :]
                S.op("dve", lambda e, sm=sm, im=im, ibv=ibv: e.tensor_scalar(
                    out=im, in0=ibv[:, 0, 0:32], scalar1=sm[:, 0:1], scalar2=None, op0=ALU.mult),
                    reads=[IB, SM[par]], writes=[SM[par]])
                for r in range(1, 4):
                    S.op("dve", lambda e, sm=sm, im=im, ibv=ibv, r=r: e.scalar_tensor_tensor(
                        out=im, in0=ibv[:, r, 0:32], scalar=sm[:, r:r + 1], in1=im, op0=ALU.mult, op1=ALU.add),
                        reads=[IB, SM[par]], writes=[SM[par]])
                S.op("dve", lambda e, im=im, sc1=sc1, i=i: e.tensor_tensor(out=sc1, in0=im, in1=causal01[:, i, :], op=ALU.mult),
                     reads=[SM[par], CST], writes=[SM[par]])
                S.op("dve", lambda e, sc1=sc1, i=i: e.tensor_tensor(out=sc1, in0=sc1, in1=sadd[:, i, :], op=ALU.add),
                     reads=[SM[par], CST], writes=[SM[par]])
                mm = m8[:, par, :]
                S.op("dve", lambda e, mm=mm, sc1=sc1: e.max(out=mm[:, 0:8], in_=sc1), reads=[SM[par]], writes=[SM[par]])
                S.op("dve", lambda e, mm=mm, sc1=sc1, sc2=sc2: e.match_replace(
                    out=sc2, in_to_replace=mm[:, 0:8], in_values=sc1, imm_value=-1e30), reads=[SM[par]], writes=[SM[par]])
                S.op("dve", lambda e, mm=mm, sc2=sc2: e.max(out=mm[:, 8:16], in_=sc2), reads=[SM[par]], writes=[SM[par]])
                sl = selm1[:, par, :]
                S.op("dve", lambda e, mm=mm, sc1=sc1, sl=sl: e.tensor_scalar(
                    out=sl, in0=sc1, scalar1=mm[:, 15:16], scalar2=-1.0, op0=ALU.is_ge, op1=ALU.add),
                    reads=[SM[par]], writes=[SM[par]])
                tb6, TB6 = bk[6]
                tb6v = tb6[:].bitcast(BF16)
                S.op("pe", lambda e, sl=sl, tb6v=tb6v: e.transpose(out=tb6v[0:32, 0:128], in_=sl, identity=C.ident_bf[:]),
                     reads=[SM[par], C.CONST], writes=[TB6])
                for r in range(4):
                    S.op("dve", lambda e, r=r, tb6v=tb6v, par=par: e.tensor_copy(out=selT[0:32, par, r, :], in_=tb6v[0:32, 0:128]),
                         reads=[TB6], writes=[SELT[par]])
                for c in range(i + 1):
                    o = i - c
                    fills = [dict(lhsT=ksT[:, g, c * 128:(c + 1) * 128], rhs=qT_nsa[:, i, :, :].rearrange("p r q -> p (r q)"), start=True, stop=False,
                                  reads=[KSL[c // 4], QN[i // 4]]),
                             dict(lhsT=C.anti_bf[:], rhs=hk_slc[:, min(o, 13), g * 4:(g + 1) * 4, :].rearrange("p h q -> p (h q)"), start=False, stop=False,
                                  reads=[C.CONST, HK]),
                             dict(lhsT=esel[:, c, :], rhs=selT[:, par, :, :].rearrange("p r q -> p (r q)"), start=False, stop=True,
                                  reads=[CST, SELT[par]])]
                    pvs = [dict(out=obv(1)[:, r, :], OUT=ob[1][1], c0=r * 128, v=v_tm[:, c, g, :],
                                start=(c == 0 and r == 0), stop=(c == i and r == 3), reads=[VT[c]]) for r in range(4)]
                    pipe.unit(fills, pvs)
                pipe.flush()
                gv = gates[:, i, g * 12:(g + 1) * 12].rearrange("p (r b) -> p r b", b=3)
                for b in range(3):
                    sb4 = sm[:, 8 + b * 4:12 + b * 4]
                    S.op("dve", lambda e, b=b, sb4=sb4: e.tensor_scalar(out=sb4, in0=obv(b)[:, :, 64], scalar1=1e-30,
                                                                        scalar2=None, op0=ALU.max),
                         reads=[ob[b][1]], writes=[SM[par]])
                    S.op("dve", lambda e, sb4=sb4: e.reciprocal(out=sb4, in_=sb4), reads=[SM[par]], writes=[SM[par]])
                    S.op("dve", lambda e, b=b, sb4=sb4, gv=gv: e.tensor_tensor(out=sb4, in0=sb4, in1=gv[:, :, b], op=ALU.mult),
                         reads=[SM[par], GT], writes=[SM[par]])
                for r in range(4):
                    ar = acc[:, par, r, :]
                    hcol = (g * 4 + r) * 64
                    S.op("dve", lambda e, r=r, ar=ar, sm=sm: e.tensor_scalar(
                        out=ar, in0=obv(0)[:, r, 0:64], scalar1=sm[:, 8 + r:9 + r], scalar2=None, op0=ALU.mult),
                        reads=[ob[0][1], SM[par]], writes=[SM[par]])
                    S.op("dve", lambda e, r=r, ar=ar, sm=sm: e.scalar_tensor_tensor(
                        out=ar, in0=obv(1)[:, r, 0:64], scalar=sm[:, 12 + r:13 + r], in1=ar, op0=ALU.mult, op1=ALU.add),
                        reads=[ob[1][1], SM[par]], writes=[SM[par]])
                    S.op("dve", lambda e, r=r, ar=ar, sm=sm, hcol=hcol, mtm=mtm: e.scalar_tensor_tensor(
                        out=mtm[:, hcol:hcol + 64], in0=obv(2)[:, r, 0:64], scalar=sm[:, 16 + r:17 + r], in1=ar,
                        op0=ALU.mult, op1=ALU.add), reads=[ob[2][1], SM[par]], writes=[MTM])
            C.dbg_dump("nsa_out", mtm[:], [NCH, 128, 512], BF16, [MTM], dst_fn=lambda a, i=i: a[i])
            if i % 4 == 0:
                mT, MTB = mixedT.next()
            tb7, TB7 = bk[7]
            tb7v = tb7[:].bitcast(BF16)
            for fc in range(8):
                src_ = swa_out[:, i, fc * 128:(fc + 1) * 128] if fc < 4 else mtm[:, (fc - 4) * 128:(fc - 3) * 128]
                S.op("pe", lambda e, fc=fc, src_=src_, tb7v=tb7v: e.transpose(
                    out=tb7v[:, fc * 128:(fc + 1) * 128], in_=src_, identity=C.ident_bf[:]),
                    reads=[MTM, SWO[i], C.CONST], writes=[TB7], sig=(fc == 7))
            S.op("act", lambda e, mT=mT, tb7v=tb7v, i=i: e.copy(
                out=mT[:, :, (i % 4) * 128:(i % 4 + 1) * 128], in_=tb7v.rearrange("p (f t) -> p f t", f=8)),
                reads=[TB7], writes=[MTB])
            if i % 4 == 3:
                out_proj_tile(C, i // 4, mT, MTB, wout, WO, 8, orot7)
    S.barrier()


_CACHE = {}


def kernel(**inputs):
    x = np.ascontiguousarray(np.asarray(inputs["x"], dtype=np.float32))
    if "prog" not in _CACHE:
        _CACHE["prog"] = build_program()
    nc, consts = _CACHE["prog"]
    shared = {name: np.ascontiguousarray(np.asarray(inputs[name], dtype=np.float32)) for name, _ in W_SPECS}
    shared.update(consts)
    in_maps = []
    for c in range(NCORES):
        m = dict(shared)
        m["x"] = x[c * SEQ_PER_CORE:(c + 1) * SEQ_PER_CORE]
        in_maps.append(m)
    res = run_bass_kernel_spmd(nc, in_maps, core_ids=list(range(NCORES)))
    return np.concatenate([np.asarray(r["y"], dtype=np.float32) for r in res.results], axis=0)
```
